# Optimizing a Trainium2 kernel written in Bass

```python
import jax
import jax.numpy as jnp
from jax import lax
import numpy as np

D_MODEL = 2048
BATCH = 32
SEQ = 256
DEPTH = 4
DEC_BATCH = 4
DEC_SEQ = 2048
PAST_LEN = 512

GRID_W = 64
D_A = 3 * D_MODEL // 8
HEAD_A = 64
H_A = D_A // HEAD_A
LORA_W = 64
LORA_A = 64
D_B = 3 * D_MODEL // 8
HEAD_B = 64
H_B = D_B // HEAD_B
G_B = 2
HG_B = H_B // G_B
N_B = 128
D_C = D_MODEL - D_A - D_B
HEAD_C = 128
H_C = D_C // HEAD_C
D_MIX = D_A + D_B + D_C
CHUNK = 128
CONV_K = 3
EPS = 1e-6
GN_EPS = 64e-5
DECAY_SCALE = 0.6065306597
D_RWKV_IN = 4 * D_A + LORA_W + LORA_A
D_XBC = D_B + 2 * G_B * N_B
IN_SIZES = (D_RWKV_IN,
            D_B, D_XBC, 2 * H_B,
            2 * D_C, D_C, D_C, D_C,
            2 * H_C, 2 * H_C)
D_IN = sum(IN_SIZES)

kernel_name = 'hybrid_rwkv7_ssd_mlstm_diffusion_step'


def _split_cols(u):
    pts, acc = [], 0
    for s in IN_SIZES[:-1]:
        acc += s
        pts.append(acc)
    return jnp.split(u, pts, axis=-1)


def _rmsnorm(x, g):
    x32 = x.astype(jnp.float32)
    y = x32 * lax.rsqrt(jnp.mean(x32 * x32, axis=-1, keepdims=True) + EPS)
    return (y * g.astype(jnp.float32)).astype(x.dtype)


def _conv_seq(u, taps, bias):
    w = taps[CONV_K // 2].astype(u.dtype)[:, None, :]
    pad = CONV_K // 2
    y = lax.conv_general_dilated(u, w, (1,), [(pad, pad)],
                                 dimension_numbers=('NWC', 'WIO', 'NWC'),
                                 feature_group_count=u.shape[-1])
    return y + bias.astype(u.dtype)


def _conv_grid(u, taps, bias):
    b, t, ch = u.shape
    rows = t // GRID_W
    ug = u.reshape(b, rows, GRID_W, ch)
    w = taps.astype(u.dtype)[:, :, None, :]
    pad = CONV_K // 2
    y = lax.conv_general_dilated(ug, w, (1, 1), [(pad, pad), (pad, pad)],
                                 dimension_numbers=('NHWC', 'HWIO', 'NHWC'),
                                 feature_group_count=ch)
    return y.reshape(b, t, ch) + bias.astype(u.dtype)


def _centred_shift(u):
    z = jnp.zeros_like(u[:, :1])
    prev = jnp.concatenate([z, u[:, :-1]], axis=1)
    nxt = jnp.concatenate([u[:, 1:], z], axis=1)
    return 0.5 * (prev + nxt)


def _rwkv_scan(s0, r, w, kh, b, kt, v):
    def step(s, inp):
        r_t, w_t, kh_t, b_t, kt_t, v_t = inp
        sa = jnp.einsum('bhvk,bhk->bhv', s, kh_t)
        s = s * w_t[:, :, None, :] - sa[..., None] * b_t[:, :, None, :] + v_t[..., None] * kt_t[:, :, None, :]
        return s, jnp.einsum('bhvk,bhk->bhv', s, r_t)
    xs = tuple(jnp.swapaxes(a, 0, 1) for a in (r, w, kh, b, kt, v))
    s_fin, y = lax.scan(step, s0, xs)
    return jnp.swapaxes(y, 0, 1), s_fin


def _rwkv_branch(u_r, p, s0):
    u = (u_r + p['rwkv_mu'] * (_centred_shift(u_r) - u_r)).astype(jnp.float32)
    r, k, v, g, wl, al = jnp.split(u, [D_A, 2 * D_A, 3 * D_A, 4 * D_A, 4 * D_A + LORA_W], axis=-1)
    bsz, t = r.shape[:2]
    heads = lambda a: a.reshape(bsz, t, H_A, HEAD_A)
    kk = heads(k * p['rwkv_kk'])
    kh = kk * lax.rsqrt(jnp.sum(kk * kk, axis=-1, keepdims=True) + 1e-12)
    rh, vh = heads(r), heads(v)
    wlt = jnp.tanh(wl)
    y = 0.0
    finals = []
    for d in range(2):
        w = jnp.exp(-DECAY_SCALE * jax.nn.sigmoid(p['rwkv_w0'][d] + wlt @ p['rwkv_wup'][d]))
        a = jax.nn.sigmoid(p['rwkv_a0'][d] + al @ p['rwkv_aup'][d])
        kt = heads(k * (1.0 + (a - 1.0) * p['rwkv_ka']))
        args = (rh, heads(w), kh, kh * heads(a), kt, vh)
        if d == 1:
            args = tuple(jnp.flip(x_, axis=1) for x_ in args)
        yd, sf = _rwkv_scan(s0[:, d].astype(jnp.float32), *args)
        if d == 1:
            yd = jnp.flip(yd, axis=1)
        bonus = jnp.sum(rh * kt * p['rwkv_rk'], axis=-1, keepdims=True) * vh
        y = y + yd + bonus
        finals.append(sf)
    mu = jnp.mean(y, axis=-1, keepdims=True)
    var = jnp.mean(jnp.square(y - mu), axis=-1, keepdims=True)
    y = ((y - mu) * lax.rsqrt(var + GN_EPS)).reshape(bsz, t, D_A) * p['rwkv_gnw'] + p['rwkv_gnb']
    return y * jax.nn.silu(g), jnp.stack(finals, axis=1)


def _ssd_scan(x, dta, bm, cm, h0):
    bsz, t = x.shape[:2]
    L = min(CHUNK, t)
    nc = t // L
    x = x.reshape(bsz, nc, L, G_B, HG_B, HEAD_B)
    a = dta.reshape(bsz, nc, L, G_B, HG_B)
    bm = bm.reshape(bsz, nc, L, G_B, N_B)
    cm = cm.reshape(bsz, nc, L, G_B, N_B)
    acs = jnp.cumsum(a, axis=2)
    mask = jnp.tril(jnp.ones((L, L), dtype=bool))[:, :, None, None]
    seg = acs[:, :, :, None] - acs[:, :, None]
    lmat = jnp.exp(jnp.where(mask, seg, -jnp.inf))
    y_diag = jnp.einsum('bclgn,bcsgn,bclsgh,bcsghp->bclghp', cm, bm, lmat, x)
    decay_st = jnp.exp(acs[:, :, -1:] - acs)
    states = jnp.einsum('bclgn,bclgh,bclghp->bcghpn', bm, decay_st, x)
    chunk_decay = jnp.exp(acs[:, :, -1])

    def step(h, inp):
        st, dec = inp
        return h * dec[..., None, None] + st, h
    h_fin, h_prev = lax.scan(step, h0, (jnp.moveaxis(states, 1, 0), jnp.moveaxis(chunk_decay, 1, 0)))
    h_prev = jnp.moveaxis(h_prev, 0, 1)
    y_off = jnp.einsum('bclgn,bcghpn,bclgh->bclghp', cm, h_prev, jnp.exp(acs))
    return (y_diag + y_off).reshape(bsz, t, G_B, HG_B, HEAD_B), h_fin


def _mamba_branch(z, xbc, dt_raw, p, conv_fn, h0):
    bsz, t = z.shape[:2]
    xbc = jax.nn.silu(conv_fn(xbc, p['ssm_conv'], p['ssm_conv_b'])).astype(jnp.float32)
    xs, bm, cm = jnp.split(xbc, [D_B, D_B + G_B * N_B], axis=-1)
    xs = xs.reshape(bsz, t, G_B, HG_B, HEAD_B)
    bm = bm.reshape(bsz, t, G_B, N_B)
    cm = cm.reshape(bsz, t, G_B, N_B)
    dt_raw = dt_raw.astype(jnp.float32)
    y = xs * p['ssm_d'].astype(jnp.float32).reshape(G_B, HG_B)[..., None]
    finals = []
    for d in range(2):
        dt = jax.nn.softplus(dt_raw[..., d * H_B:(d + 1) * H_B] + p['ssm_dt_bias'][d]).reshape(bsz, t, G_B, HG_B)
        dta = dt * (-jnp.exp(p['ssm_a_log'][d].astype(jnp.float32))).reshape(G_B, HG_B)
        args = (xs * dt[..., None], dta, bm, cm)
        if d == 1:
            args = tuple(jnp.flip(x_, axis=1) for x_ in args)
        yd, hf = _ssd_scan(*args, h0[:, d].astype(jnp.float32).reshape(bsz, G_B, HG_B, HEAD_B, N_B))
        if d == 1:
            yd = jnp.flip(yd, axis=1)
        y = y + yd
        finals.append(hf.reshape(bsz, H_B, HEAD_B, N_B))
    y = _rmsnorm(y.reshape(bsz, t, D_B) * jax.nn.silu(z.astype(jnp.float32)), p['ssm_norm'])
    return y, jnp.stack(finals, axis=1)


def _mlstm_scan(q, k, v, li, lf, c0, n0, m0):
    bsz, t = q.shape[:2]
    L = min(CHUNK, t)
    nc = t // L
    chunks = lambda a: jnp.moveaxis(a.reshape(bsz, nc, L, *a.shape[2:]), 1, 0)
    mask = jnp.tril(jnp.ones((L, L), dtype=bool))[None, :, :, None]

    def step(carry, inp):
        c, n, m = carry
        qc, kc, vc, lic, lfc = inp
        b = jnp.cumsum(lfc, axis=1)
        g = b + m[:, None]
        dmat = jnp.where(mask, b[:, :, None] - b[:, None] + lic[:, None], -jnp.inf)
        mt = jnp.maximum(g, jnp.max(dmat, axis=2))
        sc = jnp.einsum('bthd,bshd->btsh', qc, kc) * jnp.exp(dmat - mt[:, :, None])
        eg = jnp.exp(g - mt)
        num = jnp.einsum('btsh,bshd->bthd', sc, vc) + eg[..., None] * jnp.einsum('bhvk,bthk->bthv', c, qc)
        den = jnp.sum(sc, axis=2) + eg * jnp.einsum('bhk,bthk->bth', n, qc)
        h = num / jnp.maximum(jnp.abs(den), jnp.exp(-mt))[..., None]
        bl = b[:, -1]
        wlog = bl[:, None] - b + lic
        m_new = jnp.maximum(bl + m, jnp.max(wlog, axis=1))
        wk = jnp.exp(wlog - m_new[:, None])
        dec = jnp.exp(bl + m - m_new)
        c_new = dec[..., None, None] * c + jnp.einsum('bsh,bshv,bshk->bhvk', wk, vc, kc)
        n_new = dec[..., None] * n + jnp.einsum('bsh,bshk->bhk', wk, kc)
        return (c_new, n_new, m_new), h
    (cf, nf, mf), h = lax.scan(step, (c0, n0, m0), tuple(chunks(a) for a in (q, k, v, li, lf)))
    h = jnp.moveaxis(h, 0, 1).reshape(bsz, t, *q.shape[2:])
    return h, cf, nf, mf


def _mlstm_branch(qk, v, o, z, ig, fg, p, conv_fn, c0, n0, m0):
    bsz, t = v.shape[:2]
    heads = lambda a: a.reshape(bsz, t, H_C, HEAD_C)
    qk = jax.nn.silu(conv_fn(qk, p['ml_conv'], p['ml_conv_b'])).astype(jnp.float32)
    q = heads(qk[..., :D_C])
    k = heads(qk[..., D_C:]) * (HEAD_C ** -0.5)
    vh = heads(v.astype(jnp.float32))
    ig = ig.astype(jnp.float32)
    fg = fg.astype(jnp.float32)
    h = 0.0
    cs, ns, ms = [], [], []
    for d in range(2):
        li = ig[..., d * H_C:(d + 1) * H_C] + p['ml_ib'][d]
        lf = jax.nn.log_sigmoid(fg[..., d * H_C:(d + 1) * H_C] + p['ml_fb'][d])
        args = (q, k, vh, li, lf)
        if d == 1:
            args = tuple(jnp.flip(x_, axis=1) for x_ in args)
        hd, cf, nf, mf = _mlstm_scan(*args, c0[:, d].astype(jnp.float32), n0[:, d].astype(jnp.float32),
                                     m0[:, d].astype(jnp.float32))
        if d == 1:
            hd = jnp.flip(hd, axis=1)
        h = h + hd
        cs.append(cf)
        ns.append(nf)
        ms.append(mf)
    h = heads(jax.nn.sigmoid(o.astype(jnp.float32)) * h.reshape(bsz, t, D_C))
    mu = jnp.mean(h, axis=-1, keepdims=True)
    var = jnp.mean(jnp.square(h - mu), axis=-1, keepdims=True)
    h = ((h - mu) * lax.rsqrt(var + EPS)).reshape(bsz, t, D_C) * p['ml_norm']
    return h * jax.nn.silu(z.astype(jnp.float32)), jnp.stack(cs, axis=1), jnp.stack(ns, axis=1), jnp.stack(ms, axis=1)


def _mixer(h, lp, conv_fn, st):
    u = h @ lp['w_in']
    u_r, z_s, xbc, dt_raw, qk, v_m, o_m, z_m, i_m, f_m = _split_cols(u)
    y_a, s_r = _rwkv_branch(u_r, lp, st[0])
    y_b, s_s = _mamba_branch(z_s, xbc, dt_raw, lp, conv_fn, st[1])
    y_c, c_m, n_m, m_m = _mlstm_branch(qk, v_m, o_m, z_m, i_m, f_m, lp, conv_fn, st[2], st[3], st[4])
    y = jnp.concatenate([y_a, y_b, y_c], axis=-1).astype(h.dtype)
    return y @ lp['w_out'], (s_r, s_s, c_m, n_m, m_m)


def _layer(x, cvec, lp, conv_fn, st):
    mod = jax.nn.silu(cvec) @ lp['w_mod'] + lp['b_mod']
    shift, scale, gate = (m_[..., None, :] for m_ in jnp.split(mod, 3, axis=-1))
    h = _rmsnorm(x, lp['norm_g']) * (1.0 + scale) + shift
    y, new_st = _mixer(h, lp, conv_fn, st)
    return x + gate * y, new_st


def setup_inputs(seed: int = 0) -> dict:
    key = jax.random.key(seed)
    ks = iter(jax.random.split(key, 40))
    nrm = lambda shape, s: s * jax.random.normal(next(ks), shape, jnp.float32)
    uni = lambda shape, lo, hi: jax.random.uniform(next(ks), shape, jnp.float32, lo, hi)
    L = DEPTH
    return {
        'x_prompt': nrm((BATCH, SEQ, D_MODEL), 1.0),
        'x_sample': nrm((DEC_BATCH, DEC_SEQ, D_MODEL), 1.0),
        'c': nrm((DEC_BATCH, D_MODEL), 1.0),
        'c_ctx': nrm((D_MODEL,), 1.0),
        'state_rwkv': nrm((DEC_BATCH, L, 2, H_A, HEAD_A, HEAD_A), 0.5),
        'state_ssm': nrm((DEC_BATCH, L, 2, H_B, HEAD_B, N_B), 0.5),
        'state_mlstm_c': nrm((DEC_BATCH, L, 2, H_C, HEAD_C, HEAD_C), 0.5),
        'state_mlstm_n': nrm((DEC_BATCH, L, 2, H_C, HEAD_C), 0.5),
        'state_mlstm_m': nrm((DEC_BATCH, L, 2, H_C), 0.5),
        'norm_g': 1.0 + nrm((L, D_MODEL), 0.02),
        'w_mod': nrm((L, D_MODEL, 3 * D_MODEL), 0.5 * D_MODEL ** -0.5),
        'b_mod': nrm((L, 3 * D_MODEL), 0.01),
        'w_in': nrm((L, D_MODEL, D_IN), D_MODEL ** -0.5),
        'rwkv_mu': uni((L, D_RWKV_IN), 0.0, 1.0),
        'rwkv_w0': nrm((L, 2, D_A), 0.5),
        'rwkv_wup': nrm((L, 2, LORA_W, D_A), LORA_W ** -0.5),
        'rwkv_a0': nrm((L, 2, D_A), 0.5),
        'rwkv_aup': nrm((L, 2, LORA_A, D_A), LORA_A ** -0.5),
        'rwkv_kk': uni((L, D_A), 0.7, 1.0),
        'rwkv_ka': uni((L, D_A), 0.7, 1.0),
        'rwkv_rk': nrm((L, H_A, HEAD_A), 0.1),
        'rwkv_gnw': 1.0 + nrm((L, D_A), 0.02),
        'rwkv_gnb': nrm((L, D_A), 0.01),
        'ssm_conv': nrm((L, CONV_K, CONV_K, D_XBC), 1.0 / CONV_K),
        'ssm_conv_b': nrm((L, D_XBC), 0.01),
        'ssm_dt_bias': uni((L, 2, H_B), -4.6, -2.3),
        'ssm_a_log': jnp.log(uni((L, 2, H_B), 1.0, 16.0)),
        'ssm_d': 1.0 + nrm((L, H_B), 0.1),
        'ssm_norm': 1.0 + nrm((L, D_B), 0.02),
        'ml_conv': nrm((L, CONV_K, CONV_K, 2 * D_C), 1.0 / CONV_K),
        'ml_conv_b': nrm((L, 2 * D_C), 0.01),
        'ml_ib': nrm((L, 2, H_C), 0.1),
        'ml_fb': uni((L, 2, H_C), 3.0, 6.0),
        'ml_norm': 1.0 + nrm((L, D_C), 0.02),
        'w_out': nrm((L, D_MIX, D_MODEL), D_MIX ** -0.5),
        'final_g': 1.0 + nrm((D_MODEL,), 0.02),
    }


def reference(x_prompt, x_sample, c, c_ctx, state_rwkv, state_ssm, state_mlstm_c, state_mlstm_n,
              state_mlstm_m, norm_g, w_mod, b_mod, w_in, rwkv_mu, rwkv_w0, rwkv_wup, rwkv_a0, rwkv_aup,
              rwkv_kk, rwkv_ka, rwkv_rk, rwkv_gnw, rwkv_gnb, ssm_conv, ssm_conv_b, ssm_dt_bias, ssm_a_log,
              ssm_d, ssm_norm, ml_conv, ml_conv_b, ml_ib, ml_fb, ml_norm, w_out, final_g):
    lps = [dict(norm_g=norm_g[l], w_mod=w_mod[l], b_mod=b_mod[l], w_in=w_in[l], rwkv_mu=rwkv_mu[l],
                rwkv_w0=rwkv_w0[l], rwkv_wup=rwkv_wup[l], rwkv_a0=rwkv_a0[l], rwkv_aup=rwkv_aup[l],
                rwkv_kk=rwkv_kk[l], rwkv_ka=rwkv_ka[l], rwkv_rk=rwkv_rk[l], rwkv_gnw=rwkv_gnw[l],
                rwkv_gnb=rwkv_gnb[l], ssm_conv=ssm_conv[l], ssm_conv_b=ssm_conv_b[l],
                ssm_dt_bias=ssm_dt_bias[l], ssm_a_log=ssm_a_log[l], ssm_d=ssm_d[l], ssm_norm=ssm_norm[l],
                ml_conv=ml_conv[l], ml_conv_b=ml_conv_b[l], ml_ib=ml_ib[l], ml_fb=ml_fb[l],
                ml_norm=ml_norm[l], w_out=w_out[l]) for l in range(DEPTH)]

    bsz = x_prompt.shape[0]
    zero_st = (jnp.zeros((bsz, 2, H_A, HEAD_A, HEAD_A), jnp.float32),
               jnp.zeros((bsz, 2, H_B, HEAD_B, N_B), jnp.float32),
               jnp.zeros((bsz, 2, H_C, HEAD_C, HEAD_C), jnp.float32),
               jnp.zeros((bsz, 2, H_C, HEAD_C), jnp.float32),
               jnp.zeros((bsz, 2, H_C), jnp.float32))
    xp = x_prompt
    ctx_states = []
    for l in range(DEPTH):
        xp, st = _layer(xp, c_ctx, lps[l], _conv_seq, zero_st)
        ctx_states.append(st)
    y_prompt = _rmsnorm(xp, final_g)
    new_rwkv = jnp.stack([s[0] for s in ctx_states], axis=1)
    new_ssm = jnp.stack([s[1] for s in ctx_states], axis=1)
    new_mc = jnp.stack([s[2] for s in ctx_states], axis=1)
    new_mn = jnp.stack([s[3] for s in ctx_states], axis=1)
    new_mm = jnp.stack([s[4] for s in ctx_states], axis=1)

    xs = x_sample
    for l in range(DEPTH):
        cache = (state_rwkv[:, l], state_ssm[:, l], state_mlstm_c[:, l], state_mlstm_n[:, l], state_mlstm_m[:, l])
        xs, _ = _layer(xs, c, lps[l], _conv_grid, cache)
    y_sample = _rmsnorm(xs, final_g)
    return (y_prompt, y_sample, new_rwkv, new_ssm, new_mc, new_mn, new_mm)
```

```python
import contextlib
import numpy as np
import ml_dtypes
import concourse.bass as bass
import concourse.mybir as mybir
from concourse.alu_op_type import AluOpType as ALU
from concourse.bass_utils import run_bass_kernel_spmd

F32 = mybir.dt.float32
BF16 = mybir.dt.bfloat16
AF = mybir.ActivationFunctionType
AX = mybir.AxisListType

D_MODEL = 2048
DEPTH = 4
NT = 2048
NCH = 16
D_A = 768
D_B = 768
D_C = 512
D_IN = 7848
EPS = 1e-6
GN_EPS = 64e-5
DECAY_SCALE = 0.6065306597
NEG = -30000.0

TILES = []
for i in range(6): TILES.append((0 + 128 * i, 128))
for i in range(6): TILES.append((768 + 128 * i, 128))
for i in range(6): TILES.append((1536 + 128 * i, 128))
for i in range(6): TILES.append((2304 + 128 * i, 128))
TILES.append((3072, 128))
for i in range(6): TILES.append((3200 + 128 * i, 128))
for i in range(6): TILES.append((3968 + 128 * i, 128))
for i in range(2): TILES.append((4736 + 128 * i, 128))
for i in range(2): TILES.append((4992 + 128 * i, 128))
TILES.append((5248, 24))
for i in range(4): TILES.append((5272 + 128 * i, 128))
for i in range(4): TILES.append((5784 + 128 * i, 128))
for i in range(4): TILES.append((6296 + 128 * i, 128))
for i in range(4): TILES.append((6808 + 128 * i, 128))
for i in range(4): TILES.append((7320 + 128 * i, 128))
TILES.append((7832, 8))
TILES.append((7840, 8))
NTILES = len(TILES)

PCOL = {}
_n = 0
def _pc(name, n=1):
    global _n
    PCOL[name] = _n
    _n += n
_pc("mu", 25); _pc("kk", 6); _pc("ka", 6); _pc("rk", 6); _pc("gnw", 6); _pc("gnb", 6)
_pc("w0", 12); _pc("a0", 12)
_pc("sconv", 90); _pc("sconvb", 10); _pc("mconv", 72); _pc("mconvb", 8)
_pc("dtb"); _pc("alog"); _pc("ssmD", 6); _pc("ssmDrep", 12); _pc("ssmnorm", 6)
_pc("ib"); _pc("fb")
NPT = _n

CC = {}
_m = 0
def _cc(name, n):
    global _m
    CC[name] = _m
    _m += n
_cc("ident", 128); _cc("blockones", 128); _cc("ones", 128)
_cc("LT", 128); _cc("LE", 128); _cc("GT", 128); _cc("GE", 128)
_cc("biasLE", 128); _cc("biasGE", 128)
_cc("selF24", 1); _cc("selB24", 1); _cc("selF8", 1); _cc("selB8", 1)
_cc("keepc", 2048)
NCC = _m


def make_consts():
    c = np.zeros((128, NCC), np.float32)
    p = np.arange(128)[:, None]
    f = np.arange(128)[None, :]
    c[:, CC["ident"]:CC["ident"] + 128] = (p == f)
    c[:, CC["blockones"]:CC["blockones"] + 128] = (p // 64 == f // 64)
    c[:, CC["ones"]:CC["ones"] + 128] = 1.0
    c[:, CC["LT"]:CC["LT"] + 128] = (p < f)
    c[:, CC["LE"]:CC["LE"] + 128] = (p <= f)
    c[:, CC["GT"]:CC["GT"] + 128] = (p > f)
    c[:, CC["GE"]:CC["GE"] + 128] = (p >= f)
    c[:, CC["biasLE"]:CC["biasLE"] + 128] = np.where(p <= f, 0.0, NEG)
    c[:, CC["biasGE"]:CC["biasGE"] + 128] = np.where(p >= f, 0.0, NEG)
    pp = np.arange(128)
    c[:, CC["selF24"]] = (pp < 12)
    c[:, CC["selB24"]] = (pp >= 12) & (pp < 24)
    c[:, CC["selF8"]] = (pp < 4)
    c[:, CC["selB8"]] = (pp >= 4) & (pp < 8)
    kc = np.ones(2048, np.float32)
    kc[::128] = 0.0
    c[:, CC["keepc"]:CC["keepc"] + 2048] = kc[None, :]
    return c


WIN = 20000
NDMASEM = 12
NOSYNC_ENG = ("pe",)


class Buf:
    __slots__ = ("name", "writers", "readers", "excl")

    def __init__(self, name="", excl=False):
        self.name = name
        self.writers = []
        self.readers = []
        self.excl = excl


class V:
    __slots__ = ("ap", "buf")

    def __init__(self, ap, buf):
        self.ap = ap
        self.buf = buf

    def __getitem__(self, k):
        return V(self.ap[k], self.buf)

    def r(self, pat, **kw):
        return V(self.ap.rearrange(pat, **kw), self.buf)

    def bc(self, shape):
        return V(self.ap.to_broadcast(shape), self.buf)

    def cast(self, dt):
        return V(self.ap.bitcast(dt), self.buf)


def _ap(x):
    return x.ap if isinstance(x, V) else x


def _bufs(*xs):
    return [x.buf for x in xs if isinstance(x, V)]


class Sch:
    ENG = ("pe", "act", "dve", "pool", "sp")

    def __init__(self, nc):
        self.nc = nc
        self.ops = {e: [] for e in self.ENG}
        self.cnt = {e: 0 for e in self.ENG}
        self.dcnt = {e: 0 for e in self.ENG}
        self.seen = {e: {} for e in self.ENG}
        self.semkeys = set()
        self.out_events = []
        self.last_dma = {}
        self.delay_fn = {}
        self.delay_buf = {}

    def _filter(self, eng, evs):
        need = {}
        for (k, v) in evs:
            if need.get(k, 0) < v:
                need[k] = v
        res = []
        for k, v in need.items():
            if k[0] == "c" and k[1] == eng and eng in NOSYNC_ENG:
                continue
            if self.seen[eng].get(k, 0) >= v:
                continue
            self.seen[eng][k] = v
            res.append((k, v))
        return res

    @staticmethod
    def _deps(reads, writes, eng=None):
        evs = []
        for b in reads:
            evs += b.writers
            if b.excl:
                evs += [e for e in b.readers if e[0][1] != eng]
        for b in writes:
            evs += b.writers
            evs += b.readers
        return evs

    @staticmethod
    def _compact(evs):
        need = {}
        for (k, v) in evs:
            if need.get(k, 0) < v:
                need[k] = v
        return list(need.items())

    def _commit(self, ev, reads, writes):
        for b in reads:
            b.readers.append(ev)
            if len(b.readers) > 48:
                b.readers = self._compact(b.readers)
        for b in writes:
            b.writers = [ev]
            b.readers = []

    def op(self, eng, fn, reads=(), writes=(), _nodelay=False):
        if (not _nodelay) and eng in self.delay_fn:
            need = False
            for b in reads:
                if b.excl:
                    for (k, v) in b.writers:
                        if k[1] == "pe" and self.seen[eng].get(k, 0) < v:
                            need = True
            if need:
                self.op(eng, self.delay_fn[eng], reads=[b for b in reads if b.excl],
                        writes=[self.delay_buf[eng]], _nodelay=True)
        evs = self._filter(eng, self._deps(reads, writes, eng))
        i = self.cnt[eng]
        self.cnt[eng] += 1
        key = ("c", eng, i // WIN)
        val = (i % WIN) + 1
        self.semkeys.add(key)
        self.ops[eng].append((evs, fn, (key, 1)))
        ev = (key, val)
        self._commit(ev, reads, writes)
        return ev

    def dma(self, q, fn, reads=(), writes=(), is_output=False):
        i = self.dcnt[q]
        self.dcnt[q] += 1
        key = ("d", q, i % NDMASEM)
        val = 16 * (i // NDMASEM + 1)
        self.semkeys.add(key)
        evs = self._deps(reads, writes)
        if val > 16:
            evs = evs + [(key, val - 16)]
        evs = self._filter(q, evs)
        self.ops[q].append((evs, fn, (key, 16)))
        ev = (key, val)
        self.last_dma[key] = val
        self._commit(ev, reads, writes)
        if is_output:
            self.out_events.append(ev)
        return ev

    def barrier(self):
        evs = []
        for e in self.ENG:
            if self.cnt[e] > 0:
                i = self.cnt[e] - 1
                evs.append((("c", e, i // WIN), (i % WIN) + 1))
        for k, v in self.last_dma.items():
            evs.append((k, v))
        for e in self.ENG:
            f = self._filter(e, list(evs))
            if f:
                self.ops[e].append((f, None, None))

    def act(self, out, in_, func, bias=None, scale=None, accum=None):
        kw = {}
        if bias is not None:
            kw["bias"] = _ap(bias)
        if scale is not None:
            kw["scale"] = _ap(scale)
        if accum is not None:
            kw["accum_out"] = _ap(accum)
        o, i = _ap(out), _ap(in_)
        return self.op("act", lambda e: e.activation(out=o, in_=i, func=func, **kw),
                       reads=_bufs(in_, bias, scale), writes=_bufs(out, accum))

    def ts(self, eng, out, in0, s1, s2=None, op0=ALU.mult, op1=None):
        o, i, a, b = _ap(out), _ap(in0), _ap(s1), _ap(s2)
        if op1 is None:
            fn = lambda e: e.tensor_scalar(out=o, in0=i, scalar1=a, scalar2=None, op0=op0)
        else:
            fn = lambda e: e.tensor_scalar(out=o, in0=i, scalar1=a, scalar2=b, op0=op0, op1=op1)
        return self.op(eng, fn, reads=_bufs(in0, s1, s2), writes=_bufs(out))

    def tt(self, eng, out, in0, in1, op):
        o, a, b = _ap(out), _ap(in0), _ap(in1)
        return self.op(eng, lambda e: e.tensor_tensor(out=o, in0=a, in1=b, op=op),
                       reads=_bufs(in0, in1), writes=_bufs(out))

    def stt(self, out, in0, scalar, in1, op0, op1):
        o, a, sc, b = _ap(out), _ap(in0), _ap(scalar), _ap(in1)
        return self.op("dve", lambda e: e.scalar_tensor_tensor(out=o, in0=a, scalar=sc, in1=b, op0=op0, op1=op1),
                       reads=_bufs(in0, scalar, in1), writes=_bufs(out))

    def cp(self, eng, out, in_):
        o, i = _ap(out), _ap(in_)
        if eng == "act":
            return self.op("act", lambda e: e.activation(out=o, in_=i, func=AF.Copy),
                           reads=_bufs(in_), writes=_bufs(out))
        return self.op(eng, lambda e: e.tensor_copy(out=o, in_=i), reads=_bufs(in_), writes=_bufs(out))

    def memset(self, eng, out, val):
        o = _ap(out)
        return self.op(eng, lambda e: e.memset(o, val), writes=_bufs(out))

    def recip(self, out, in_):
        o, i = _ap(out), _ap(in_)
        return self.op("dve", lambda e: e.reciprocal(out=o, in_=i), reads=_bufs(in_), writes=_bufs(out))

    def scan(self, out, d0, d1, initial, op0, op1):
        o, a, b, ini = _ap(out), _ap(d0), _ap(d1), _ap(initial)
        return self.op("dve", lambda e: e.tensor_tensor_scan(out=o, data0=a, data1=b, initial=ini, op0=op0, op1=op1),
                       reads=_bufs(d0, d1, initial), writes=_bufs(out))

    def reduce(self, out, in_, op, axis=None):
        o, i = _ap(out), _ap(in_)
        ax = AX.X if axis is None else axis
        return self.op("dve", lambda e: e.tensor_reduce(out=o, in_=i, axis=ax, op=op),
                       reads=_bufs(in_), writes=_bufs(out))

    def mm(self, out, lhsT, rhs, start=True, stop=True):
        o, a, b = _ap(out), _ap(lhsT), _ap(rhs)
        return self.op("pe", lambda e: e.matmul(o, lhsT=a, rhs=b, start=start, stop=stop),
                       reads=_bufs(lhsT, rhs), writes=_bufs(out))

    def tr(self, out, in_, ident):
        o, a, b = _ap(out), _ap(in_), _ap(ident)
        return self.op("pe", lambda e: e.transpose(out=o, in_=a, identity=b),
                       reads=_bufs(in_, ident), writes=_bufs(out))

    def ld(self, out, src, q="sp"):
        o, i = _ap(out), _ap(src)
        return self.dma(q, lambda e: e.dma_start(out=o, in_=i), reads=_bufs(src), writes=_bufs(out))

    def st(self, dst, in_, q="pool", is_output=False):
        o, i = _ap(dst), _ap(in_)
        return self.dma(q, lambda e: e.dma_start(out=o, in_=i), reads=_bufs(in_), writes=_bufs(dst),
                        is_output=is_output)

    def emit(self):
        nc = self.nc
        keys = sorted(self.semkeys)
        with contextlib.ExitStack() as st:
            sems = {}
            for k in keys:
                sems[k] = st.enter_context(nc.semaphore("s_%s_%s_%d" % k))
            fin = list(self.out_events)
            for e in self.ENG:
                if self.cnt[e] > 0 and e != "sp":
                    i = self.cnt[e] - 1
                    fin.append((("c", e, i // WIN), (i % WIN) + 1))
            for k, v in self.last_dma.items():
                fin.append((k, v))
            fin = self._filter("sp", fin)
            block = st.enter_context(nc.Block())

            def run(engobj, name, extra=None):
                for (evs, fn, inc) in self.ops[name]:
                    for (k, v) in evs:
                        engobj.wait_ge(sems[k], v)
                    if fn is not None:
                        fn(engobj).then_inc(sems[inc[0]], inc[1])
                if extra:
                    for (k, v) in extra:
                        engobj.wait_ge(sems[k], v)

            @block.tensor
            def _(e):
                run(e, "pe")

            @block.scalar
            def _(e):
                run(e, "act")

            @block.vector
            def _(e):
                run(e, "dve")

            @block.gpsimd
            def _(e):
                run(e, "pool")

            @block.sync
            def _(e):
                run(e, "sp", fin)


class Arena:
    def __init__(self, ap, ncols):
        self.ap = ap
        self.n = ncols
        self.off = 0
        self.peak = 0

    def f32(self, ncols, name=""):
        nal = (ncols + 7) // 8 * 8
        assert self.off + nal <= self.n, ("arena overflow", name, self.off, nal, self.n)
        v = V(self.ap[:, self.off:self.off + ncols], Buf(name))
        self.off += nal
        self.peak = max(self.peak, self.off)
        return v

    def bf16(self, ncols, name=""):
        assert ncols % 2 == 0
        v = self.f32(ncols // 2, name)
        return V(v.ap.bitcast(BF16), v.buf)

    def mark(self):
        return self.off

    def reset(self, m):
        self.off = m


ARENA_COLS = 44000
RW_PREACT = 2


def build(depth=DEPTH, debug=False, phases=("mod", "norm", "proj", "rwkv", "ssd", "mlstm", "out", "final")):
    nc = bass.Bass("TRN2", target_bir_lowering=False)

    def din(name, shape, dt=F32):
        return nc.dram_tensor(name, list(shape), dt, kind="ExternalInput").ap()

    def dout(name, shape, dt=F32):
        return nc.dram_tensor(name, list(shape), dt, kind="ExternalOutput").ap()

    L = depth
    x0 = din("x0", [NT, D_MODEL])
    cond_d = din("cond", [128, 16])
    cst_d = din("cst", [128, NCC])
    ptab_d = din("ptab", [L, 128, NPT])
    tokmask_d = din("tokmask", [4, NT])
    tokmaskb_d = din("tokmaskb", [4, NT], BF16)
    keep_d = din("keep", [128, 1])
    wmod_d = din("w_mod", [L, D_MODEL, 3 * D_MODEL])
    bmodT_d = din("b_modT", [L, 128, 48])
    ngT_d = din("norm_gT", [L, 128, 16])
    win_d = din("w_in", [L, D_MODEL, D_IN])
    wout_d = din("w_out", [L, D_MODEL, D_MODEL])
    lora_d = din("lora", [L, 128, 2, D_A])
    mlnorm_d = din("ml_norm", [L, D_C])
    fg_d = din("final_g", [D_MODEL])
    srw_d = din("srw", [L, 2, 12, 64, 64])
    sss_d = din("sss", [L, 2, 12, 64, 128])
    smc_d = din("smc", [L, 2, 4, 128, 128])
    smn_d = din("smn", [L, 2, 4, 128])
    smm_d = din("smm", [L, 8])

    y_d = dout("y", [NT, D_MODEL])
    nrw_d = dout("nrw", [8, L, 2, 12, 64, 64])
    nss_d = dout("nss", [8, L, 2, 12, 64, 128])
    nmc_d = dout("nmc", [8, L, 2, 4, 128, 128])
    nmn_d = dout("nmn", [8, L, 2, 4, 128])
    nmm_d = dout("nmm", [8, L, 8])
    if debug:
        uT_d = dout("uT", [NTILES * 128, NT])
        yT_d = dout("yTs", [D_MODEL, NT], BF16)
    else:
        uT_d = nc.dram_tensor("uT", [NTILES * 128, NT], F32).ap()
        yT_d = nc.dram_tensor("yTs", [D_MODEL, NT], BF16).ap()
    gscr_d = nc.dram_tensor("gscr", [L, 16, 128], F32).ap()

    with contextlib.ExitStack() as stack:
        A_t = stack.enter_context(nc.sbuf_tensor("arena", [128, ARENA_COLS], F32))
        PS_t = stack.enter_context(nc.psum_tensor("psum", [128, 4096], F32))
        s = Sch(nc)
        ar = Arena(A_t[:, :], ARENA_COLS)

        psbufs = {}

        def ps(bank, c0=0, c1=512, sub=0, p0=0, p1=128):
            key = (bank, 0)
            if key not in psbufs:
                psbufs[key] = Buf("ps%d_%d" % key, excl=True)
            return V(PS_t[p0:p1, bank * 512 + c0: bank * 512 + c1], psbufs[key])

        def psb(bank, c0=0, c1=1024, sub=0, p0=0, p1=128):
            key = (bank, 0)
            if key not in psbufs:
                psbufs[key] = Buf("ps%d_%d" % key, excl=True)
            return V(PS_t[p0:p1, bank * 512: bank * 512 + 512].bitcast(BF16)[:, c0:c1], psbufs[key])

        XB = [Buf("x%d" % c) for c in range(NCH)]
        UB = [Buf("u%d" % j) for j in range(NTILES)]
        YB = [Buf("y%d" % j) for j in range(16)]
        GB = [Buf("g%d" % l) for l in range(L)]
        OUTB = Buf("stateout")

        cst = ar.f32(NCC - 2048, "cst")
        s.ld(cst, V(cst_d[:, 0:NCC - 2048], Buf()))
        ident = cst[:, CC["ident"]:CC["ident"] + 128]
        onesf = cst[:, CC["ones"]:CC["ones"] + 128]
        identb = ar.bf16(128, "identb")
        s.cp("dve", identb, ident)
        bonesb = ar.bf16(128, "bonesb")
        s.cp("dve", bonesb, cst[:, CC["blockones"]:CC["blockones"] + 128])
        onesb = ar.bf16(128, "onesb")
        s.cp("dve", onesb, onesf)
        keepcol = ar.f32(8, "keep")
        s.ld(keepcol[:, 0:1], V(keep_d, Buf()))
        ptab = ar.f32(L * NPT, "ptab")
        s.ld(ptab.r("p (l n) -> p l n", l=L), V(ptab_d.rearrange("l p n -> p l n"), Buf()))
        scrA = ar.f32(8, "scrA")
        scrD = ar.f32(8, "scrD")
        _ia, _sa, _sd = ident.ap[:, 0:1], scrA.ap[:, 0:1], scrD.ap[:, 0:8]
        s.delay_fn["act"] = lambda e: e.activation(out=_sa, in_=_ia, func=AF.Copy)
        s.delay_fn["dve"] = lambda e: e.memset(_sd, 0.0)
        s.delay_buf["act"] = scrA.buf
        s.delay_buf["dve"] = scrD.buf
        modT = [ar.f32(48, "modT%d" % l) for l in range(L)]
        Gsc = [ar.f32(16, "Gsc%d" % l) for l in range(L)]

        def PT(l, name, i=0):
            c = l * NPT + PCOL[name] + i
            return ptab[:, c:c + 1]

        def cmask(name):
            return cst[:, CC[name]:CC[name] + 128]

        MARK0 = ar.mark()

        def newphase():
            s.barrier()
            ar.reset(MARK0)

        def phase_mod():
            newphase()
            condT = ar.f32(16, "condT")
            s.ld(condT, V(cond_d, Buf()))
            scond = ar.f32(16, "scond")
            s.act(scond, condT, AF.Silu)
            bm = ar.f32(48 * L, "bm")
            s.ld(bm.r("p (l n) -> p l n", l=L), V(bmodT_d.rearrange("l p n -> p l n"), Buf()))
            ng = ar.f32(16 * L, "ng")
            s.ld(ng.r("p (l n) -> p l n", l=L), V(ngT_d.rearrange("l p n -> p l n"), Buf()))
            WM = [ar.f32(16 * 384, "wm%d" % i) for i in range(2)]
            gsb = ar.f32(128, "gsb")
            it = 0
            for l in range(L):
                pm = ps(l % 2, 0, 48)
                for nb in range(16):
                    wm = WM[it % 2]
                    it += 1
                    wm3 = wm.r("p (k n) -> p k n", k=16)
                    s.ld(wm3, V(wmod_d[l][:, nb * 384:(nb + 1) * 384].rearrange("(k p) n -> p k n", p=128), Buf()))
                    for j in range(3):
                        n = nb * 3 + j
                        for kc in range(16):
                            s.mm(pm[:, n:n + 1], wm3[:, kc, j * 128:(j + 1) * 128], scond[:, kc:kc + 1],
                                 start=(kc == 0), stop=(kc == 15))
                s.tt("dve", modT[l], pm, bm[:, l * 48:(l + 1) * 48], ALU.add)
                s.stt(Gsc[l], modT[l][:, 16:32], 1.0, ng[:, l * 16:(l + 1) * 16], ALU.add, ALU.mult)
                pg = ps(2 + l % 2, 0, 128, p0=0, p1=16)
                s.tr(pg, modT[l][:, 32:48], ident)
                s.cp("act", gsb[0:16, :], pg)
                s.st(V(gscr_d[l], GB[l]), gsb[0:16, :], q="sp")

        def phase_normproj(l):
            newphase()
            xsrc = x0 if l == 0 else y_d
            hT = ar.bf16(16 * NT, "hT")
            hT3 = hT.r("p (k t) -> p k t", k=16)
            hTB = [Buf("hT%d" % i) for i in range(4)]
            XT = [ar.f32(NT, "xt%d" % i) for i in range(2)]
            XN = [ar.bf16(NT, "xn%d" % i) for i in range(2)]
            junk = ar.bf16(NT, "junk")
            ss = ar.f32(16, "ss")
            rs = ar.f32(16, "rs")
            shift = modT[l][:, 0:16]
            for c in range(NCH):
                xt = XT[c % 2]
                xn = XN[c % 2]
                src = V(xsrc[c * 128:(c + 1) * 128, :], XB[c] if l > 0 else Buf())
                s.ld(xt, src)
                s.act(junk, xt, AF.Square, accum=ss[:, c:c + 1])
                s.ts("dve", rs[:, c:c + 1], ss[:, c:c + 1], 1.0 / D_MODEL, EPS, ALU.mult, ALU.add)
                s.act(rs[:, c:c + 1], rs[:, c:c + 1], AF.Sqrt)
                s.recip(rs[:, c:c + 1], rs[:, c:c + 1])
                s.ts("dve", xn, xt, rs[:, c:c + 1], None, ALU.mult)
                for half in range(2):
                    pb = psb(6 + half)
                    for k in range(8):
                        kk = half * 8 + k
                        s.tr(pb[:, k * 128:(k + 1) * 128], xn[:, kk * 128:(kk + 1) * 128], identb)
                    for k in range(8):
                        kk = half * 8 + k
                        dst = V(hT3.ap[:, kk, c * 128:(c + 1) * 128], hTB[c // 4])
                        s.act(dst, pb[:, k * 128:(k + 1) * 128], AF.Identity,
                              bias=shift[:, kk:kk + 1], scale=Gsc[l][:, kk:kk + 1])
            WF = [ar.f32(16 * 128, "wf%d" % i) for i in range(2)]
            WBt = [ar.bf16(16 * 128, "wb%d" % i) for i in range(2)]
            UST = [ar.f32(NT, "ust%d" % i) for i in range(2)]

            def load_w(j):
                c0, w = TILES[j]
                wf3 = WF[j % 2].r("p (k n) -> p k n", k=16)
                wb3 = WBt[j % 2].r("p (k n) -> p k n", k=16)
                s.ld(wf3[:, :, 0:w], V(win_d[l][:, c0:c0 + w].rearrange("(k p) n -> p k n", p=128), Buf()))
                s.cp("pool" if j % 2 else "act", wb3[:, :, 0:w], wf3[:, :, 0:w])

            load_w(0)
            for j in range(NTILES):
                c0, w = TILES[j]
                if j + 1 < NTILES:
                    load_w(j + 1)
                wb3 = WBt[j % 2].r("p (k n) -> p k n", k=16)
                ust = UST[j % 2]
                for tb in range(4):
                    pp = ps((j % 2) * 4 + tb, p0=0, p1=w)
                    for kc in range(16):
                        rhs = V(hT3.ap[:, kc, tb * 512:(tb + 1) * 512], hTB[tb])
                        s.mm(pp, wb3[:, kc, 0:w], rhs, start=(kc == 0), stop=(kc == 15))
                    s.cp("act" if tb % 2 else "dve", ust[0:w, tb * 512:(tb + 1) * 512], pp)
                s.st(V(uT_d[j * 128:j * 128 + w, :], UB[j]), ust[0:w, :], q="pool", is_output=debug)

        def phase_out(l):
            newphase()
            xsrc = x0 if l == 0 else y_d
            yT = ar.bf16(16 * NT, "yT")
            yT3 = yT.r("p (k t) -> p k t", k=16)
            yTB = [Buf("yT%d" % i) for i in range(16)]
            for k in range(16):
                s.ld(V(yT3.ap[:, k, :], yTB[k]), V(yT_d[k * 128:(k + 1) * 128, :], YB[k]))
            grow = ar.f32(D_MODEL, "grow")
            s.ld(grow, V(gscr_d[l].rearrange("a b -> (a b)").partition_broadcast(128), GB[l]))
            WF = [ar.f32(16 * 256, "wf%d" % i) for i in range(2)]
            WBt = [ar.bf16(16 * 256, "wb%d" % i) for i in range(2)]
            XT = [ar.f32(256, "xt%d" % i) for i in range(3)]
            TM = [ar.f32(256, "tm%d" % i) for i in range(3)]

            def load_w(jb):
                wf3 = WF[jb % 2].r("p (k n) -> p k n", k=16)
                wb3 = WBt[jb % 2].r("p (k n) -> p k n", k=16)
                s.ld(wf3, V(wout_d[l][:, jb * 256:(jb + 1) * 256].rearrange("(k p) n -> p k n", p=128), Buf()))
                s.cp("pool" if jb % 2 else "act", wb3, wf3)

            load_w(0)
            it = 0
            for jb in range(8):
                if jb + 1 < 8:
                    load_w(jb + 1)
                wb3 = WBt[jb % 2].r("p (k n) -> p k n", k=16)
                for c in range(NCH):
                    xt = XT[it % 3]
                    tm = TM[it % 3]
                    pp = ps(it % 4, 0, 256)
                    it += 1
                    s.ld(xt, V(xsrc[c * 128:(c + 1) * 128, jb * 256:(jb + 1) * 256], XB[c] if l > 0 else Buf()))
                    for kc in range(16):
                        s.mm(pp, V(yT3.ap[:, kc, c * 128:(c + 1) * 128], yTB[kc]), wb3[:, kc, :],
                             start=(kc == 0), stop=(kc == 15))
                    s.tt("dve", tm, pp, grow[:, jb * 256:(jb + 1) * 256], ALU.mult)
                    s.tt("pool", tm, tm, xt, ALU.add)
                    s.st(V(y_d[c * 128:(c + 1) * 128, jb * 256:(jb + 1) * 256], XB[c]), tm, q="sp")

        def phase_final():
            newphase()
            fgrow = ar.f32(D_MODEL, "fgrow")
            s.ld(fgrow, V(fg_d.partition_broadcast(128), Buf()))
            XT = [ar.f32(NT, "xt%d" % i) for i in range(2)]
            YO = [ar.f32(NT, "yo%d" % i) for i in range(2)]
            junk = ar.bf16(NT, "junk")
            ss = ar.f32(16, "ss")
            rs = ar.f32(16, "rs")
            for c in range(NCH):
                xt = XT[c % 2]
                s.ld(xt, V(y_d[c * 128:(c + 1) * 128, :], XB[c]))
                s.act(junk, xt, AF.Square, accum=ss[:, c:c + 1])
                s.ts("dve", rs[:, c:c + 1], ss[:, c:c + 1], 1.0 / D_MODEL, EPS, ALU.mult, ALU.add)
                s.act(rs[:, c:c + 1], rs[:, c:c + 1], AF.Sqrt)
                s.recip(rs[:, c:c + 1], rs[:, c:c + 1])
                s.stt(YO[c % 2], xt, rs[:, c:c + 1], fgrow, ALU.mult, ALU.mult)
                s.st(V(y_d[c * 128:(c + 1) * 128, :], XB[c]), YO[c % 2], q="sp", is_output=True)

        def chunk_order(d):
            return list(range(NCH)) if d == 0 else list(range(NCH - 1, -1, -1))

        def is_boundary(c, d):
            return (c % 2 == 0) if d == 0 else (c % 2 == 1)

        def is_seq_end(c, d):
            return (c % 2 == 1) if d == 0 else (c % 2 == 0)

        def run_pipeline(n, pre_fn, seq_fn, NB, PREACT):
            active = []
            nxt_pre = 0
            nxt_seq = 0
            seq_done = 0
            pre_done = [False] * n
            while seq_done < n:
                while (nxt_pre < n and nxt_pre < seq_done + NB
                       and sum(1 for a in active if a[0] == "p") < PREACT):
                    active.append(("p", nxt_pre, pre_fn(nxt_pre)))
                    nxt_pre += 1
                if (nxt_seq < n and pre_done[nxt_seq] and nxt_seq == seq_done
                        and not any(a[0] == "s" for a in active)):
                    active.append(("s", nxt_seq, seq_fn(nxt_seq)))
                    nxt_seq += 1
                assert active
                for a in list(active):
                    try:
                        next(a[2])
                    except StopIteration:
                        active.remove(a)
                        if a[0] == "p":
                            pre_done[a[1]] = True
                        else:
                            seq_done += 1

        def conv_silu(l, tile, tapname, ti, biasname, CP, XL, XR, ACC, mL, mR, out, scale=None):
            s.ld(CP[:, 65:65 + NT], V(uT_d[tile * 128:(tile + 1) * 128, :], UB[tile]))
            s.tt("pool", XL[:, 64:64 + NT], CP[:, 64:64 + NT], mL, ALU.mult)
            s.tt("pool", XR[:, 64:64 + NT], CP[:, 66:66 + NT], mR, ALU.mult)
            first = True
            for kh in range(3):
                for kw in range(3):
                    tap = PT(l, tapname, ti * 9 + kh * 3 + kw)
                    off = 64 * (kh - 1)
                    if kw == 0:
                        src = XL[:, 64 + off:64 + off + NT]
                    elif kw == 1:
                        src = CP[:, 65 + off:65 + off + NT]
                    else:
                        src = XR[:, 64 + off:64 + off + NT]
                    if first:
                        s.ts("dve", ACC, src, tap, None, ALU.mult)
                        first = False
                    else:
                        s.stt(ACC, src, tap, ACC, ALU.mult, ALU.add)
            s.act(out, ACC, AF.Silu, bias=PT(l, biasname, ti))
            if scale is not None:
                s.ts("pool", out, out, scale, None, ALU.mult)

        def phase_ssd(l):
            newphase()
            biasM = [cmask("biasLE"), cmask("biasGE")]
            keepc24 = ar.f32(NT, "keepc24")
            s.ld(keepc24[0:24, :], V(cst_d[0:24, CC["keepc"]:CC["keepc"] + NT], Buf()))
            xsT = ar.bf16(16 * 768, "xsT")
            xsT3 = xsT.r("p (c f) -> p c f", c=16)
            BT = ar.bf16(16 * 256, "BT")
            BT4 = BT.r("p (c g n) -> p c g n", c=16, g=2)
            BTf = [ar.bf16(NT, "BTf%d" % g) for g in range(2)]
            CTf = [ar.bf16(NT, "CTf%d" % g) for g in range(2)]
            GT = ar.bf16(16 * 256, "GT")
            GT4 = GT.r("p (c g n) -> p c g n", c=16, g=2)
            DI = ar.bf16(12 * 128, "DI")
            DI3 = DI.r("p (h n) -> p h n", h=12)
            sT = ar.f32(16 * 72, "sT")
            sT3 = sT.r("p (c n) -> p c n", c=16)
            A2 = ar.f32(NT, "A2")
            M1 = ar.mark()
            CPs = [ar.f32(2178, "cp%d" % i) for i in range(2)]
            XL = ar.f32(2176, "xl")
            XR = ar.f32(2176, "xr")
            ACC = ar.f32(NT, "acc")
            mL = ar.f32(NT, "mL")
            mR = ar.f32(NT, "mR")
            XF = [ar.bf16(NT, "xf%d" % i) for i in range(2)]
            s.ld(mL, V(tokmask_d[2].partition_broadcast(128), Buf()))
            s.ld(mR, V(tokmask_d[3].partition_broadcast(128), Buf()))
            for i in range(2):
                s.memset("pool", CPs[i], 0.0)
            s.memset("pool", XL, 0.0)
            s.memset("pool", XR, 0.0)
            for i in range(10):
                if i < 6:
                    out = XF[i % 2]
                elif i < 8:
                    out = BTf[i - 6]
                else:
                    out = CTf[i - 8]
                conv_silu(l, 31 + i, "sconv", i, "sconvb", CPs[i % 2], XL, XR, ACC, mL, mR, out)
                if i < 8:
                    for half in range(2):
                        pb = psb(6 + half)
                        for k in range(8):
                            c = half * 8 + k
                            s.tr(pb[:, k * 128:(k + 1) * 128], out[:, c * 128:(c + 1) * 128], identb)
                        if i < 6:
                            dst = V(xsT3.ap[:, half * 8:half * 8 + 8, i * 128:(i + 1) * 128], xsT.buf)
                        else:
                            dst = V(BT4.ap[:, half * 8:half * 8 + 8, i - 6, :], BT.buf)
                        s.cp("act" if half else "dve", dst, pb.r("p (k n) -> p k n", k=8))
            for c in range(NCH):
                pg = ps(4 + c % 2, 0, 256)
                for g in range(2):
                    s.mm(pg[:, g * 128:(g + 1) * 128], BTf[g][:, c * 128:(c + 1) * 128], CTf[g][:, c * 128:(c + 1) * 128])
                s.cp("act" if c % 2 else "dve", V(GT4.ap[:, c, :, :], GT.buf), pg.r("p (g n) -> p g n", g=2))
            for h in range(12):
                s.ts("pool", DI3[:, h, :], ident, PT(l, "ssmDrep", h), None, ALU.mult)
            s.barrier()
            ar.reset(M1)
            DT = ar.f32(NT, "DT")
            E1 = ar.f32(NT, "E1")
            DTA = ar.f32(NT, "DTA")
            PF = ar.f32(NT, "PF")
            SUF = ar.f32(NT, "SUF")
            XD = ar.f32(NT, "XD")
            nega = ar.f32(8, "nega")
            R24 = slice(0, 24)
            s.ld(DT[R24, :], V(uT_d[41 * 128:41 * 128 + 24, :], UB[41]))
            s.act(E1[R24, :], DT[R24, :], AF.Exp, bias=PT(l, "dtb")[R24, :])
            s.act(DT[R24, :], E1[R24, :], AF.Ln, bias=1.0)
            s.act(nega[R24, 0:1], PT(l, "alog")[R24, :], AF.Exp)
            s.ts("dve", nega[R24, 0:1], nega[R24, 0:1], -1.0, None, ALU.mult)
            s.ts("dve", DTA[R24, :], DT[R24, :], nega[R24, 0:1], None, ALU.mult)
            s.scan(PF[R24, :], keepc24[R24, :], DTA[R24, :], 0.0, ALU.mult, ALU.add)
            PF3 = PF.r("p (c t) -> p c t", c=16)
            TOTb = PF3[R24, :, 127:128].bc([24, 16, 128])
            SUF3 = SUF.r("p (c t) -> p c t", c=16)
            s.tt("dve", SUF3[R24], TOTb, PF3[R24], ALU.subtract)
            s.tt("dve", SUF[R24, :], SUF[R24, :], DTA[R24, :], ALU.add)
            s.ts("dve", A2[R24, :], PF[R24, :], cst[R24, CC["selF24"]:CC["selF24"] + 1], None, ALU.mult)
            s.stt(A2[R24, :], SUF[R24, :], cst[R24, CC["selB24"]:CC["selB24"] + 1], A2[R24, :], ALU.mult, ALU.add)
            A23 = A2.r("p (c t) -> p c t", c=16)
            E13 = E1.r("p (c t) -> p c t", c=16)
            s.tt("dve", E13[R24], TOTb, A23[R24], ALU.subtract)
            s.act(E1[R24, :], E1[R24, :], AF.Exp)
            s.tt("dve", XD[R24, :], DT[R24, :], E1[R24, :], ALU.mult)
            s.ts("dve", SUF[R24, :], A2[R24, :], -1.0, None, ALU.mult)
            for c in range(NCH):
                pt_ = ps(4 + c % 2, 0, 72)
                cs = slice(c * 128, (c + 1) * 128)
                s.tr(pt_[:, 0:24], SUF[R24, cs], ident[R24, 0:24])
                s.tr(pt_[:, 24:48], DT[R24, cs], ident[R24, 0:24])
                s.tr(pt_[:, 48:72], XD[R24, cs], ident[R24, 0:24])
                s.cp("act" if c % 2 else "dve", sT3[:, c, :], pt_)
            s.barrier()
            ar.reset(M1)
            ZF = ar.f32(NT, "ZF")
            YP = ar.f32(NT, "YP")
            YG = [ar.bf16(NT, "YG%d" % j) for j in range(6)]
            SQ = ar.bf16(NT, "SQ")
            SS = ar.f32(NT, "SS")
            A2H = [ar.f32(NT, "A2H%d" % i) for i in range(2)]
            HSp = ar.f32(128, "HSp")
            HBP = [ar.bf16(128, "HBP%d" % i) for i in range(2)]
            XDTP = [[ar.bf16(128, "XDTP%d%d" % (i, r)) for r in range(2)] for i in range(2)]
            XSP = [[ar.bf16(128, "XSP%d%d" % (i, r)) for r in range(2)] for i in range(2)]
            XDEC = [[ar.bf16(64, "XDEC%d%d" % (i, r)) for r in range(2)] for i in range(2)]
            ARG = [[ar.f32(128, "ARG%d%d" % (i, r)) for r in range(2)] for i in range(2)]
            EA = [[ar.f32(128, "EA%d%d" % (i, r)) for r in range(2)] for i in range(2)]
            MT = [[ar.bf16(128, "MT%d%d" % (i, r)) for r in range(2)] for i in range(2)]
            CD = [[ar.bf16(128, "CD%d%d" % (i, r)) for r in range(2)] for i in range(2)]
            SIN = ar.f32(128, "SIN")
            SO = [ar.f32(128, "SO%d" % i) for i in range(2)]
            for hh in range(2):
                s.memset("pool", HBP[hh], 0.0)
                for r in range(2):
                    s.memset("pool", XDTP[hh][r], 0.0)
                    s.memset("pool", XSP[hh][r], 0.0)
            sidx = 0
            for j in range(6):
                g = j // 3
                s.ld(ZF, V(uT_d[(25 + j) * 128:(26 + j) * 128, :], UB[25 + j]))
                s.act(ZF, ZF, AF.Silu)
                for d in range(2):
                    for hh in range(2):
                        hd = d * 12 + 2 * j + hh
                        s.ts("dve", A2H[hh][R24, :], A2[R24, :], ident[R24, hd:hd + 1], None, ALU.mult)
                    s.ld(SIN, V(sss_d[l, d, 2 * j:2 * j + 2].rearrange("h p n -> (h p) n"), Buf()))
                    pi = ps(7, 0, 128)
                    s.tr(pi, SIN, ident)
                    s.cp("act", HSp, pi)
                    for hh in range(2):
                        s.cp("dve", HBP[hh][:, hh * 64:(hh + 1) * 64], HSp[:, hh * 64:(hh + 1) * 64])
                    order = chunk_order(d)

                    def ssd_pre(ci, j=j, d=d, g=g, order=order):
                        c = order[ci]
                        r = ci % 2
                        cs = slice(c * 128, (c + 1) * 128)
                        pa = ps(r, 0, 256)
                        for hh in range(2):
                            h = 2 * j + hh
                            hd = d * 12 + h
                            pah = pa[:, hh * 128:(hh + 1) * 128]
                            s.mm(pah, onesf[R24, :], A2H[hh][R24, cs])
                        for hh in range(2):
                            h = 2 * j + hh
                            hd = d * 12 + h
                            pah = pa[:, hh * 128:(hh + 1) * 128]
                            s.stt(ARG[hh][r], pah, sT3[:, c, hd:hd + 1], biasM[d], ALU.add, ALU.add)
                            s.act(EA[hh][r], pah, AF.Exp)
                            s.act(ARG[hh][r], ARG[hh][r], AF.Exp)
                            xs_h = V(xsT3.ap[:, c, h * 64:(h + 1) * 64], xsT.buf)
                            s.ts("dve", XDTP[hh][r][:, hh * 64:(hh + 1) * 64], xs_h, sT3[:, c, 24 + hd:25 + hd], None, ALU.mult)
                            s.ts("dve", XDEC[hh][r], xs_h, sT3[:, c, 48 + hd:49 + hd], None, ALU.mult)
                            yield
                            s.tt("dve", MT[hh][r], ARG[hh][r], V(GT4.ap[:, c, g, :], GT.buf), ALU.mult)
                            s.tt("dve", CD[hh][r], CTf[g][:, cs], EA[hh][r], ALU.mult)
                            if d == 0:
                                s.cp("act", XSP[hh][r][:, hh * 64:(hh + 1) * 64], xs_h)
                            yield

                    def ssd_seq(ci, j=j, d=d, g=g, order=order):
                        nonlocal sidx
                        c = order[ci]
                        r = ci % 2
                        cs = slice(c * 128, (c + 1) * 128)
                        if ci > 0 and is_boundary(c, d):
                            s.ts("dve", HSp, HSp, keepcol[:, 0:1], None, ALU.mult)
                            for hh in range(2):
                                s.cp("act", HBP[hh][:, hh * 64:(hh + 1) * 64], HSp[:, hh * 64:(hh + 1) * 64])
                        py = ps(2 + r, 0, 128)
                        for hh in range(2):
                            h = 2 * j + hh
                            s.mm(py, XDTP[hh][r], MT[hh][r], start=(hh == 0), stop=False)
                            last = (hh == 1 and d == 1)
                            s.mm(py, HBP[hh], CD[hh][r], start=False, stop=last)
                            if d == 0:
                                s.mm(py, XSP[hh][r], DI3[:, h, :], start=False, stop=(hh == 1))
                        pst = ps(4 + r, 0, 128)
                        endc = 127 if d == 0 else 0
                        for hh in range(2):
                            s.mm(pst[:, hh * 64:(hh + 1) * 64], V(BT4.ap[:, c, g, :], BT.buf), XDEC[hh][r])
                        yield
                        for hh in range(2):
                            hsl = slice(hh * 64, (hh + 1) * 64)
                            s.stt(HSp[:, hsl], HSp[:, hsl], EA[hh][r][:, endc:endc + 1], pst[:, hsl], ALU.mult, ALU.add)
                            s.cp("act", HBP[hh][:, hsl], HSp[:, hsl])
                        if d == 0:
                            s.cp("act", YP[:, cs], py)
                        else:
                            s.tt("dve", YP[:, cs], YP[:, cs], py, ALU.add)
                        yield
                        if is_seq_end(c, d):
                            seq = c // 2
                            po = ps(6, 0, 128)
                            s.tr(po, HSp, ident)
                            so = SO[sidx % 2]
                            sidx += 1
                            s.cp("act", so, po)
                            s.st(V(nss_d[seq, l, d, 2 * j:2 * j + 2].rearrange("h p n -> (h p) n"), OUTB), so,
                                 q="sp", is_output=True)
                        yield

                    run_pipeline(NCH, ssd_pre, ssd_seq, 2, 1)
                s.tt("dve", YP, YP, ZF, ALU.mult)
                s.cp("pool", YG[j], YP)
                s.act(SQ, YP, AF.Square)
                for tb in range(4):
                    pq = ps(4 + tb % 2, 0, 512, sub=0)
                    s.mm(pq, onesb, SQ[:, tb * 512:(tb + 1) * 512])
                    if j == 0:
                        s.cp("act", SS[:, tb * 512:(tb + 1) * 512], pq)
                    else:
                        s.tt("dve", SS[:, tb * 512:(tb + 1) * 512], SS[:, tb * 512:(tb + 1) * 512], pq, ALU.add)
            s.act(SS, SS, AF.Ln, scale=1.0 / D_B, bias=EPS)
            s.act(SS, SS, AF.Exp, scale=-0.5)
            for j in range(6):
                s.stt(SQ, YG[j], PT(l, "ssmnorm", j), SS, ALU.mult, ALU.mult)
                s.st(V(yT_d[768 + j * 128:768 + (j + 1) * 128, :], YB[6 + j]), SQ, q="sp", is_output=debug)

        def phase_mlstm(l):
            newphase()
            biasM = [cmask("biasLE"), cmask("biasGE")]
            R8 = slice(0, 8)
            QF = [ar.bf16(NT, "QF%d" % h) for h in range(4)]
            KF = [ar.bf16(NT, "KF%d" % h) for h in range(4)]
            KT = ar.bf16(16 * 512, "KT")
            KT3 = KT.r("p (c f) -> p c f", c=16)
            VT = ar.bf16(16 * 4 * 130, "VT")
            VT4 = VT.r("p (c h n) -> p c h n", c=16, h=4)
            BM = ar.f32(NT, "BM")
            sT = ar.f32(16 * 24, "sT")
            sT3 = sT.r("p (c n) -> p c n", c=16)
            SC = ar.f32(16 * 4, "SC")
            SC3 = SC.r("p (c n) -> p c n", c=16)
            SCB = ar.f32(8 * 64, "SCB")
            SCB3 = SCB.r("p (h n) -> p h n", h=8)
            MN = ar.f32(16, "MN")
            mlrow = ar.f32(512, "mlrow")
            s.ld(mlrow, V(mlnorm_d[l].partition_broadcast(128), Buf()))
            M1 = ar.mark()
            CPs = [ar.f32(2178, "cp%d" % i) for i in range(2)]
            XL = ar.f32(2176, "xl")
            XR = ar.f32(2176, "xr")
            ACC = ar.f32(NT, "acc")
            mL = ar.f32(NT, "mL")
            mR = ar.f32(NT, "mR")
            TF = ar.f32(NT, "tf")
            TB = ar.bf16(NT, "tb")
            s.ld(mL, V(tokmask_d[2].partition_broadcast(128), Buf()))
            s.ld(mR, V(tokmask_d[3].partition_broadcast(128), Buf()))
            for i in range(2):
                s.memset("pool", CPs[i], 0.0)
            s.memset("pool", XL, 0.0)
            s.memset("pool", XR, 0.0)
            s.memset("pool", VT, 1.0)
            for i in range(8):
                out = QF[i] if i < 4 else KF[i - 4]
                conv_silu(l, 42 + i, "mconv", i, "mconvb", CPs[i % 2], XL, XR, ACC, mL, mR, out,
                          scale=(None if i < 4 else 128.0 ** -0.5))
                if i >= 4:
                    h = i - 4
                    for half in range(2):
                        pb = psb(6 + half)
                        for k in range(8):
                            c = half * 8 + k
                            s.tr(pb[:, k * 128:(k + 1) * 128], out[:, c * 128:(c + 1) * 128], identb)
                        dst = V(KT3.ap[:, half * 8:half * 8 + 8, h * 128:(h + 1) * 128], KT.buf)
                        s.cp("act" if half else "dve", dst, pb.r("p (k n) -> p k n", k=8))
            for h in range(4):
                s.ld(TF, V(uT_d[(50 + h) * 128:(51 + h) * 128, :], UB[50 + h]))
                s.cp("act", TB, TF)
                for half in range(2):
                    pb = psb(6 + half)
                    for k in range(8):
                        c = half * 8 + k
                        s.tr(pb[:, k * 128:(k + 1) * 128], TB[:, c * 128:(c + 1) * 128], identb)
                    dst = V(VT4.ap[:, half * 8:half * 8 + 8, h, 0:128], VT.buf)
                    s.cp("act" if half else "dve", dst, pb.r("p (k n) -> p k n", k=8))
            s.barrier()
            ar.reset(M1)
            keepc8 = ar.f32(NT, "keepc8")
            s.ld(keepc8[R8, :], V(cst_d[0:8, CC["keepc"]:CC["keepc"] + NT], Buf()))
            LI = ar.f32(NT, "LI")
            LF = ar.f32(NT, "LF")
            PF = ar.f32(NT, "PF")
            SUF = ar.f32(NT, "SUF")
            Bb = ar.f32(NT, "Bb")
            WL = ar.f32(NT, "WL")
            Q1 = ar.f32(NT, "Q1")
            nfb = ar.f32(8, "nfb")
            BLt = ar.f32(16, "BL")
            WMX = ar.f32(16, "WMX")
            LMX = ar.f32(16, "LMX")
            MIN_ = ar.f32(16, "MIN")
            MTP = ar.f32(16, "MTP")
            MF = ar.f32(16, "MF")
            MBk = ar.f32(16, "MB")
            MIF = ar.f32(16, "MIF")
            MIB = ar.f32(16, "MIB")
            TMP = ar.f32(16, "TMP")
            m0 = ar.f32(8, "m0")
            s.ld(LI[R8, :], V(uT_d[62 * 128:62 * 128 + 8, :], UB[62]))
            s.ld(LF[R8, :], V(uT_d[63 * 128:63 * 128 + 8, :], UB[63]))
            s.ld(m0[R8, 0:1], V(smm_d[l].rearrange("(a b) -> a b", b=1), Buf()))
            s.ts("dve", LI[R8, :], LI[R8, :], PT(l, "ib")[R8, :], None, ALU.add)
            s.ts("dve", nfb[R8, 0:1], PT(l, "fb")[R8, :], -1.0, None, ALU.mult)
            s.act(LF[R8, :], LF[R8, :], AF.Exp, scale=-1.0, bias=nfb[R8, 0:1])
            s.act(LF[R8, :], LF[R8, :], AF.Ln, bias=1.0)
            s.ts("dve", LF[R8, :], LF[R8, :], -1.0, None, ALU.mult)
            s.scan(PF[R8, :], keepc8[R8, :], LF[R8, :], 0.0, ALU.mult, ALU.add)
            PF3 = PF.r("p (c t) -> p c t", c=16)
            TOT = PF3[R8, :, 127:128]
            TOTb = TOT.bc([8, 16, 128])
            SUF3 = SUF.r("p (c t) -> p c t", c=16)
            s.tt("dve", SUF3[R8], TOTb, PF3[R8], ALU.subtract)
            s.tt("dve", SUF[R8, :], SUF[R8, :], LF[R8, :], ALU.add)
            selF = cst[R8, CC["selF8"]:CC["selF8"] + 1]
            selB = cst[R8, CC["selB8"]:CC["selB8"] + 1]
            s.ts("dve", Bb[R8, :], PF[R8, :], selF, None, ALU.mult)
            s.stt(Bb[R8, :], SUF[R8, :], selB, Bb[R8, :], ALU.mult, ALU.add)
            s.cp("dve", BLt[R8, :].r("p (c o) -> p c o", o=1), TOT)
            Bb3 = Bb.r("p (c t) -> p c t", c=16)
            WL3 = WL.r("p (c t) -> p c t", c=16)
            s.tt("dve", WL3[R8], TOTb, Bb3[R8], ALU.subtract)
            s.tt("dve", WL[R8, :], WL[R8, :], LI[R8, :], ALU.add)
            s.reduce(WMX[R8, :], WL3[R8], ALU.max)
            s.reduce(LMX[R8, :], LI.r("p (c t) -> p c t", c=16)[R8], ALU.max)
            s.tt("dve", Q1[R8, :], LI[R8, :], Bb[R8, :], ALU.subtract)
            EG = LF
            WK = SUF
            for d, (MM, MI) in enumerate(((MF, MIF), (MBk, MIB))):
                order = chunk_order(d)
                prev = m0[R8, 0:1]
                for ci, c in enumerate(order):
                    mi = MI[R8, c:c + 1]
                    if ci > 0 and is_boundary(c, d):
                        s.tt("dve", mi, prev, keepcol[R8, 0:1], ALU.mult)
                    else:
                        s.cp("dve", mi, prev)
                    s.stt(MM[R8, c:c + 1], mi, BLt[R8, c:c + 1], WMX[R8, c:c + 1], ALU.add, ALU.max)
                    prev = MM[R8, c:c + 1]
            s.ts("dve", MN[R8, :], MF[R8, :], selF, None, ALU.mult)
            s.stt(MN[R8, :], MBk[R8, :], selB, MN[R8, :], ALU.mult, ALU.add)
            s.ts("dve", MIN_[R8, :], MIF[R8, :], selF, None, ALU.mult)
            s.stt(MIN_[R8, :], MIB[R8, :], selB, MIN_[R8, :], ALU.mult, ALU.add)
            s.tt("dve", MTP[R8, :], MIN_[R8, :], LMX[R8, :], ALU.max)
            col = lambda t: t[R8, :].r("p (c o) -> p c o", o=1)
            s.act(SC3[R8, :, 0:1], col(MTP), AF.Exp, scale=-1.0)
            s.tt("dve", TMP[R8, :], BLt[R8, :], MIN_[R8, :], ALU.add)
            s.tt("dve", TMP[R8, :], TMP[R8, :], MN[R8, :], ALU.subtract)
            s.act(SC3[R8, :, 1:2], col(TMP), AF.Exp)
            BM3 = BM.r("p (c t) -> p c t", c=16)
            s.tt("dve", BM3[R8], Bb3[R8], col(MTP).bc([8, 16, 128]), ALU.subtract)
            EG3 = EG.r("p (c t) -> p c t", c=16)
            s.tt("dve", EG3[R8], BM3[R8], col(MIN_).bc([8, 16, 128]), ALU.add)
            s.act(EG[R8, :], EG[R8, :], AF.Exp)
            WK3 = WK.r("p (c t) -> p c t", c=16)
            s.tt("dve", WK3[R8], WL3[R8], col(MN).bc([8, 16, 128]), ALU.subtract)
            s.act(WK[R8, :], WK[R8, :], AF.Exp)
            for c in range(NCH):
                pt_ = ps(4 + c % 2, 0, 24)
                cs = slice(c * 128, (c + 1) * 128)
                s.tr(pt_[:, 0:8], Q1[R8, cs], ident[R8, 0:8])
                s.tr(pt_[:, 8:16], EG[R8, cs], ident[R8, 0:8])
                s.tr(pt_[:, 16:24], WK[R8, cs], ident[R8, 0:8])
                s.cp("act" if c % 2 else "dve", sT3[:, c, :], pt_)
            SCH = [ar.f32(64, "SCH%d" % i) for i in range(2)]
            for hd in range(8):
                s.ts("dve", SCH[hd % 2][R8, :], SC[R8, :], ident[R8, hd:hd + 1], None, ALU.mult)
                pq = ps(hd % 2, 0, 64)
                s.mm(pq, onesf[R8, :], SCH[hd % 2][R8, :])
                s.cp("act", SCB3[:, hd, :], pq)
            MO = ar.f32(8, "MO")
            MN3 = MN.r("p (q two) -> p q two", two=2)
            s.ts("dve", MO[R8, 0:8], MN3[R8, :, 1], selF, None, ALU.mult)
            s.stt(MO[R8, 0:8], MN3[R8, :, 0], selB, MO[R8, 0:8], ALU.mult, ALU.add)
            pmo = ps(2, 0, 8, p0=0, p1=8)
            s.tr(pmo, MO[R8, 0:8], ident[R8, 0:8])
            MO2 = ar.f32(8, "MO2")
            s.cp("act", MO2[R8, 0:8], pmo)
            s.st(V(nmm_d[:, l, :], OUTB), MO2[R8, 0:8], q="sp", is_output=True)
            s.barrier()
            ar.reset(M1)
            BMH = ar.f32(NT, "BMH")
            TF = ar.f32(NT, "tf")
            TB = ar.bf16(NT, "tb")
            OTh = ar.bf16(NT, "OTh")
            ZTh = ar.bf16(NT, "ZTh")
            HAh = ar.f32(NT, "HAh")
            HG = ar.f32(NT, "HG")
            YBh = ar.bf16(NT, "YBh")
            YOh = [ar.bf16(NT, "YOh%d" % i) for i in range(2)]
            CN = ar.f32(132, "CN")
            CNb = ar.bf16(132, "CNb")
            SIN = ar.f32(128, "SIN")
            ARG = [ar.f32(128, "ARG%d" % r) for r in range(2)]
            SCT = [ar.bf16(128, "SCT%d" % r) for r in range(2)]
            KW = [ar.bf16(128, "KW%d" % r) for r in range(2)]
            T1 = [ar.f32(132, "T1%d" % r) for r in range(2)]
            TOTt = [ar.f32(132, "TOT%d" % r) for r in range(2)]
            dn = [ar.f32(8, "dn%d" % r) for r in range(2)]
            SO = [ar.f32(128, "SO%d" % i) for i in range(2)]
            ncol = ar.f32(8, "ncol")
            st16 = ar.f32(32, "st16")
            sidx = 0
            HA3 = HAh.r("p (c n) -> p c n", c=16)
            for h in range(4):
                for kind, base, dstt in (("o", 54, OTh), ("z", 58, ZTh)):
                    s.ld(TF, V(uT_d[(base + h) * 128:(base + h + 1) * 128, :], UB[base + h]))
                    s.act(TB, TF, AF.Sigmoid if kind == "o" else AF.Silu)
                    d3 = dstt.r("p (c n) -> p c n", c=16)
                    for half in range(2):
                        pb = psb(6 + half)
                        for k in range(8):
                            c = half * 8 + k
                            s.tr(pb[:, k * 128:(k + 1) * 128], TB[:, c * 128:(c + 1) * 128], identb)
                        s.cp("act" if half else "dve", d3[:, half * 8:half * 8 + 8, :], pb.r("p (k n) -> p k n", k=8))
                s.memset("pool", HAh, 0.0)
                for d in range(2):
                    hd = d * 4 + h
                    s.ts("dve", BMH[R8, :], BM[R8, :], ident[R8, hd:hd + 1], None, ALU.mult)
                    s.ld(SIN, V(smc_d[l, d, h], Buf()))
                    pi = ps(7, 0, 128)
                    s.tr(pi, SIN, ident)
                    s.cp("act", CN[:, 0:128], pi)
                    s.ld(CN[:, 128:129], V(smn_d[l, d, h].rearrange("(k o) -> k o", o=1), Buf()))
                    s.cp("dve", CNb[:, 0:129], CN[:, 0:129])
                    order = chunk_order(d)

                    def ml_pre(ci, h=h, d=d, hd=hd, order=order):
                        c = order[ci]
                        r = ci % 2
                        cs = slice(c * 128, (c + 1) * 128)
                        pa = ps(r, 0, 256)
                        s.mm(pa[:, 0:128], onesf[R8, :], BMH[R8, cs])
                        s.mm(pa[:, 128:256], KF[h][:, cs], QF[h][:, cs])
                        s.stt(ARG[r], pa[:, 0:128], sT3[:, c, hd:hd + 1], biasM[d], ALU.add, ALU.add)
                        s.ts("pool", KW[r], V(KT3.ap[:, c, h * 128:(h + 1) * 128], KT.buf), sT3[:, c, 16 + hd:17 + hd], None, ALU.mult)
                        yield
                        s.act(ARG[r], ARG[r], AF.Exp)
                        s.tt("dve", SCT[r], ARG[r], pa[:, 128:256], ALU.mult)
                        yield

                    def ml_seq(ci, h=h, d=d, hd=hd, order=order):
                        nonlocal sidx
                        c = order[ci]
                        r = ci % 2
                        cs = slice(c * 128, (c + 1) * 128)
                        if ci > 0 and is_boundary(c, d):
                            s.ts("dve", CN[:, 0:129], CN[:, 0:129], keepcol[:, 0:1], None, ALU.mult)
                            s.cp("act", CNb[:, 0:129], CN[:, 0:129])
                        vt1 = V(VT4.ap[:, c, h, 0:129], VT.buf)
                        pn = ps(2 + r, 0, 129)
                        pin = ps(2 + r, 256, 385)
                        s.mm(pn, SCT[r], vt1)
                        s.mm(pin, QF[h][:, cs], CNb[:, 0:129])
                        pu = ps(4 + r, 0, 129)
                        s.mm(pu, KW[r], vt1)
                        emt = SCB3[:, hd, c * 4 + 0:c * 4 + 1]
                        dec = SCB3[:, hd, c * 4 + 1:c * 4 + 2]
                        yield
                        s.stt(CN[:, 0:129], CN[:, 0:129], dec, pu, ALU.mult, ALU.add)
                        s.act(T1[r][:, 0:129], pin, AF.Copy, scale=sT3[:, c, 8 + hd:9 + hd])
                        s.cp("act", CNb[:, 0:129], CN[:, 0:129])
                        yield
                        s.tt("dve", TOTt[r][:, 0:129], T1[r][:, 0:129], pn, ALU.add)
                        s.stt(dn[r][:, 0:1], TOTt[r][:, 128:129], -1.0, TOTt[r][:, 128:129], ALU.mult, ALU.max)
                        s.ts("dve", dn[r][:, 0:1], dn[r][:, 0:1], emt, None, ALU.max)
                        s.recip(dn[r][:, 0:1], dn[r][:, 0:1])
                        hsl = HA3[:, c, :]
                        s.stt(hsl, TOTt[r][:, 0:128], dn[r][:, 0:1], hsl, ALU.mult, ALU.add)
                        yield
                        if is_seq_end(c, d):
                            seq = c // 2
                            po = ps(6, 0, 128)
                            s.tr(po, CN[:, 0:128], ident)
                            so = SO[sidx % 2]
                            sidx += 1
                            s.cp("act", so, po)
                            s.st(V(nmc_d[seq, l, d, h], OUTB), so, q="sp", is_output=True)
                            s.cp("dve", ncol[:, sidx % 2:sidx % 2 + 1], CN[:, 128:129])
                            s.st(V(nmn_d[seq, l, d, h].rearrange("(k o) -> k o", o=1), OUTB),
                                 ncol[:, sidx % 2:sidx % 2 + 1], q="sp", is_output=True)
                        yield

                    run_pipeline(NCH, ml_pre, ml_seq, 2, 1)
                HG3 = HG.r("p (c n) -> p c n", c=16)
                s.tt("dve", HG, HAh, OTh, ALU.mult)
                s.reduce(st16[:, 0:16], HG3, ALU.add)
                s.ts("dve", st16[:, 0:16], st16[:, 0:16], -1.0 / 128, None, ALU.mult)
                s.tt("dve", HG3, HG3, st16[:, 0:16].r("p (c o) -> p c o", o=1).bc([128, 16, 128]), ALU.add)
                s.tt("pool", HAh, HG, HG, ALU.mult)
                s.reduce(st16[:, 16:32], HA3, ALU.add)
                s.ts("dve", st16[:, 16:32], st16[:, 16:32], 1.0 / 128, EPS, ALU.mult, ALU.add)
                s.act(st16[:, 16:32], st16[:, 16:32], AF.Sqrt)
                s.recip(st16[:, 16:32], st16[:, 16:32])
                s.tt("dve", HG3, HG3, st16[:, 16:32].r("p (c o) -> p c o", o=1).bc([128, 16, 128]), ALU.mult)
                s.tt("pool", HG3, HG3, mlrow[:, h * 128:(h + 1) * 128].r("p (o n) -> p o n", o=1).bc([128, 16, 128]), ALU.mult)
                s.tt("dve", YBh, HG, ZTh, ALU.mult)
                yo = YOh[h % 2]
                for half in range(2):
                    pb = psb(6 + half)
                    for k in range(8):
                        c = half * 8 + k
                        s.tr(pb[:, k * 128:(k + 1) * 128], YBh[:, c * 128:(c + 1) * 128], identb)
                    s.cp("act" if half else "dve", yo[:, half * 1024:(half + 1) * 1024], pb)
                s.st(V(yT_d[1536 + h * 128:1536 + (h + 1) * 128, :], YB[12 + h]), yo, q="sp", is_output=debug)

        def phase_rwkv(l):
            newphase()
            keepc = ar.bf16(NT, "keepc")
            lorab = ar.bf16(2 * D_A, "lorab")
            lorab3 = lorab.r("p (d n) -> p d n", d=2)
            LIN = ar.bf16(NT, "LIN")
            M1 = ar.mark()
            masks = {
                0: dict(strict=cmask("LT"), incl=cmask("LE"), strictT=cmask("GT")),
                1: dict(strict=cmask("GT"), incl=cmask("GE"), strictT=cmask("LT")),
            }

            def shift_tools():
                PAD = [ar.f32(NT + 8, "pad%d" % i) for i in range(2)]
                for i in range(2):
                    s.memset("pool", PAD[i], 0.0)
                T1 = ar.f32(NT, "T1")
                T2 = ar.f32(NT, "T2")
                mL = ar.bf16(NT, "mL")
                mR = ar.bf16(NT, "mR")
                s.ld(mL, V(tokmaskb_d[0].partition_broadcast(128), Buf()))
                s.ld(mR, V(tokmaskb_d[1].partition_broadcast(128), Buf()))
                cnt = [0]

                def shiftmix(tile, mucol, out):
                    P = PAD[cnt[0] % 2]
                    cnt[0] += 1
                    s.ld(P[:, 1:1 + NT], V(uT_d[tile * 128:(tile + 1) * 128, :], UB[tile]))
                    s.tt("dve", T1, P[:, 0:NT], mL, ALU.mult)
                    s.tt("pool", T2, P[:, 2:2 + NT], mR, ALU.mult)
                    s.tt("pool", T1, T1, T2, ALU.add)
                    s.stt(T1, T1, 0.5, P[:, 1:1 + NT], ALU.mult, ALU.subtract)
                    s.stt(out, T1, mucol, P[:, 1:1 + NT], ALU.mult, ALU.add)
                return shiftmix, T1, T2

            shiftmix, T1, T2 = shift_tools()
            s.ld(T2, V(cst_d[:, CC["keepc"]:CC["keepc"] + NT], Buf()))
            s.cp("dve", keepc, T2)
            loraw = ar.f32(2 * D_A, "loraw")
            s.ld(loraw.r("p (d n) -> p d n", d=2), V(lora_d[l], Buf()))
            s.cp("act", lorab, loraw)
            WLAL = ar.f32(NT, "WLAL")
            shiftmix(24, PT(l, "mu", 24), WLAL)
            s.act(LIN[0:64, :], WLAL[0:64, :], AF.Tanh)
            s.act(LIN[64:128, :], WLAL[64:128, :], AF.Copy)

            for j in range(6):
                s.barrier()
                ar.reset(M1)
                R = ar.bf16(NT, "R")
                K = ar.bf16(NT, "K")
                Vv = ar.bf16(NT, "V")
                G = ar.bf16(NT, "G")
                KH = ar.f32(NT, "KH")
                YP = ar.f32(NT, "YP")
                VTP = ar.bf16(16 * 2 * 128, "VTP")
                VTP4 = VTP.r("p (c h n) -> p c h n", c=16, h=2)
                M2 = ar.mark()
                shiftmix, T1, T2 = shift_tools()
                shiftmix(j, PT(l, "mu", j), R)
                shiftmix(6 + j, PT(l, "mu", 6 + j), K)
                shiftmix(12 + j, PT(l, "mu", 12 + j), Vv)
                shiftmix(18 + j, PT(l, "mu", 18 + j), G)
                SQb = ar.bf16(NT, "SQb")
                s.ts("dve", KH, K, PT(l, "kk", j), None, ALU.mult)
                s.act(SQb, KH, AF.Square)
                for tb in range(4):
                    pq = ps(tb % 2, 0, 512)
                    ts_ = slice(tb * 512, (tb + 1) * 512)
                    s.mm(pq, bonesb, SQb[:, ts_])
                    s.act(T1[:, ts_], pq, AF.Ln, bias=1e-12)
                s.act(T1, T1, AF.Exp, scale=-0.5)
                s.tt("dve", KH, KH, T1, ALU.mult)
                s.memset("pool", VTP, 0.0)
                for half in range(2):
                    pb = psb(6 + half)
                    for k in range(8):
                        c = half * 8 + k
                        s.tr(pb[:, k * 128:(k + 1) * 128], Vv[:, c * 128:(c + 1) * 128], identb)
                    pb3 = pb.r("p (k n) -> p k n", k=8)
                    for hh in range(2):
                        dst = V(VTP4.ap[:, half * 8:half * 8 + 8, hh, hh * 64:(hh + 1) * 64], VTP.buf)
                        s.cp("act" if hh else "dve", dst, pb3[:, :, hh * 64:(hh + 1) * 64])
                for d in range(2):
                    s.barrier()
                    ar.reset(M2)
                    mk = masks[d]
                    T1 = ar.f32(NT, "T1")
                    SW = ar.f32(NT, "SW")
                    Aa = ar.bf16(NT, "Aa")
                    CS = ar.f32(NT, "CS")
                    KT_ = ar.bf16(NT, "KT")
                    Bf = ar.bf16(NT, "Bf")
                    E = ar.f32(NT, "E")
                    TOTS = ar.f32(16, "TOTS")
                    KR = ar.bf16(2 * NT, "KR")
                    KR4 = KR.r("p (c a t) -> p c a t", c=16, a=2)
                    BTl = ar.bf16(NT, "BTl")
                    KTl = ar.bf16(NT, "KTl")
                    BH = ar.bf16(NT, "BH")
                    KHt = ar.bf16(NT, "KHt")
                    BHT = ar.bf16(16 * 128, "BHT")
                    BHT3 = BHT.r("p (c n) -> p c n", c=16)
                    KHT = ar.bf16(16 * 128, "KHT")
                    KHT3 = KHT.r("p (c n) -> p c n", c=16)
                    KAT = ar.bf16(16 * 128, "KAT")
                    KAT3 = KAT.r("p (c n) -> p c n", c=16)
                    WLc = ar.f32(16, "WLc")
                    c16 = lambda t: t.r("p (c t) -> p c t", c=16)
                    for tb in range(4):
                        ts_ = slice(tb * 512, (tb + 1) * 512)
                        pw = ps(tb % 2, 0, 512)
                        s.mm(pw, lorab3[0:64, d, j * 128:(j + 1) * 128], LIN[0:64, ts_])
                        s.act(SW[:, ts_], pw, AF.Sigmoid, bias=PT(l, "w0", d * 6 + j))
                        pa = ps(2 + tb % 2, 0, 512)
                        s.mm(pa, lorab3[64:128, d, j * 128:(j + 1) * 128], LIN[64:128, ts_])
                        s.act(Aa[:, ts_], pa, AF.Sigmoid, bias=PT(l, "a0", d * 6 + j))
                    s.scan(CS, keepc, SW, 0.0, ALU.mult, ALU.add)
                    CS3 = c16(CS)
                    s.cp("dve", TOTS.r("p (c o) -> p c o", o=1), CS3[:, :, 127:128])
                    TOTb = TOTS.r("p (c o) -> p c o", o=1).bc([128, 16, 128])
                    s.act(WLc, TOTS, AF.Exp, scale=-DECAY_SCALE)
                    E3 = c16(E)
                    if d == 1:
                        s.tt("dve", E3, TOTb, CS3, ALU.subtract)
                        s.tt("dve", CS, E, SW, ALU.add)
                    s.ts("dve", T1, Aa, -1.0, PT(l, "ka", j), ALU.add, ALU.mult)
                    s.stt(KT_, T1, 1.0, K, ALU.add, ALU.mult)
                    s.tt("pool", Bf, KH, Aa, ALU.mult)
                    s.tt("dve", E3, TOTb, CS3, ALU.subtract)
                    s.act(E, E, AF.Exp, scale=-DECAY_SCALE)
                    s.tt("dve", BH, Bf, E, ALU.mult)
                    s.tt("pool", KHt, KT_, E, ALU.mult)
                    s.act(E, CS, AF.Exp, scale=DECAY_SCALE)
                    s.tt("dve", BTl, Bf, E, ALU.mult)
                    s.tt("pool", KTl, KT_, E, ALU.mult)
                    s.act(E, CS, AF.Exp, scale=-DECAY_SCALE)
                    s.tt("dve", V(KR4.ap[:, :, 1, :], KR.buf), c16(R), E3, ALU.mult)
                    s.tt("dve", T1, CS, SW, ALU.subtract)
                    s.act(T1, T1, AF.Exp, scale=-DECAY_SCALE)
                    s.tt("dve", V(KR4.ap[:, :, 0, :], KR.buf), c16(KH), c16(T1), ALU.mult)
                    for (src3, dst3, dstb) in ((c16(BH), BHT3, BHT), (c16(KHt), KHT3, KHT),
                                               (V(KR4.ap[:, :, 0, :], KR.buf), KAT3, KAT)):
                        for half in range(2):
                            pb = psb(6 + half)
                            for k in range(8):
                                c = half * 8 + k
                                s.tr(pb[:, k * 128:(k + 1) * 128], src3[:, c, :], identb)
                            s.cp("act" if half else "dve", V(dst3.ap[:, half * 8:half * 8 + 8, :], dstb.buf),
                                 pb.r("p (k n) -> p k n", k=8))
                    s.stt(T1, KT_, PT(l, "rk", j), R, ALU.mult, ALU.mult)
                    s.cp("act", KHt, T1)
                    for tb in range(4):
                        ts_ = slice(tb * 512, (tb + 1) * 512)
                        pq = ps(tb % 2, 0, 512)
                        s.mm(pq, bonesb, KHt[:, ts_])
                        if d == 0:
                            s.tt("dve", YP[:, ts_], pq, Vv[:, ts_], ALU.mult)
                        else:
                            s.tt("dve", T1[:, ts_], pq, Vv[:, ts_], ALU.mult)
                    if d == 1:
                        s.tt("pool", YP, YP, T1, ALU.add)
                    ST = ar.f32(64, "ST")
                    SBP = [ar.bf16(128, "SBP%d" % hh) for hh in range(2)]
                    ZIN = ar.f32(2 * 128, "ZIN")
                    ZIN3 = ZIN.r("p (h n) -> p h n", h=2)
                    MASKA = ar.bf16(256, "MASKA")
                    MASKB = ar.bf16(256, "MASKB")
                    MASKT = ar.bf16(128, "MASKT")
                    s.ts("dve", MASKA[:, 0:128], mk["strict"], -1.0, None, ALU.mult)
                    s.cp("dve", MASKA[:, 128:256], mk["incl"])
                    s.cp("dve", MASKB[:, 0:128], mk["strict"])
                    s.cp("dve", MASKB[:, 128:256], mk["incl"])
                    s.ts("dve", MASKT, mk["strictT"], -1.0, None, ALU.mult)
                    NB = 2
                    SA = [[ar.bf16(256, "SA%d%d" % (hh, r)) for r in range(NB)] for hh in range(2)]
                    SBq = [[ar.bf16(256, "SB%d%d" % (hh, r)) for r in range(NB)] for hh in range(2)]
                    MM_ = [[[ar.bf16(256, "MM%d%d%d" % (hh, r, q)) for q in range(2)] for r in range(NB)] for hh in range(2)]
                    XX = [[[ar.bf16(128, "XX%d%d%d" % (hh, r, q)) for q in range(2)] for r in range(NB)] for hh in range(2)]
                    XT7 = [[ar.bf16(128, "XT7%d%d" % (hh, r)) for r in range(NB)] for hh in range(2)]
                    NAP = [[ar.bf16(128, "NAP%d%d" % (hh, r)) for r in range(NB)] for hh in range(2)]
                    SO = [ar.f32(128, "SO%d" % i) for i in range(2)]
                    for hh in range(2):
                        s.memset("pool", SBP[hh], 0.0)
                        for r in range(NB):
                            s.memset("pool", NAP[hh][r], 0.0)
                    s.memset("pool", ZIN, 0.0)
                    for hh in range(2):
                        s.ld(ZIN3[0:64, hh, hh * 64:(hh + 1) * 64], V(srw_d[l, d, 2 * j + hh], Buf()))
                    s.barrier()
                    for hh in range(2):
                        hs = slice(hh * 64, (hh + 1) * 64)
                        pi = ps(2 + hh, 0, 64)
                        s.tr(pi, ZIN3[0:64, hh, :], ident[0:64, 0:64])
                        s.cp("act", ST[hs, :], pi[hs, :])
                        s.cp("dve", SBP[hh][hs, hs], ST[hs, :])
                    order = chunk_order(d)
                    sidx = [0]

                    def rw_pre(ci, order=order, d=d, j=j):
                        c = order[ci]
                        r = ci % NB
                        cs = slice(c * 128, (c + 1) * 128)
                        krc = V(KR4.ap[:, c, :, :], KR.buf).r("p a t -> p (a t)")
                        kac = V(KR4.ap[:, c, 0, :], KR.buf)
                        rho = V(KR4.ap[:, c, 1, :], KR.buf)
                        for hh in range(2):
                            hs = slice(hh * 64, (hh + 1) * 64)
                            p1 = ps(0 + hh, 0, 256, sub=0)
                            p2 = ps(0 + hh, 256, 512, sub=1)
                            p3 = ps(2 + hh, 0, 128)
                            s.mm(p1, BTl[hs, cs], krc[hs, :])
                            s.mm(p2, KTl[hs, cs], krc[hs, :])
                            s.mm(p3, kac[hs, :], BTl[hs, cs])
                            s.tt("dve", SA[hh][r], p1, MASKA, ALU.mult)
                            s.tt("dve", SBq[hh][r], p2, MASKB, ALU.mult)
                            s.tt("dve", MM_[hh][r][0][:, 0:128], p3, MASKT, ALU.mult)
                            s.cp("pool", MM_[hh][r][0][:, 128:256], SA[hh][r][:, 0:128])
                            pq = ps(2 + hh, 128, 192)
                            s.mm(pq, SBq[hh][r][:, 0:128], V(VTP4.ap[:, c, hh, hh * 64:(hh + 1) * 64], VTP.buf))
                            ksl = slice(hh * 64, (hh + 1) * 64)
                            qsl = slice((1 - hh) * 64, (2 - hh) * 64)
                            s.cp("pool", XX[hh][r][0][:, ksl], V(KAT3.ap[:, c, ksl], KAT.buf))
                            s.cp("act", XX[hh][r][0][:, qsl], pq)
                            yield
                        for lev in range(7):
                            q = lev % 2
                            for hh in range(2):
                                Mc = MM_[hh][r][q]
                                Xc = XX[hh][r][q]
                                px = ps(4 + hh, 0, 128)
                                s.mm(px, Mc[:, 128:256], Xc, start=True, stop=False)
                                s.mm(px, identb, Xc, start=False, stop=True)
                                s.cp("act", XX[hh][r][1 - q], px)
                                if lev < 6:
                                    pm = ps(6 + hh, 0, 256)
                                    s.mm(pm[:, 0:128], Mc[:, 128:256], Mc[:, 0:128])
                                    s.mm(pm[:, 128:256], Mc[:, 0:128], Mc[:, 128:256])
                                    s.cp("dve", MM_[hh][r][1 - q], pm)
                            yield
                        for hh in range(2):
                            pb = psb(2 + hh, 640, 768)
                            s.tr(pb, XX[hh][r][1], identb)
                            s.cp("dve", XT7[hh][r], pb)
                        yield

                    def rw_seq(ci, order=order, d=d, j=j):
                        c = order[ci]
                        r = ci % NB
                        cs = slice(c * 128, (c + 1) * 128)
                        krc = V(KR4.ap[:, c, :, :], KR.buf).r("p a t -> p (a t)")
                        kac = V(KR4.ap[:, c, 0, :], KR.buf)
                        rho = V(KR4.ap[:, c, 1, :], KR.buf)
                        if ci > 0 and is_boundary(c, d):
                            s.ts("dve", ST, ST, keepcol[:, 0:1], None, ALU.mult)
                            for hh in range(2):
                                hs = slice(hh * 64, (hh + 1) * 64)
                                s.cp("act", SBP[hh][hs, hs], ST[hs, :])
                        for hh in range(2):
                            hs = slice(hh * 64, (hh + 1) * 64)
                            usl = slice((1 - hh) * 64, (2 - hh) * 64)
                            pA = ps(2 + hh, 192, 256)
                            s.mm(pA, XT7[hh][r][hs, :], SBP[hh][hs, hs])
                            s.stt(NAP[hh][r][:, hs], pA, -1.0, XX[hh][r][1][:, usl], ALU.mult, ALU.subtract)
                        yield
                        py = ps(3, 384, 512)
                        for hh in range(2):
                            hs = slice(hh * 64, (hh + 1) * 64)
                            s.mm(py, SBP[hh][hs, :], rho[hs, :], start=(hh == 0), stop=False)
                            s.mm(py, NAP[hh][r], SA[hh][r][:, 128:256], start=False, stop=False)
                            s.mm(py, V(VTP4.ap[:, c, hh, :], VTP.buf), SBq[hh][r][:, 128:256], start=False, stop=(hh == 1))
                        s.tt("dve", YP[:, cs], YP[:, cs], py, ALU.add)
                        for hh in range(2):
                            hs = slice(hh * 64, (hh + 1) * 64)
                            pS = ps(2 + hh, 256, 320)
                            s.mm(pS, V(BHT3.ap[:, c, :], BHT.buf), NAP[hh][r][:, hs], start=True, stop=False)
                            s.mm(pS, V(KHT3.ap[:, c, :], KHT.buf), V(VTP4.ap[:, c, hh, hs], VTP.buf), start=False, stop=True)
                            s.stt(ST[hs, :], ST[hs, :], WLc[hs, c:c + 1], pS[hs, :], ALU.mult, ALU.add)
                            s.cp("act", SBP[hh][hs, hs], ST[hs, :])
                        yield
                        if is_seq_end(c, d):
                            seq = c // 2
                            po = ps(2, 384, 512, p0=0, p1=64)
                            s.tr(po, ST, ident)
                            so = SO[sidx[0] % 2]
                            sidx[0] += 1
                            s.cp("act", so[0:64, :], po)
                            s.st(V(nrw_d[seq, l, d, 2 * j:2 * j + 2].rearrange("h v k -> v h k"), OUTB),
                                 so[0:64, :].r("p (h k) -> p h k", h=2), q="sp", is_output=True)
                        yield

                    run_pipeline(NCH, rw_pre, rw_seq, NB, 1)
                s.barrier()
                ar.reset(M2)
                T1 = ar.f32(NT, "T1")
                RB = ar.bf16(NT, "RB")
                D_ = ar.f32(NT, "D_")
                s.cp("act", RB, YP)
                for tb in range(4):
                    ts_ = slice(tb * 512, (tb + 1) * 512)
                    pq = ps(2 + tb % 2, 0, 512)
                    s.mm(pq, bonesb, RB[:, ts_])
                    s.stt(D_[:, ts_], pq, -1.0 / 64, YP[:, ts_], ALU.mult, ALU.add)
                s.act(RB, D_, AF.Square)
                for tb in range(4):
                    ts_ = slice(tb * 512, (tb + 1) * 512)
                    pq = ps(tb % 2, 0, 512)
                    s.mm(pq, bonesb, RB[:, ts_])
                    s.act(T1[:, ts_], pq, AF.Ln, scale=1.0 / 64, bias=GN_EPS)
                s.act(T1, T1, AF.Exp, scale=-0.5)
                s.tt("dve", D_, D_, T1, ALU.mult)
                s.ts("dve", D_, D_, PT(l, "gnw", j), PT(l, "gnb", j), ALU.mult, ALU.add)
                s.act(T1, G, AF.Silu)
                s.tt("dve", RB, D_, T1, ALU.mult)
                s.st(V(yT_d[j * 128:(j + 1) * 128, :], YB[j]), RB, q="sp", is_output=debug)

        if "mod" in phases:
            phase_mod()
        for l in range(L):
            if "norm" in phases:
                phase_normproj(l)
            if "rwkv" in phases:
                phase_rwkv(l)
            if "ssd" in phases:
                phase_ssd(l)
            if "mlstm" in phases:
                phase_mlstm(l)
            if "out" in phases:
                phase_out(l)
        if "final" in phases:
            phase_final()
        s.emit()
        build.stats = dict(cnt=dict(s.cnt), dcnt=dict(s.dcnt), peak=ar.peak)
    return nc


def _ptab(inp, l, prompt):
    t = np.zeros((128, NPT), np.float32)
    mu = inp["rwkv_mu"][l]
    for i in range(25):
        c0 = TILES[i][0]
        t[:, PCOL["mu"] + i] = mu[c0:c0 + 128]
    for j in range(6):
        sl = slice(j * 128, (j + 1) * 128)
        t[:, PCOL["kk"] + j] = inp["rwkv_kk"][l][sl]
        t[:, PCOL["ka"] + j] = inp["rwkv_ka"][l][sl]
        t[:, PCOL["rk"] + j] = inp["rwkv_rk"][l].reshape(-1)[sl]
        t[:, PCOL["gnw"] + j] = inp["rwkv_gnw"][l][sl]
        t[:, PCOL["gnb"] + j] = inp["rwkv_gnb"][l][sl]
        for d in range(2):
            t[:, PCOL["w0"] + d * 6 + j] = inp["rwkv_w0"][l, d][sl]
            t[:, PCOL["a0"] + d * 6 + j] = inp["rwkv_a0"][l, d][sl]
        t[:, PCOL["ssmD"] + j] = np.repeat(inp["ssm_d"][l][2 * j:2 * j + 2], 64)
        t[:, PCOL["ssmnorm"] + j] = inp["ssm_norm"][l][sl]
    for h in range(12):
        t[:, PCOL["ssmDrep"] + h] = inp["ssm_d"][l][h]
    sc = inp["ssm_conv"][l]
    mc = inp["ml_conv"][l]
    for i in range(10):
        sl = slice(i * 128, (i + 1) * 128)
        for kh in range(3):
            for kw in range(3):
                if prompt and kh != 1:
                    continue
                t[:, PCOL["sconv"] + i * 9 + kh * 3 + kw] = sc[kh, kw, sl]
        t[:, PCOL["sconvb"] + i] = inp["ssm_conv_b"][l][sl]
    for i in range(8):
        sl = slice(i * 128, (i + 1) * 128)
        for kh in range(3):
            for kw in range(3):
                if prompt and kh != 1:
                    continue
                t[:, PCOL["mconv"] + i * 9 + kh * 3 + kw] = mc[kh, kw, sl]
        t[:, PCOL["mconvb"] + i] = inp["ml_conv_b"][l][sl]
    t[0:24, PCOL["dtb"]] = inp["ssm_dt_bias"][l].reshape(-1)
    t[0:24, PCOL["alog"]] = inp["ssm_a_log"][l].reshape(-1)
    t[0:8, PCOL["ib"]] = inp["ml_ib"][l].reshape(-1)
    t[0:8, PCOL["fb"]] = inp["ml_fb"][l].reshape(-1)
    return t


def make_in_maps(inp, depth=DEPTH):
    f = lambda a: np.ascontiguousarray(np.asarray(a, dtype=np.float32))
    inp = {k: f(v) for k, v in inp.items()}
    L = depth
    cstv = make_consts()
    shared = {
        "cst": cstv,
        "w_mod": inp["w_mod"][:L],
        "b_modT": f(inp["b_mod"][:L].reshape(L, 48, 128).transpose(0, 2, 1)),
        "norm_gT": f(inp["norm_g"][:L].reshape(L, 16, 128).transpose(0, 2, 1)),
        "w_in": inp["w_in"][:L],
        "w_out": inp["w_out"][:L],
        "lora": f(np.concatenate([inp["rwkv_wup"][:L].transpose(0, 2, 1, 3), inp["rwkv_aup"][:L].transpose(0, 2, 1, 3)], axis=1)),
        "ml_norm": inp["ml_norm"][:L],
        "final_g": inp["final_g"],
    }
    pt_s = f(np.stack([_ptab(inp, l, False) for l in range(L)]))
    pt_p = f(np.stack([_ptab(inp, l, True) for l in range(L)]))
    t = np.arange(NT)
    tm_s = np.stack([(t != 0), (t != NT - 1), (t % 64 != 0), (t % 64 != 63)]).astype(np.float32)
    tm_p = np.stack([(t % 256 != 0), (t % 256 != 255), (t % 256 != 0), (t % 256 != 255)]).astype(np.float32)
    maps = []
    for core in range(8):
        m = dict(shared)
        if core < 4:
            b = core
            m["x0"] = inp["x_sample"][b]
            m["cond"] = f(inp["c"][b].reshape(16, 128).T)
            m["ptab"] = pt_s
            m["tokmask"] = tm_s
            m["tokmaskb"] = tm_s.astype(ml_dtypes.bfloat16)
            m["keep"] = np.ones((128, 1), np.float32)
            m["srw"] = inp["state_rwkv"][b][:L]
            m["sss"] = inp["state_ssm"][b][:L]
            m["smc"] = inp["state_mlstm_c"][b][:L]
            m["smn"] = inp["state_mlstm_n"][b][:L]
            m["smm"] = f(inp["state_mlstm_m"][b][:L].reshape(L, 8))
        else:
            g = core - 4
            m["x0"] = f(inp["x_prompt"][g * 8:(g + 1) * 8].reshape(NT, D_MODEL))
            m["cond"] = f(inp["c_ctx"].reshape(16, 128).T)
            m["ptab"] = pt_p
            m["tokmask"] = tm_p
            m["tokmaskb"] = tm_p.astype(ml_dtypes.bfloat16)
            m["keep"] = np.zeros((128, 1), np.float32)
            m["srw"] = np.zeros((L, 2, 12, 64, 64), np.float32)
            m["sss"] = np.zeros((L, 2, 12, 64, 128), np.float32)
            m["smc"] = np.zeros((L, 2, 4, 128, 128), np.float32)
            m["smn"] = np.zeros((L, 2, 4, 128), np.float32)
            m["smm"] = np.zeros((L, 8), np.float32)
        maps.append(m)
    return maps


_NC_CACHE = {}


def kernel(**inputs):
    if "nc" not in _NC_CACHE:
        _NC_CACHE["nc"] = build(DEPTH)
    nc = _NC_CACHE["nc"]
    maps = make_in_maps(inputs, DEPTH)
    res = run_bass_kernel_spmd(nc, maps, core_ids=list(range(8)))
    r = res.results
    y_sample = np.stack([r[b]["y"] for b in range(4)]).astype(np.float32)
    y_prompt = np.concatenate([r[4 + g]["y"].reshape(8, 256, D_MODEL) for g in range(4)]).astype(np.float32)
    cat = lambda k: np.concatenate([r[4 + g][k] for g in range(4)]).astype(np.float32)
    new_rwkv = cat("nrw")
    new_ssm = cat("nss")
    new_mc = cat("nmc")
    new_mn = cat("nmn")
    new_mm = cat("nmm").reshape(32, DEPTH, 2, 4)
    return (y_prompt, y_sample, new_rwkv, new_ssm, new_mc, new_mn, new_mm)
```

```python
import contextlib
import numpy as np
import ml_dtypes
import concourse.bass as bass
import concourse.mybir as mybir
from concourse.alu_op_type import AluOpType as ALU
from concourse.bass_utils import run_bass_kernel_spmd

F32 = mybir.dt.float32
BF16 = mybir.dt.bfloat16
AF = mybir.ActivationFunctionType
AX = mybir.AxisListType

D_MODEL = 2048
DEPTH = 4
NT = 2048
NCH = 16
D_A = 768
D_B = 768
D_C = 512
D_IN = 7848
EPS = 1e-6
GN_EPS = 64e-5
DECAY_SCALE = 0.6065306597
NEG = -30000.0

TILES = []
for i in range(6): TILES.append((0 + 128 * i, 128))
for i in range(6): TILES.append((768 + 128 * i, 128))
for i in range(6): TILES.append((1536 + 128 * i, 128))
for i in range(6): TILES.append((2304 + 128 * i, 128))
TILES.append((3072, 128))
for i in range(6): TILES.append((3200 + 128 * i, 128))
for i in range(6): TILES.append((3968 + 128 * i, 128))
for i in range(2): TILES.append((4736 + 128 * i, 128))
for i in range(2): TILES.append((4992 + 128 * i, 128))
TILES.append((5248, 24))
for i in range(4): TILES.append((5272 + 128 * i, 128))
for i in range(4): TILES.append((5784 + 128 * i, 128))
for i in range(4): TILES.append((6296 + 128 * i, 128))
for i in range(4): TILES.append((6808 + 128 * i, 128))
for i in range(4): TILES.append((7320 + 128 * i, 128))
TILES.append((7832, 8))
TILES.append((7840, 8))
NTILES = len(TILES)

PCOL = {}
_n = 0
def _pc(name, n=1):
    global _n
    PCOL[name] = _n
    _n += n
_pc("mu", 25); _pc("kk", 6); _pc("ka", 6); _pc("rk", 6); _pc("gnw", 6); _pc("gnb", 6)
_pc("w0", 12); _pc("a0", 12)
_pc("sconv", 90); _pc("sconvb", 10); _pc("mconv", 72); _pc("mconvb", 8)
_pc("dtb"); _pc("alog"); _pc("ssmD", 6); _pc("ssmDrep", 12); _pc("ssmnorm", 6)
_pc("ib"); _pc("fb")
NPT = _n

CC = {}
_m = 0
def _cc(name, n):
    global _m
    CC[name] = _m
    _m += n
_cc("ident", 128); _cc("blockones", 128); _cc("ones", 128)
_cc("LT", 128); _cc("LE", 128); _cc("GT", 128); _cc("GE", 128)
_cc("biasLE", 128); _cc("biasGE", 128)
_cc("selF24", 1); _cc("selB24", 1); _cc("selF8", 1); _cc("selB8", 1)
_cc("keepc", 2048)
NCC = _m


def make_consts():
    c = np.zeros((128, NCC), np.float32)
    p = np.arange(128)[:, None]
    f = np.arange(128)[None, :]
    c[:, CC["ident"]:CC["ident"] + 128] = (p == f)
    c[:, CC["blockones"]:CC["blockones"] + 128] = (p // 64 == f // 64)
    c[:, CC["ones"]:CC["ones"] + 128] = 1.0
    c[:, CC["LT"]:CC["LT"] + 128] = (p < f)
    c[:, CC["LE"]:CC["LE"] + 128] = (p <= f)
    c[:, CC["GT"]:CC["GT"] + 128] = (p > f)
    c[:, CC["GE"]:CC["GE"] + 128] = (p >= f)
    c[:, CC["biasLE"]:CC["biasLE"] + 128] = np.where(p <= f, 0.0, NEG)
    c[:, CC["biasGE"]:CC["biasGE"] + 128] = np.where(p >= f, 0.0, NEG)
    pp = np.arange(128)
    c[:, CC["selF24"]] = (pp < 12)
    c[:, CC["selB24"]] = (pp >= 12) & (pp < 24)
    c[:, CC["selF8"]] = (pp < 4)
    c[:, CC["selB8"]] = (pp >= 4) & (pp < 8)
    kc = np.ones(2048, np.float32)
    kc[::128] = 0.0
    c[:, CC["keepc"]:CC["keepc"] + 2048] = kc[None, :]
    return c


WIN = 20000
NDMASEM = 12
NOSYNC_ENG = ("pe",)


class Buf:
    __slots__ = ("name", "writers", "readers", "excl")

    def __init__(self, name="", excl=False):
        self.name = name
        self.writers = []
        self.readers = []
        self.excl = excl


class V:
    __slots__ = ("ap", "buf")

    def __init__(self, ap, buf):
        self.ap = ap
        self.buf = buf

    def __getitem__(self, k):
        return V(self.ap[k], self.buf)

    def r(self, pat, **kw):
        return V(self.ap.rearrange(pat, **kw), self.buf)

    def bc(self, shape):
        return V(self.ap.to_broadcast(shape), self.buf)

    def cast(self, dt):
        return V(self.ap.bitcast(dt), self.buf)


def _ap(x):
    return x.ap if isinstance(x, V) else x


def _bufs(*xs):
    return [x.buf for x in xs if isinstance(x, V)]


class Sch:
    ENG = ("pe", "act", "dve", "pool", "sp")

    def __init__(self, nc):
        self.nc = nc
        self.ops = {e: [] for e in self.ENG}
        self.cnt = {e: 0 for e in self.ENG}
        self.dcnt = {e: 0 for e in self.ENG}
        self.seen = {e: {} for e in self.ENG}
        self.semkeys = set()
        self.out_events = []
        self.last_dma = {}
        self.delay_fn = {}
        self.delay_buf = {}
        self.delay_on = False

    def _filter(self, eng, evs):
        need = {}
        for (k, v) in evs:
            if need.get(k, 0) < v:
                need[k] = v
        res = []
        for k, v in need.items():
            if k[0] == "c" and k[1] == eng and eng in NOSYNC_ENG:
                continue
            if self.seen[eng].get(k, 0) >= v:
                continue
            self.seen[eng][k] = v
            res.append((k, v))
        return res

    @staticmethod
    def _deps(reads, writes, eng=None):
        evs = []
        for b in reads:
            evs += b.writers
            if b.excl:
                evs += [e for e in b.readers if e[0][1] != eng]
        for b in writes:
            evs += b.writers
            evs += b.readers
        return evs

    @staticmethod
    def _compact(evs):
        need = {}
        for (k, v) in evs:
            if need.get(k, 0) < v:
                need[k] = v
        return list(need.items())

    def _commit(self, ev, reads, writes):
        for b in reads:
            b.readers.append(ev)
            if len(b.readers) > 48:
                b.readers = self._compact(b.readers)
        for b in writes:
            b.writers = [ev]
            b.readers = []

    def op(self, eng, fn, reads=(), writes=(), _nodelay=False):
        if (not _nodelay) and self.delay_on and eng in self.delay_fn:
            need = False
            for b in reads:
                if b.excl:
                    for (k, v) in b.writers:
                        if k[1] == "pe" and self.seen[eng].get(k, 0) < v:
                            need = True
            if need:
                self.op(eng, self.delay_fn[eng], reads=[b for b in reads if b.excl],
                        writes=[self.delay_buf[eng]], _nodelay=True)
        evs = self._filter(eng, self._deps(reads, writes, eng))
        i = self.cnt[eng]
        self.cnt[eng] += 1
        key = ("c", eng, i // WIN)
        val = (i % WIN) + 1
        self.semkeys.add(key)
        self.ops[eng].append((evs, fn, (key, 1)))
        ev = (key, val)
        self._commit(ev, reads, writes)
        return ev

    def dma(self, q, fn, reads=(), writes=(), is_output=False):
        i = self.dcnt[q]
        self.dcnt[q] += 1
        key = ("d", q, i % NDMASEM)
        val = 16 * (i // NDMASEM + 1)
        self.semkeys.add(key)
        evs = self._deps(reads, writes)
        if val > 16:
            evs = evs + [(key, val - 16)]
        evs = self._filter(q, evs)
        self.ops[q].append((evs, fn, (key, 16)))
        ev = (key, val)
        self.last_dma[key] = val
        self._commit(ev, reads, writes)
        if is_output:
            self.out_events.append(ev)
        return ev

    def barrier(self):
        evs = []
        for e in self.ENG:
            if self.cnt[e] > 0:
                i = self.cnt[e] - 1
                evs.append((("c", e, i // WIN), (i % WIN) + 1))
        for k, v in self.last_dma.items():
            evs.append((k, v))
        for e in self.ENG:
            f = self._filter(e, list(evs))
            if f:
                self.ops[e].append((f, None, None))

    def act(self, out, in_, func, bias=None, scale=None, accum=None):
        kw = {}
        if bias is not None:
            kw["bias"] = _ap(bias)
        if scale is not None:
            kw["scale"] = _ap(scale)
        if accum is not None:
            kw["accum_out"] = _ap(accum)
        o, i = _ap(out), _ap(in_)
        return self.op("act", lambda e: e.activation(out=o, in_=i, func=func, **kw),
                       reads=_bufs(in_, bias, scale), writes=_bufs(out, accum))

    def ts(self, eng, out, in0, s1, s2=None, op0=ALU.mult, op1=None):
        o, i, a, b = _ap(out), _ap(in0), _ap(s1), _ap(s2)
        if op1 is None:
            fn = lambda e: e.tensor_scalar(out=o, in0=i, scalar1=a, scalar2=None, op0=op0)
        else:
            fn = lambda e: e.tensor_scalar(out=o, in0=i, scalar1=a, scalar2=b, op0=op0, op1=op1)
        return self.op(eng, fn, reads=_bufs(in0, s1, s2), writes=_bufs(out))

    def tt(self, eng, out, in0, in1, op):
        o, a, b = _ap(out), _ap(in0), _ap(in1)
        return self.op(eng, lambda e: e.tensor_tensor(out=o, in0=a, in1=b, op=op),
                       reads=_bufs(in0, in1), writes=_bufs(out))

    def stt(self, out, in0, scalar, in1, op0, op1):
        o, a, sc, b = _ap(out), _ap(in0), _ap(scalar), _ap(in1)
        return self.op("dve", lambda e: e.scalar_tensor_tensor(out=o, in0=a, scalar=sc, in1=b, op0=op0, op1=op1),
                       reads=_bufs(in0, scalar, in1), writes=_bufs(out))

    def cp(self, eng, out, in_):
        o, i = _ap(out), _ap(in_)
        if eng == "act":
            return self.op("act", lambda e: e.activation(out=o, in_=i, func=AF.Copy),
                           reads=_bufs(in_), writes=_bufs(out))
        return self.op(eng, lambda e: e.tensor_copy(out=o, in_=i), reads=_bufs(in_), writes=_bufs(out))

    def memset(self, eng, out, val):
        o = _ap(out)
        return self.op(eng, lambda e: e.memset(o, val), writes=_bufs(out))

    def recip(self, out, in_):
        o, i = _ap(out), _ap(in_)
        return self.op("dve", lambda e: e.reciprocal(out=o, in_=i), reads=_bufs(in_), writes=_bufs(out))

    def scan(self, out, d0, d1, initial, op0, op1):
        o, a, b, ini = _ap(out), _ap(d0), _ap(d1), _ap(initial)
        return self.op("dve", lambda e: e.tensor_tensor_scan(out=o, data0=a, data1=b, initial=ini, op0=op0, op1=op1),
                       reads=_bufs(d0, d1, initial), writes=_bufs(out))

    def reduce(self, out, in_, op, axis=None):
        o, i = _ap(out), _ap(in_)
        ax = AX.X if axis is None else axis
        return self.op("dve", lambda e: e.tensor_reduce(out=o, in_=i, axis=ax, op=op),
                       reads=_bufs(in_), writes=_bufs(out))

    def mm(self, out, lhsT, rhs, start=True, stop=True):
        o, a, b = _ap(out), _ap(lhsT), _ap(rhs)
        return self.op("pe", lambda e: e.matmul(o, lhsT=a, rhs=b, start=start, stop=stop),
                       reads=_bufs(lhsT, rhs), writes=_bufs(out))

    def tr(self, out, in_, ident):
        o, a, b = _ap(out), _ap(in_), _ap(ident)
        return self.op("pe", lambda e: e.transpose(out=o, in_=a, identity=b),
                       reads=_bufs(in_, ident), writes=_bufs(out))

    def ld(self, out, src, q="sp"):
        o, i = _ap(out), _ap(src)
        return self.dma(q, lambda e: e.dma_start(out=o, in_=i), reads=_bufs(src), writes=_bufs(out))

    def st(self, dst, in_, q="pool", is_output=False):
        o, i = _ap(dst), _ap(in_)
        return self.dma(q, lambda e: e.dma_start(out=o, in_=i), reads=_bufs(in_), writes=_bufs(dst),
                        is_output=is_output)

    def emit(self):
        nc = self.nc
        keys = sorted(self.semkeys)
        with contextlib.ExitStack() as st:
            sems = {}
            for k in keys:
                sems[k] = st.enter_context(nc.semaphore("s_%s_%s_%d" % k))
            fin = list(self.out_events)
            for e in self.ENG:
                if self.cnt[e] > 0 and e != "sp":
                    i = self.cnt[e] - 1
                    fin.append((("c", e, i // WIN), (i % WIN) + 1))
            for k, v in self.last_dma.items():
                fin.append((k, v))
            fin = self._filter("sp", fin)
            block = st.enter_context(nc.Block())

            def run(engobj, name, extra=None):
                for (evs, fn, inc) in self.ops[name]:
                    for (k, v) in evs:
                        engobj.wait_ge(sems[k], v)
                    if fn is not None:
                        fn(engobj).then_inc(sems[inc[0]], inc[1])
                if extra:
                    for (k, v) in extra:
                        engobj.wait_ge(sems[k], v)

            @block.tensor
            def _(e):
                run(e, "pe")

            @block.scalar
            def _(e):
                run(e, "act")

            @block.vector
            def _(e):
                run(e, "dve")

            @block.gpsimd
            def _(e):
                run(e, "pool")

            @block.sync
            def _(e):
                run(e, "sp", fin)


class Arena:
    def __init__(self, ap, ncols):
        self.ap = ap
        self.n = ncols
        self.off = 0
        self.peak = 0

    def f32(self, ncols, name=""):
        nal = (ncols + 7) // 8 * 8
        assert self.off + nal <= self.n, ("arena overflow", name, self.off, nal, self.n)
        v = V(self.ap[:, self.off:self.off + ncols], Buf(name))
        self.off += nal
        self.peak = max(self.peak, self.off)
        return v

    def bf16(self, ncols, name=""):
        assert ncols % 2 == 0
        v = self.f32(ncols // 2, name)
        return V(v.ap.bitcast(BF16), v.buf)

    def mark(self):
        return self.off

    def reset(self, m):
        self.off = m


ARENA_COLS = 44000
RW_PREACT = 2


def build(depth=DEPTH, debug=False, phases=("mod", "norm", "proj", "rwkv", "ssd", "mlstm", "out", "final")):
    nc = bass.Bass("TRN2", target_bir_lowering=False)

    def din(name, shape, dt=F32):
        return nc.dram_tensor(name, list(shape), dt, kind="ExternalInput").ap()

    def dout(name, shape, dt=F32):
        return nc.dram_tensor(name, list(shape), dt, kind="ExternalOutput").ap()

    L = depth
    x0 = din("x0", [NT, D_MODEL])
    cond_d = din("cond", [128, 16])
    cst_d = din("cst", [128, NCC])
    ptab_d = din("ptab", [L, 128, NPT])
    tokmask_d = din("tokmask", [4, NT])
    tokmaskb_d = din("tokmaskb", [4, NT], BF16)
    keep_d = din("keep", [128, 1])
    wmod_d = din("w_mod", [L, D_MODEL, 3 * D_MODEL])
    bmodT_d = din("b_modT", [L, 128, 48])
    ngT_d = din("norm_gT", [L, 128, 16])
    win_d = din("w_in", [L, D_MODEL, D_IN])
    wout_d = din("w_out", [L, D_MODEL, D_MODEL])
    lora_d = din("lora", [L, 128, 2, D_A])
    mlnorm_d = din("ml_norm", [L, D_C])
    fg_d = din("final_g", [D_MODEL])
    srw_d = din("srw", [L, 2, 12, 64, 64])
    sss_d = din("sss", [L, 2, 12, 64, 128])
    smc_d = din("smc", [L, 2, 4, 128, 128])
    smn_d = din("smn", [L, 2, 4, 128])
    smm_d = din("smm", [L, 8])

    y_d = dout("y", [NT, D_MODEL])
    nrw_d = dout("nrw", [8, L, 2, 12, 64, 64])
    nss_d = dout("nss", [8, L, 2, 12, 64, 128])
    nmc_d = dout("nmc", [8, L, 2, 4, 128, 128])
    nmn_d = dout("nmn", [8, L, 2, 4, 128])
    nmm_d = dout("nmm", [8, L, 8])
    if debug:
        uT_d = dout("uT", [NTILES * 128, NT])
        yT_d = dout("yTs", [D_MODEL, NT], BF16)
    else:
        uT_d = nc.dram_tensor("uT", [NTILES * 128, NT], F32).ap()
        yT_d = nc.dram_tensor("yTs", [D_MODEL, NT], BF16).ap()
    gscr_d = nc.dram_tensor("gscr", [L, 16, 128], F32).ap()

    with contextlib.ExitStack() as stack:
        A_t = stack.enter_context(nc.sbuf_tensor("arena", [128, ARENA_COLS], F32))
        PS_t = stack.enter_context(nc.psum_tensor("psum", [128, 4096], F32))
        s = Sch(nc)
        ar = Arena(A_t[:, :], ARENA_COLS)

        psbufs = {}

        def ps(bank, c0=0, c1=512, sub=0, p0=0, p1=128):
            key = (bank, 0)
            if key not in psbufs:
                psbufs[key] = Buf("ps%d_%d" % key, excl=True)
            return V(PS_t[p0:p1, bank * 512 + c0: bank * 512 + c1], psbufs[key])

        def psb(bank, c0=0, c1=1024, sub=0, p0=0, p1=128):
            key = (bank, 0)
            if key not in psbufs:
                psbufs[key] = Buf("ps%d_%d" % key, excl=True)
            return V(PS_t[p0:p1, bank * 512: bank * 512 + 512].bitcast(BF16)[:, c0:c1], psbufs[key])

        XB = [Buf("x%d" % c) for c in range(NCH)]
        UB = [Buf("u%d" % j) for j in range(NTILES)]
        YB = [Buf("y%d" % j) for j in range(16)]
        GB = [Buf("g%d" % l) for l in range(L)]
        OUTB = Buf("stateout")

        cst = ar.f32(NCC - 2048, "cst")
        s.ld(cst, V(cst_d[:, 0:NCC - 2048], Buf()))
        ident = cst[:, CC["ident"]:CC["ident"] + 128]
        onesf = cst[:, CC["ones"]:CC["ones"] + 128]
        identb = ar.bf16(128, "identb")
        s.cp("dve", identb, ident)
        bonesb = ar.bf16(128, "bonesb")
        s.cp("dve", bonesb, cst[:, CC["blockones"]:CC["blockones"] + 128])
        onesb = ar.bf16(128, "onesb")
        s.cp("dve", onesb, onesf)
        keepcol = ar.f32(8, "keep")
        s.ld(keepcol[:, 0:1], V(keep_d, Buf()))
        ptab = ar.f32(L * NPT, "ptab")
        s.ld(ptab.r("p (l n) -> p l n", l=L), V(ptab_d.rearrange("l p n -> p l n"), Buf()))
        scrA = ar.f32(8, "scrA")
        scrD = ar.f32(8, "scrD")
        _ia, _sa, _sd = ident.ap[:, 0:1], scrA.ap[:, 0:1], scrD.ap[:, 0:8]
        s.delay_fn["act"] = lambda e: e.activation(out=_sa, in_=_ia, func=AF.Copy)
        s.delay_fn["dve"] = lambda e: e.memset(_sd, 0.0)
        s.delay_buf["act"] = scrA.buf
        s.delay_buf["dve"] = scrD.buf
        modT = [ar.f32(48, "modT%d" % l) for l in range(L)]
        Gsc = [ar.f32(16, "Gsc%d" % l) for l in range(L)]

        def PT(l, name, i=0):
            c = l * NPT + PCOL[name] + i
            return ptab[:, c:c + 1]

        def cmask(name):
            return cst[:, CC[name]:CC[name] + 128]

        MARK0 = ar.mark()

        def newphase():
            s.barrier()
            ar.reset(MARK0)
            s.delay_on = False

        def phase_mod():
            newphase()
            condT = ar.f32(16, "condT")
            s.ld(condT, V(cond_d, Buf()))
            scond = ar.f32(16, "scond")
            s.act(scond, condT, AF.Silu)
            bm = ar.f32(48 * L, "bm")
            s.ld(bm.r("p (l n) -> p l n", l=L), V(bmodT_d.rearrange("l p n -> p l n"), Buf()))
            ng = ar.f32(16 * L, "ng")
            s.ld(ng.r("p (l n) -> p l n", l=L), V(ngT_d.rearrange("l p n -> p l n"), Buf()))
            WM = [ar.f32(16 * 384, "wm%d" % i) for i in range(2)]
            gsb = ar.f32(128, "gsb")
            it = 0
            for l in range(L):
                pm = ps(l % 2, 0, 48)
                for nb in range(16):
                    wm = WM[it % 2]
                    it += 1
                    wm3 = wm.r("p (k n) -> p k n", k=16)
                    s.ld(wm3, V(wmod_d[l][:, nb * 384:(nb + 1) * 384].rearrange("(k p) n -> p k n", p=128), Buf()))
                    for j in range(3):
                        n = nb * 3 + j
                        for kc in range(16):
                            s.mm(pm[:, n:n + 1], wm3[:, kc, j * 128:(j + 1) * 128], scond[:, kc:kc + 1],
                                 start=(kc == 0), stop=(kc == 15))
                s.tt("dve", modT[l], pm, bm[:, l * 48:(l + 1) * 48], ALU.add)
                s.stt(Gsc[l], modT[l][:, 16:32], 1.0, ng[:, l * 16:(l + 1) * 16], ALU.add, ALU.mult)
                pg = ps(2 + l % 2, 0, 128, p0=0, p1=16)
                s.tr(pg, modT[l][:, 32:48], ident)
                s.cp("act", gsb[0:16, :], pg)
                s.st(V(gscr_d[l], GB[l]), gsb[0:16, :], q="sp")

        def phase_normproj(l):
            newphase()
            xsrc = x0 if l == 0 else y_d
            hT = ar.bf16(16 * NT, "hT")
            hT3 = hT.r("p (k t) -> p k t", k=16)
            hTB = [Buf("hT%d" % i) for i in range(4)]
            XT = [ar.f32(NT, "xt%d" % i) for i in range(2)]
            XN = [ar.bf16(NT, "xn%d" % i) for i in range(2)]
            junk = ar.bf16(NT, "junk")
            ss = ar.f32(16, "ss")
            rs = ar.f32(16, "rs")
            shift = modT[l][:, 0:16]
            for c in range(NCH):
                xt = XT[c % 2]
                xn = XN[c % 2]
                src = V(xsrc[c * 128:(c + 1) * 128, :], XB[c] if l > 0 else Buf())
                s.ld(xt, src)
                s.act(junk, xt, AF.Square, accum=ss[:, c:c + 1])
                s.ts("dve", rs[:, c:c + 1], ss[:, c:c + 1], 1.0 / D_MODEL, EPS, ALU.mult, ALU.add)
                s.act(rs[:, c:c + 1], rs[:, c:c + 1], AF.Sqrt)
                s.recip(rs[:, c:c + 1], rs[:, c:c + 1])
                s.ts("dve", xn, xt, rs[:, c:c + 1], None, ALU.mult)
                for half in range(2):
                    pb = psb(6 + half)
                    for k in range(8):
                        kk = half * 8 + k
                        s.tr(pb[:, k * 128:(k + 1) * 128], xn[:, kk * 128:(kk + 1) * 128], identb)
                    for k in range(8):
                        kk = half * 8 + k
                        dst = V(hT3.ap[:, kk, c * 128:(c + 1) * 128], hTB[c // 4])
                        s.act(dst, pb[:, k * 128:(k + 1) * 128], AF.Identity,
                              bias=shift[:, kk:kk + 1], scale=Gsc[l][:, kk:kk + 1])
            WF = [ar.f32(16 * 128, "wf%d" % i) for i in range(2)]
            WBt = [ar.bf16(16 * 128, "wb%d" % i) for i in range(2)]
            UST = [ar.f32(NT, "ust%d" % i) for i in range(2)]

            def load_w(j):
                c0, w = TILES[j]
                wf3 = WF[j % 2].r("p (k n) -> p k n", k=16)
                wb3 = WBt[j % 2].r("p (k n) -> p k n", k=16)
                s.ld(wf3[:, :, 0:w], V(win_d[l][:, c0:c0 + w].rearrange("(k p) n -> p k n", p=128), Buf()))
                s.cp("pool" if j % 2 else "act", wb3[:, :, 0:w], wf3[:, :, 0:w])

            load_w(0)
            for j in range(NTILES):
                c0, w = TILES[j]
                if j + 1 < NTILES:
                    load_w(j + 1)
                wb3 = WBt[j % 2].r("p (k n) -> p k n", k=16)
                ust = UST[j % 2]
                for tb in range(4):
                    pp = ps((j % 2) * 4 + tb, p0=0, p1=w)
                    for kc in range(16):
                        rhs = V(hT3.ap[:, kc, tb * 512:(tb + 1) * 512], hTB[tb])
                        s.mm(pp, wb3[:, kc, 0:w], rhs, start=(kc == 0), stop=(kc == 15))
                    s.cp("act" if tb % 2 else "dve", ust[0:w, tb * 512:(tb + 1) * 512], pp)
                s.st(V(uT_d[j * 128:j * 128 + w, :], UB[j]), ust[0:w, :], q="pool", is_output=debug)

        def phase_out(l):
            newphase()
            xsrc = x0 if l == 0 else y_d
            yT = ar.bf16(16 * NT, "yT")
            yT3 = yT.r("p (k t) -> p k t", k=16)
            yTB = [Buf("yT%d" % i) for i in range(16)]
            for k in range(16):
                s.ld(V(yT3.ap[:, k, :], yTB[k]), V(yT_d[k * 128:(k + 1) * 128, :], YB[k]))
            grow = ar.f32(D_MODEL, "grow")
            s.ld(grow, V(gscr_d[l].rearrange("a b -> (a b)").partition_broadcast(128), GB[l]))
            WF = [ar.f32(16 * 256, "wf%d" % i) for i in range(2)]
            WBt = [ar.bf16(16 * 256, "wb%d" % i) for i in range(2)]
            XT = [ar.f32(256, "xt%d" % i) for i in range(3)]
            TM = [ar.f32(256, "tm%d" % i) for i in range(3)]

            def load_w(jb):
                wf3 = WF[jb % 2].r("p (k n) -> p k n", k=16)
                wb3 = WBt[jb % 2].r("p (k n) -> p k n", k=16)
                s.ld(wf3, V(wout_d[l][:, jb * 256:(jb + 1) * 256].rearrange("(k p) n -> p k n", p=128), Buf()))
                s.cp("pool" if jb % 2 else "act", wb3, wf3)

            load_w(0)
            it = 0
            for jb in range(8):
                if jb + 1 < 8:
                    load_w(jb + 1)
                wb3 = WBt[jb % 2].r("p (k n) -> p k n", k=16)
                for c in range(NCH):
                    xt = XT[it % 3]
                    tm = TM[it % 3]
                    pp = ps(it % 4, 0, 256)
                    it += 1
                    s.ld(xt, V(xsrc[c * 128:(c + 1) * 128, jb * 256:(jb + 1) * 256], XB[c] if l > 0 else Buf()))
                    for kc in range(16):
                        s.mm(pp, V(yT3.ap[:, kc, c * 128:(c + 1) * 128], yTB[kc]), wb3[:, kc, :],
                             start=(kc == 0), stop=(kc == 15))
                    s.tt("dve", tm, pp, grow[:, jb * 256:(jb + 1) * 256], ALU.mult)
                    s.tt("pool", tm, tm, xt, ALU.add)
                    s.st(V(y_d[c * 128:(c + 1) * 128, jb * 256:(jb + 1) * 256], XB[c]), tm, q="sp")

        def phase_final():
            newphase()
            fgrow = ar.f32(D_MODEL, "fgrow")
            s.ld(fgrow, V(fg_d.partition_broadcast(128), Buf()))
            XT = [ar.f32(NT, "xt%d" % i) for i in range(2)]
            YO = [ar.f32(NT, "yo%d" % i) for i in range(2)]
            junk = ar.bf16(NT, "junk")
            ss = ar.f32(16, "ss")
            rs = ar.f32(16, "rs")
            for c in range(NCH):
                xt = XT[c % 2]
                s.ld(xt, V(y_d[c * 128:(c + 1) * 128, :], XB[c]))
                s.act(junk, xt, AF.Square, accum=ss[:, c:c + 1])
                s.ts("dve", rs[:, c:c + 1], ss[:, c:c + 1], 1.0 / D_MODEL, EPS, ALU.mult, ALU.add)
                s.act(rs[:, c:c + 1], rs[:, c:c + 1], AF.Sqrt)
                s.recip(rs[:, c:c + 1], rs[:, c:c + 1])
                s.stt(YO[c % 2], xt, rs[:, c:c + 1], fgrow, ALU.mult, ALU.mult)
                s.st(V(y_d[c * 128:(c + 1) * 128, :], XB[c]), YO[c % 2], q="sp", is_output=True)

        def chunk_order(d):
            return list(range(NCH)) if d == 0 else list(range(NCH - 1, -1, -1))

        def is_boundary(c, d):
            return (c % 2 == 0) if d == 0 else (c % 2 == 1)

        def is_seq_end(c, d):
            return (c % 2 == 1) if d == 0 else (c % 2 == 0)

        def run_pipeline(n, pre_fn, seq_fn, NB, PREACT):
            active = []
            nxt_pre = 0
            nxt_seq = 0
            seq_done = 0
            pre_done = [False] * n
            while seq_done < n:
                while (nxt_pre < n and nxt_pre < seq_done + NB
                       and sum(1 for a in active if a[0] == "p") < PREACT):
                    active.append(("p", nxt_pre, pre_fn(nxt_pre)))
                    nxt_pre += 1
                if (nxt_seq < n and pre_done[nxt_seq] and nxt_seq == seq_done
                        and not any(a[0] == "s" for a in active)):
                    active.append(("s", nxt_seq, seq_fn(nxt_seq)))
                    nxt_seq += 1
                assert active
                for a in list(active):
                    try:
                        next(a[2])
                    except StopIteration:
                        active.remove(a)
                        if a[0] == "p":
                            pre_done[a[1]] = True
                        else:
                            seq_done += 1

        def conv_silu(l, tile, tapname, ti, biasname, CP, XL, XR, ACC, mL, mR, out, scale=None):
            s.ld(CP[:, 65:65 + NT], V(uT_d[tile * 128:(tile + 1) * 128, :], UB[tile]))
            s.tt("pool", XL[:, 64:64 + NT], CP[:, 64:64 + NT], mL, ALU.mult)
            s.tt("pool", XR[:, 64:64 + NT], CP[:, 66:66 + NT], mR, ALU.mult)
            first = True
            for kh in range(3):
                for kw in range(3):
                    tap = PT(l, tapname, ti * 9 + kh * 3 + kw)
                    off = 64 * (kh - 1)
                    if kw == 0:
                        src = XL[:, 64 + off:64 + off + NT]
                    elif kw == 1:
                        src = CP[:, 65 + off:65 + off + NT]
                    else:
                        src = XR[:, 64 + off:64 + off + NT]
                    if first:
                        s.ts("dve", ACC, src, tap, None, ALU.mult)
                        first = False
                    else:
                        s.stt(ACC, src, tap, ACC, ALU.mult, ALU.add)
            s.act(out, ACC, AF.Silu, bias=PT(l, biasname, ti))
            if scale is not None:
                s.ts("pool", out, out, scale, None, ALU.mult)

        def phase_ssd(l):
            newphase()
            s.delay_on = True
            biasM = [cmask("biasLE"), cmask("biasGE")]
            keepc24 = ar.f32(NT, "keepc24")
            s.ld(keepc24[0:24, :], V(cst_d[0:24, CC["keepc"]:CC["keepc"] + NT], Buf()))
            xsT = ar.bf16(16 * 768, "xsT")
            xsT3 = xsT.r("p (c f) -> p c f", c=16)
            BT = ar.bf16(16 * 256, "BT")
            BT4 = BT.r("p (c g n) -> p c g n", c=16, g=2)
            BTf = [ar.bf16(NT, "BTf%d" % g) for g in range(2)]
            CTf = [ar.bf16(NT, "CTf%d" % g) for g in range(2)]
            GT = ar.bf16(16 * 256, "GT")
            GT4 = GT.r("p (c g n) -> p c g n", c=16, g=2)
            DI = ar.bf16(12 * 128, "DI")
            DI3 = DI.r("p (h n) -> p h n", h=12)
            sT = ar.f32(16 * 72, "sT")
            sT3 = sT.r("p (c n) -> p c n", c=16)
            A2 = ar.f32(NT, "A2")
            M1 = ar.mark()
            CPs = [ar.f32(2178, "cp%d" % i) for i in range(2)]
            XL = ar.f32(2176, "xl")
            XR = ar.f32(2176, "xr")
            ACC = ar.f32(NT, "acc")
            mL = ar.f32(NT, "mL")
            mR = ar.f32(NT, "mR")
            XF = [ar.bf16(NT, "xf%d" % i) for i in range(2)]
            s.ld(mL, V(tokmask_d[2].partition_broadcast(128), Buf()))
            s.ld(mR, V(tokmask_d[3].partition_broadcast(128), Buf()))
            for i in range(2):
                s.memset("pool", CPs[i], 0.0)
            s.memset("pool", XL, 0.0)
            s.memset("pool", XR, 0.0)
            for i in range(10):
                if i < 6:
                    out = XF[i % 2]
                elif i < 8:
                    out = BTf[i - 6]
                else:
                    out = CTf[i - 8]
                conv_silu(l, 31 + i, "sconv", i, "sconvb", CPs[i % 2], XL, XR, ACC, mL, mR, out)
                if i < 8:
                    for half in range(2):
                        pb = psb(6 + half)
                        for k in range(8):
                            c = half * 8 + k
                            s.tr(pb[:, k * 128:(k + 1) * 128], out[:, c * 128:(c + 1) * 128], identb)
                        if i < 6:
                            dst = V(xsT3.ap[:, half * 8:half * 8 + 8, i * 128:(i + 1) * 128], xsT.buf)
                        else:
                            dst = V(BT4.ap[:, half * 8:half * 8 + 8, i - 6, :], BT.buf)
                        s.cp("act" if half else "dve", dst, pb.r("p (k n) -> p k n", k=8))
            for c in range(NCH):
                pg = ps(4 + c % 2, 0, 256)
                for g in range(2):
                    s.mm(pg[:, g * 128:(g + 1) * 128], BTf[g][:, c * 128:(c + 1) * 128], CTf[g][:, c * 128:(c + 1) * 128])
                s.cp("act" if c % 2 else "dve", V(GT4.ap[:, c, :, :], GT.buf), pg.r("p (g n) -> p g n", g=2))
            for h in range(12):
                s.ts("pool", DI3[:, h, :], ident, PT(l, "ssmDrep", h), None, ALU.mult)
            s.barrier()
            ar.reset(M1)
            DT = ar.f32(NT, "DT")
            E1 = ar.f32(NT, "E1")
            DTA = ar.f32(NT, "DTA")
            PF = ar.f32(NT, "PF")
            SUF = ar.f32(NT, "SUF")
            XD = ar.f32(NT, "XD")
            nega = ar.f32(8, "nega")
            R24 = slice(0, 24)
            s.ld(DT[R24, :], V(uT_d[41 * 128:41 * 128 + 24, :], UB[41]))
            s.act(E1[R24, :], DT[R24, :], AF.Exp, bias=PT(l, "dtb")[R24, :])
            s.act(DT[R24, :], E1[R24, :], AF.Ln, bias=1.0)
            s.act(nega[R24, 0:1], PT(l, "alog")[R24, :], AF.Exp)
            s.ts("dve", nega[R24, 0:1], nega[R24, 0:1], -1.0, None, ALU.mult)
            s.ts("dve", DTA[R24, :], DT[R24, :], nega[R24, 0:1], None, ALU.mult)
            s.scan(PF[R24, :], keepc24[R24, :], DTA[R24, :], 0.0, ALU.mult, ALU.add)
            PF3 = PF.r("p (c t) -> p c t", c=16)
            TOTb = PF3[R24, :, 127:128].bc([24, 16, 128])
            SUF3 = SUF.r("p (c t) -> p c t", c=16)
            s.tt("dve", SUF3[R24], TOTb, PF3[R24], ALU.subtract)
            s.tt("dve", SUF[R24, :], SUF[R24, :], DTA[R24, :], ALU.add)
            s.ts("dve", A2[R24, :], PF[R24, :], cst[R24, CC["selF24"]:CC["selF24"] + 1], None, ALU.mult)
            s.stt(A2[R24, :], SUF[R24, :], cst[R24, CC["selB24"]:CC["selB24"] + 1], A2[R24, :], ALU.mult, ALU.add)
            A23 = A2.r("p (c t) -> p c t", c=16)
            E13 = E1.r("p (c t) -> p c t", c=16)
            s.tt("dve", E13[R24], TOTb, A23[R24], ALU.subtract)
            s.act(E1[R24, :], E1[R24, :], AF.Exp)
            s.tt("dve", XD[R24, :], DT[R24, :], E1[R24, :], ALU.mult)
            s.ts("dve", SUF[R24, :], A2[R24, :], -1.0, None, ALU.mult)
            for c in range(NCH):
                pt_ = ps(4 + c % 2, 0, 72)
                cs = slice(c * 128, (c + 1) * 128)
                s.tr(pt_[:, 0:24], SUF[R24, cs], ident[R24, 0:24])
                s.tr(pt_[:, 24:48], DT[R24, cs], ident[R24, 0:24])
                s.tr(pt_[:, 48:72], XD[R24, cs], ident[R24, 0:24])
                s.cp("act" if c % 2 else "dve", sT3[:, c, :], pt_)
            s.barrier()
            ar.reset(M1)
            ZF = ar.f32(NT, "ZF")
            YP = ar.f32(NT, "YP")
            YG = [ar.bf16(NT, "YG%d" % j) for j in range(6)]
            SQ = ar.bf16(NT, "SQ")
            SS = ar.f32(NT, "SS")
            A2H = [ar.f32(NT, "A2H%d" % i) for i in range(2)]
            HSp = ar.f32(128, "HSp")
            HBP = [ar.bf16(128, "HBP%d" % i) for i in range(2)]
            XDTP = [[ar.bf16(128, "XDTP%d%d" % (i, r)) for r in range(2)] for i in range(2)]
            XSP = [[ar.bf16(128, "XSP%d%d" % (i, r)) for r in range(2)] for i in range(2)]
            XDEC = [[ar.bf16(64, "XDEC%d%d" % (i, r)) for r in range(2)] for i in range(2)]
            ARG = [[ar.f32(128, "ARG%d%d" % (i, r)) for r in range(2)] for i in range(2)]
            EA = [[ar.f32(128, "EA%d%d" % (i, r)) for r in range(2)] for i in range(2)]
            MT = [[ar.bf16(128, "MT%d%d" % (i, r)) for r in range(2)] for i in range(2)]
            CD = [[ar.bf16(128, "CD%d%d" % (i, r)) for r in range(2)] for i in range(2)]
            SIN = ar.f32(128, "SIN")
            SO = [ar.f32(128, "SO%d" % i) for i in range(2)]
            for hh in range(2):
                s.memset("pool", HBP[hh], 0.0)
                for r in range(2):
                    s.memset("pool", XDTP[hh][r], 0.0)
                    s.memset("pool", XSP[hh][r], 0.0)
            sidx = 0
            for j in range(6):
                g = j // 3
                s.ld(ZF, V(uT_d[(25 + j) * 128:(26 + j) * 128, :], UB[25 + j]))
                s.act(ZF, ZF, AF.Silu)
                for d in range(2):
                    for hh in range(2):
                        hd = d * 12 + 2 * j + hh
                        s.ts("dve", A2H[hh][R24, :], A2[R24, :], ident[R24, hd:hd + 1], None, ALU.mult)
                    s.ld(SIN, V(sss_d[l, d, 2 * j:2 * j + 2].rearrange("h p n -> (h p) n"), Buf()))
                    pi = ps(7, 0, 128)
                    s.tr(pi, SIN, ident)
                    s.cp("act", HSp, pi)
                    for hh in range(2):
                        s.cp("dve", HBP[hh][:, hh * 64:(hh + 1) * 64], HSp[:, hh * 64:(hh + 1) * 64])
                    order = chunk_order(d)

                    def ssd_pre(ci, j=j, d=d, g=g, order=order):
                        c = order[ci]
                        r = ci % 2
                        cs = slice(c * 128, (c + 1) * 128)
                        pa = ps(r, 0, 256)
                        for hh in range(2):
                            h = 2 * j + hh
                            hd = d * 12 + h
                            pah = pa[:, hh * 128:(hh + 1) * 128]
                            s.mm(pah, onesf[R24, :], A2H[hh][R24, cs])
                        for hh in range(2):
                            h = 2 * j + hh
                            hd = d * 12 + h
                            pah = pa[:, hh * 128:(hh + 1) * 128]
                            s.stt(ARG[hh][r], pah, sT3[:, c, hd:hd + 1], biasM[d], ALU.add, ALU.add)
                            s.act(EA[hh][r], pah, AF.Exp)
                            s.act(ARG[hh][r], ARG[hh][r], AF.Exp)
                            xs_h = V(xsT3.ap[:, c, h * 64:(h + 1) * 64], xsT.buf)
                            s.ts("dve", XDTP[hh][r][:, hh * 64:(hh + 1) * 64], xs_h, sT3[:, c, 24 + hd:25 + hd], None, ALU.mult)
                            s.ts("dve", XDEC[hh][r], xs_h, sT3[:, c, 48 + hd:49 + hd], None, ALU.mult)
                            yield
                            s.tt("dve", MT[hh][r], ARG[hh][r], V(GT4.ap[:, c, g, :], GT.buf), ALU.mult)
                            s.tt("dve", CD[hh][r], CTf[g][:, cs], EA[hh][r], ALU.mult)
                            if d == 0:
                                s.cp("act", XSP[hh][r][:, hh * 64:(hh + 1) * 64], xs_h)
                            yield

                    def ssd_seq(ci, j=j, d=d, g=g, order=order):
                        nonlocal sidx
                        c = order[ci]
                        r = ci % 2
                        cs = slice(c * 128, (c + 1) * 128)
                        if ci > 0 and is_boundary(c, d):
                            s.ts("dve", HSp, HSp, keepcol[:, 0:1], None, ALU.mult)
                            for hh in range(2):
                                s.cp("act", HBP[hh][:, hh * 64:(hh + 1) * 64], HSp[:, hh * 64:(hh + 1) * 64])
                        py = ps(2 + r, 0, 128)
                        for hh in range(2):
                            h = 2 * j + hh
                            s.mm(py, XDTP[hh][r], MT[hh][r], start=(hh == 0), stop=False)
                            last = (hh == 1 and d == 1)
                            s.mm(py, HBP[hh], CD[hh][r], start=False, stop=last)
                            if d == 0:
                                s.mm(py, XSP[hh][r], DI3[:, h, :], start=False, stop=(hh == 1))
                        pst = ps(4 + r, 0, 128)
                        endc = 127 if d == 0 else 0
                        for hh in range(2):
                            s.mm(pst[:, hh * 64:(hh + 1) * 64], V(BT4.ap[:, c, g, :], BT.buf), XDEC[hh][r])
                        yield
                        for hh in range(2):
                            hsl = slice(hh * 64, (hh + 1) * 64)
                            s.stt(HSp[:, hsl], HSp[:, hsl], EA[hh][r][:, endc:endc + 1], pst[:, hsl], ALU.mult, ALU.add)
                            s.cp("act", HBP[hh][:, hsl], HSp[:, hsl])
                        if d == 0:
                            s.cp("act", YP[:, cs], py)
                        else:
                            s.tt("dve", YP[:, cs], YP[:, cs], py, ALU.add)
                        yield
                        if is_seq_end(c, d):
                            seq = c // 2
                            po = ps(6, 0, 128)
                            s.tr(po, HSp, ident)
                            so = SO[sidx % 2]
                            sidx += 1
                            s.cp("act", so, po)
                            s.st(V(nss_d[seq, l, d, 2 * j:2 * j + 2].rearrange("h p n -> (h p) n"), OUTB), so,
                                 q="sp", is_output=True)
                        yield

                    run_pipeline(NCH, ssd_pre, ssd_seq, 2, 1)
                s.tt("dve", YP, YP, ZF, ALU.mult)
                s.cp("pool", YG[j], YP)
                s.act(SQ, YP, AF.Square)
                for tb in range(4):
                    pq = ps(4 + tb % 2, 0, 512, sub=0)
                    s.mm(pq, onesb, SQ[:, tb * 512:(tb + 1) * 512])
                    if j == 0:
                        s.cp("act", SS[:, tb * 512:(tb + 1) * 512], pq)
                    else:
                        s.tt("dve", SS[:, tb * 512:(tb + 1) * 512], SS[:, tb * 512:(tb + 1) * 512], pq, ALU.add)
            s.act(SS, SS, AF.Ln, scale=1.0 / D_B, bias=EPS)
            s.act(SS, SS, AF.Exp, scale=-0.5)
            for j in range(6):
                s.stt(SQ, YG[j], PT(l, "ssmnorm", j), SS, ALU.mult, ALU.mult)
                s.st(V(yT_d[768 + j * 128:768 + (j + 1) * 128, :], YB[6 + j]), SQ, q="sp", is_output=debug)

        def phase_mlstm(l):
            newphase()
            s.delay_on = True
            biasM = [cmask("biasLE"), cmask("biasGE")]
            R8 = slice(0, 8)
            QF = [ar.bf16(NT, "QF%d" % h) for h in range(4)]
            KF = [ar.bf16(NT, "KF%d" % h) for h in range(4)]
            KT = ar.bf16(16 * 512, "KT")
            KT3 = KT.r("p (c f) -> p c f", c=16)
            VT = ar.bf16(16 * 4 * 130, "VT")
            VT4 = VT.r("p (c h n) -> p c h n", c=16, h=4)
            BM = ar.f32(NT, "BM")
            sT = ar.f32(16 * 24, "sT")
            sT3 = sT.r("p (c n) -> p c n", c=16)
            SC = ar.f32(16 * 4, "SC")
            SC3 = SC.r("p (c n) -> p c n", c=16)
            SCB = ar.f32(8 * 64, "SCB")
            SCB3 = SCB.r("p (h n) -> p h n", h=8)
            MN = ar.f32(16, "MN")
            mlrow = ar.f32(512, "mlrow")
            s.ld(mlrow, V(mlnorm_d[l].partition_broadcast(128), Buf()))
            M1 = ar.mark()
            CPs = [ar.f32(2178, "cp%d" % i) for i in range(2)]
            XL = ar.f32(2176, "xl")
            XR = ar.f32(2176, "xr")
            ACC = ar.f32(NT, "acc")
            mL = ar.f32(NT, "mL")
            mR = ar.f32(NT, "mR")
            TF = ar.f32(NT, "tf")
            TB = ar.bf16(NT, "tb")
            s.ld(mL, V(tokmask_d[2].partition_broadcast(128), Buf()))
            s.ld(mR, V(tokmask_d[3].partition_broadcast(128), Buf()))
            for i in range(2):
                s.memset("pool", CPs[i], 0.0)
            s.memset("pool", XL, 0.0)
            s.memset("pool", XR, 0.0)
            s.memset("pool", VT, 1.0)
            for i in range(8):
                out = QF[i] if i < 4 else KF[i - 4]
                conv_silu(l, 42 + i, "mconv", i, "mconvb", CPs[i % 2], XL, XR, ACC, mL, mR, out,
                          scale=(None if i < 4 else 128.0 ** -0.5))
                if i >= 4:
                    h = i - 4
                    for half in range(2):
                        pb = psb(6 + half)
                        for k in range(8):
                            c = half * 8 + k
                            s.tr(pb[:, k * 128:(k + 1) * 128], out[:, c * 128:(c + 1) * 128], identb)
                        dst = V(KT3.ap[:, half * 8:half * 8 + 8, h * 128:(h + 1) * 128], KT.buf)
                        s.cp("act" if half else "dve", dst, pb.r("p (k n) -> p k n", k=8))
            for h in range(4):
                s.ld(TF, V(uT_d[(50 + h) * 128:(51 + h) * 128, :], UB[50 + h]))
                s.cp("act", TB, TF)
                for half in range(2):
                    pb = psb(6 + half)
                    for k in range(8):
                        c = half * 8 + k
                        s.tr(pb[:, k * 128:(k + 1) * 128], TB[:, c * 128:(c + 1) * 128], identb)
                    dst = V(VT4.ap[:, half * 8:half * 8 + 8, h, 0:128], VT.buf)
                    s.cp("act" if half else "dve", dst, pb.r("p (k n) -> p k n", k=8))
            s.barrier()
            ar.reset(M1)
            keepc8 = ar.f32(NT, "keepc8")
            s.ld(keepc8[R8, :], V(cst_d[0:8, CC["keepc"]:CC["keepc"] + NT], Buf()))
            LI = ar.f32(NT, "LI")
            LF = ar.f32(NT, "LF")
            PF = ar.f32(NT, "PF")
            SUF = ar.f32(NT, "SUF")
            Bb = ar.f32(NT, "Bb")
            WL = ar.f32(NT, "WL")
            Q1 = ar.f32(NT, "Q1")
            nfb = ar.f32(8, "nfb")
            BLt = ar.f32(16, "BL")
            WMX = ar.f32(16, "WMX")
            LMX = ar.f32(16, "LMX")
            MIN_ = ar.f32(16, "MIN")
            MTP = ar.f32(16, "MTP")
            MF = ar.f32(16, "MF")
            MBk = ar.f32(16, "MB")
            MIF = ar.f32(16, "MIF")
            MIB = ar.f32(16, "MIB")
            TMP = ar.f32(16, "TMP")
            m0 = ar.f32(8, "m0")
            s.ld(LI[R8, :], V(uT_d[62 * 128:62 * 128 + 8, :], UB[62]))
            s.ld(LF[R8, :], V(uT_d[63 * 128:63 * 128 + 8, :], UB[63]))
            s.ld(m0[R8, 0:1], V(smm_d[l].rearrange("(a b) -> a b", b=1), Buf()))
            s.ts("dve", LI[R8, :], LI[R8, :], PT(l, "ib")[R8, :], None, ALU.add)
            s.ts("dve", nfb[R8, 0:1], PT(l, "fb")[R8, :], -1.0, None, ALU.mult)
            s.act(LF[R8, :], LF[R8, :], AF.Exp, scale=-1.0, bias=nfb[R8, 0:1])
            s.act(LF[R8, :], LF[R8, :], AF.Ln, bias=1.0)
            s.ts("dve", LF[R8, :], LF[R8, :], -1.0, None, ALU.mult)
            s.scan(PF[R8, :], keepc8[R8, :], LF[R8, :], 0.0, ALU.mult, ALU.add)
            PF3 = PF.r("p (c t) -> p c t", c=16)
            TOT = PF3[R8, :, 127:128]
            TOTb = TOT.bc([8, 16, 128])
            SUF3 = SUF.r("p (c t) -> p c t", c=16)
            s.tt("dve", SUF3[R8], TOTb, PF3[R8], ALU.subtract)
            s.tt("dve", SUF[R8, :], SUF[R8, :], LF[R8, :], ALU.add)
            selF = cst[R8, CC["selF8"]:CC["selF8"] + 1]
            selB = cst[R8, CC["selB8"]:CC["selB8"] + 1]
            s.ts("dve", Bb[R8, :], PF[R8, :], selF, None, ALU.mult)
            s.stt(Bb[R8, :], SUF[R8, :], selB, Bb[R8, :], ALU.mult, ALU.add)
            s.cp("dve", BLt[R8, :].r("p (c o) -> p c o", o=1), TOT)
            Bb3 = Bb.r("p (c t) -> p c t", c=16)
            WL3 = WL.r("p (c t) -> p c t", c=16)
            s.tt("dve", WL3[R8], TOTb, Bb3[R8], ALU.subtract)
            s.tt("dve", WL[R8, :], WL[R8, :], LI[R8, :], ALU.add)
            s.reduce(WMX[R8, :], WL3[R8], ALU.max)
            s.reduce(LMX[R8, :], LI.r("p (c t) -> p c t", c=16)[R8], ALU.max)
            s.tt("dve", Q1[R8, :], LI[R8, :], Bb[R8, :], ALU.subtract)
            EG = LF
            WK = SUF
            for d, (MM, MI) in enumerate(((MF, MIF), (MBk, MIB))):
                order = chunk_order(d)
                prev = m0[R8, 0:1]
                for ci, c in enumerate(order):
                    mi = MI[R8, c:c + 1]
                    if ci > 0 and is_boundary(c, d):
                        s.tt("dve", mi, prev, keepcol[R8, 0:1], ALU.mult)
                    else:
                        s.cp("dve", mi, prev)
                    s.stt(MM[R8, c:c + 1], mi, BLt[R8, c:c + 1], WMX[R8, c:c + 1], ALU.add, ALU.max)
                    prev = MM[R8, c:c + 1]
            s.ts("dve", MN[R8, :], MF[R8, :], selF, None, ALU.mult)
            s.stt(MN[R8, :], MBk[R8, :], selB, MN[R8, :], ALU.mult, ALU.add)
            s.ts("dve", MIN_[R8, :], MIF[R8, :], selF, None, ALU.mult)
            s.stt(MIN_[R8, :], MIB[R8, :], selB, MIN_[R8, :], ALU.mult, ALU.add)
            s.tt("dve", MTP[R8, :], MIN_[R8, :], LMX[R8, :], ALU.max)
            col = lambda t: t[R8, :].r("p (c o) -> p c o", o=1)
            s.act(SC3[R8, :, 0:1], col(MTP), AF.Exp, scale=-1.0)
            s.tt("dve", TMP[R8, :], BLt[R8, :], MIN_[R8, :], ALU.add)
            s.tt("dve", TMP[R8, :], TMP[R8, :], MN[R8, :], ALU.subtract)
            s.act(SC3[R8, :, 1:2], col(TMP), AF.Exp)
            BM3 = BM.r("p (c t) -> p c t", c=16)
            s.tt("dve", BM3[R8], Bb3[R8], col(MTP).bc([8, 16, 128]), ALU.subtract)
            EG3 = EG.r("p (c t) -> p c t", c=16)
            s.tt("dve", EG3[R8], BM3[R8], col(MIN_).bc([8, 16, 128]), ALU.add)
            s.act(EG[R8, :], EG[R8, :], AF.Exp)
            WK3 = WK.r("p (c t) -> p c t", c=16)
            s.tt("dve", WK3[R8], WL3[R8], col(MN).bc([8, 16, 128]), ALU.subtract)
            s.act(WK[R8, :], WK[R8, :], AF.Exp)
            for c in range(NCH):
                pt_ = ps(4 + c % 2, 0, 24)
                cs = slice(c * 128, (c + 1) * 128)
                s.tr(pt_[:, 0:8], Q1[R8, cs], ident[R8, 0:8])
                s.tr(pt_[:, 8:16], EG[R8, cs], ident[R8, 0:8])
                s.tr(pt_[:, 16:24], WK[R8, cs], ident[R8, 0:8])
                s.cp("act" if c % 2 else "dve", sT3[:, c, :], pt_)
            SCH = [ar.f32(64, "SCH%d" % i) for i in range(2)]
            for hd in range(8):
                s.ts("dve", SCH[hd % 2][R8, :], SC[R8, :], ident[R8, hd:hd + 1], None, ALU.mult)
                pq = ps(hd % 2, 0, 64)
                s.mm(pq, onesf[R8, :], SCH[hd % 2][R8, :])
                s.cp("act", SCB3[:, hd, :], pq)
            MO = ar.f32(8, "MO")
            MN3 = MN.r("p (q two) -> p q two", two=2)
            s.ts("dve", MO[R8, 0:8], MN3[R8, :, 1], selF, None, ALU.mult)
            s.stt(MO[R8, 0:8], MN3[R8, :, 0], selB, MO[R8, 0:8], ALU.mult, ALU.add)
            pmo = ps(2, 0, 8, p0=0, p1=8)
            s.tr(pmo, MO[R8, 0:8], ident[R8, 0:8])
            MO2 = ar.f32(8, "MO2")
            s.cp("act", MO2[R8, 0:8], pmo)
            s.st(V(nmm_d[:, l, :], OUTB), MO2[R8, 0:8], q="sp", is_output=True)
            s.barrier()
            ar.reset(M1)
            BMH = ar.f32(NT, "BMH")
            TF = ar.f32(NT, "tf")
            TB = ar.bf16(NT, "tb")
            OTh = ar.bf16(NT, "OTh")
            ZTh = ar.bf16(NT, "ZTh")
            HAh = ar.f32(NT, "HAh")
            HG = ar.f32(NT, "HG")
            YBh = ar.bf16(NT, "YBh")
            YOh = [ar.bf16(NT, "YOh%d" % i) for i in range(2)]
            CN = ar.f32(132, "CN")
            CNb = ar.bf16(132, "CNb")
            SIN = ar.f32(128, "SIN")
            ARG = [ar.f32(128, "ARG%d" % r) for r in range(2)]
            SCT = [ar.bf16(128, "SCT%d" % r) for r in range(2)]
            KW = [ar.bf16(128, "KW%d" % r) for r in range(2)]
            T1 = [ar.f32(132, "T1%d" % r) for r in range(2)]
            TOTt = [ar.f32(132, "TOT%d" % r) for r in range(2)]
            dn = [ar.f32(8, "dn%d" % r) for r in range(2)]
            SO = [ar.f32(128, "SO%d" % i) for i in range(2)]
            ncol = ar.f32(8, "ncol")
            st16 = ar.f32(32, "st16")
            sidx = 0
            HA3 = HAh.r("p (c n) -> p c n", c=16)
            for h in range(4):
                for kind, base, dstt in (("o", 54, OTh), ("z", 58, ZTh)):
                    s.ld(TF, V(uT_d[(base + h) * 128:(base + h + 1) * 128, :], UB[base + h]))
                    s.act(TB, TF, AF.Sigmoid if kind == "o" else AF.Silu)
                    d3 = dstt.r("p (c n) -> p c n", c=16)
                    for half in range(2):
                        pb = psb(6 + half)
                        for k in range(8):
                            c = half * 8 + k
                            s.tr(pb[:, k * 128:(k + 1) * 128], TB[:, c * 128:(c + 1) * 128], identb)
                        s.cp("act" if half else "dve", d3[:, half * 8:half * 8 + 8, :], pb.r("p (k n) -> p k n", k=8))
                s.memset("pool", HAh, 0.0)
                for d in range(2):
                    hd = d * 4 + h
                    s.ts("dve", BMH[R8, :], BM[R8, :], ident[R8, hd:hd + 1], None, ALU.mult)
                    s.ld(SIN, V(smc_d[l, d, h], Buf()))
                    pi = ps(7, 0, 128)
                    s.tr(pi, SIN, ident)
                    s.cp("act", CN[:, 0:128], pi)
                    s.ld(CN[:, 128:129], V(smn_d[l, d, h].rearrange("(k o) -> k o", o=1), Buf()))
                    s.cp("dve", CNb[:, 0:129], CN[:, 0:129])
                    order = chunk_order(d)

                    def ml_pre(ci, h=h, d=d, hd=hd, order=order):
                        c = order[ci]
                        r = ci % 2
                        cs = slice(c * 128, (c + 1) * 128)
                        pa = ps(r, 0, 256)
                        s.mm(pa[:, 0:128], onesf[R8, :], BMH[R8, cs])
                        s.mm(pa[:, 128:256], KF[h][:, cs], QF[h][:, cs])
                        s.stt(ARG[r], pa[:, 0:128], sT3[:, c, hd:hd + 1], biasM[d], ALU.add, ALU.add)
                        s.ts("pool", KW[r], V(KT3.ap[:, c, h * 128:(h + 1) * 128], KT.buf), sT3[:, c, 16 + hd:17 + hd], None, ALU.mult)
                        yield
                        s.act(ARG[r], ARG[r], AF.Exp)
                        s.tt("dve", SCT[r], ARG[r], pa[:, 128:256], ALU.mult)
                        yield

                    def ml_seq(ci, h=h, d=d, hd=hd, order=order):
                        nonlocal sidx
                        c = order[ci]
                        r = ci % 2
                        cs = slice(c * 128, (c + 1) * 128)
                        if ci > 0 and is_boundary(c, d):
                            s.ts("dve", CN[:, 0:129], CN[:, 0:129], keepcol[:, 0:1], None, ALU.mult)
                            s.cp("act", CNb[:, 0:129], CN[:, 0:129])
                        vt1 = V(VT4.ap[:, c, h, 0:129], VT.buf)
                        pn = ps(2 + r, 0, 129)
                        pin = ps(2 + r, 256, 385)
                        s.mm(pn, SCT[r], vt1)
                        s.mm(pin, QF[h][:, cs], CNb[:, 0:129])
                        pu = ps(4 + r, 0, 129)
                        s.mm(pu, KW[r], vt1)
                        emt = SCB3[:, hd, c * 4 + 0:c * 4 + 1]
                        dec = SCB3[:, hd, c * 4 + 1:c * 4 + 2]
                        yield
                        s.stt(CN[:, 0:129], CN[:, 0:129], dec, pu, ALU.mult, ALU.add)
                        s.act(T1[r][:, 0:129], pin, AF.Copy, scale=sT3[:, c, 8 + hd:9 + hd])
                        s.cp("act", CNb[:, 0:129], CN[:, 0:129])
                        yield
                        s.tt("dve", TOTt[r][:, 0:129], T1[r][:, 0:129], pn, ALU.add)
                        s.stt(dn[r][:, 0:1], TOTt[r][:, 128:129], -1.0, TOTt[r][:, 128:129], ALU.mult, ALU.max)
                        s.ts("dve", dn[r][:, 0:1], dn[r][:, 0:1], emt, None, ALU.max)
                        s.recip(dn[r][:, 0:1], dn[r][:, 0:1])
                        hsl = HA3[:, c, :]
                        s.stt(hsl, TOTt[r][:, 0:128], dn[r][:, 0:1], hsl, ALU.mult, ALU.add)
                        yield
                        if is_seq_end(c, d):
                            seq = c // 2
                            po = ps(6, 0, 128)
                            s.tr(po, CN[:, 0:128], ident)
                            so = SO[sidx % 2]
                            sidx += 1
                            s.cp("act", so, po)
                            s.st(V(nmc_d[seq, l, d, h], OUTB), so, q="sp", is_output=True)
                            s.cp("dve", ncol[:, sidx % 2:sidx % 2 + 1], CN[:, 128:129])
                            s.st(V(nmn_d[seq, l, d, h].rearrange("(k o) -> k o", o=1), OUTB),
                                 ncol[:, sidx % 2:sidx % 2 + 1], q="sp", is_output=True)
                        yield

                    run_pipeline(NCH, ml_pre, ml_seq, 2, 1)
                HG3 = HG.r("p (c n) -> p c n", c=16)
                s.tt("dve", HG, HAh, OTh, ALU.mult)
                s.reduce(st16[:, 0:16], HG3, ALU.add)
                s.ts("dve", st16[:, 0:16], st16[:, 0:16], -1.0 / 128, None, ALU.mult)
                s.tt("dve", HG3, HG3, st16[:, 0:16].r("p (c o) -> p c o", o=1).bc([128, 16, 128]), ALU.add)
                s.tt("pool", HAh, HG, HG, ALU.mult)
                s.reduce(st16[:, 16:32], HA3, ALU.add)
                s.ts("dve", st16[:, 16:32], st16[:, 16:32], 1.0 / 128, EPS, ALU.mult, ALU.add)
                s.act(st16[:, 16:32], st16[:, 16:32], AF.Sqrt)
                s.recip(st16[:, 16:32], st16[:, 16:32])
                s.tt("dve", HG3, HG3, st16[:, 16:32].r("p (c o) -> p c o", o=1).bc([128, 16, 128]), ALU.mult)
                s.tt("pool", HG3, HG3, mlrow[:, h * 128:(h + 1) * 128].r("p (o n) -> p o n", o=1).bc([128, 16, 128]), ALU.mult)
                s.tt("dve", YBh, HG, ZTh, ALU.mult)
                yo = YOh[h % 2]
                for half in range(2):
                    pb = psb(6 + half)
                    for k in range(8):
                        c = half * 8 + k
                        s.tr(pb[:, k * 128:(k + 1) * 128], YBh[:, c * 128:(c + 1) * 128], identb)
                    s.cp("act" if half else "dve", yo[:, half * 1024:(half + 1) * 1024], pb)
                s.st(V(yT_d[1536 + h * 128:1536 + (h + 1) * 128, :], YB[12 + h]), yo, q="sp", is_output=debug)

        def phase_rwkv(l):
            newphase()
            keepc = ar.bf16(NT, "keepc")
            lorab = ar.bf16(2 * D_A, "lorab")
            lorab3 = lorab.r("p (d n) -> p d n", d=2)
            LIN = ar.bf16(NT, "LIN")
            M1 = ar.mark()
            masks = {
                0: dict(strict=cmask("LT"), incl=cmask("LE"), strictT=cmask("GT")),
                1: dict(strict=cmask("GT"), incl=cmask("GE"), strictT=cmask("LT")),
            }

            def shift_tools():
                PAD = [ar.f32(NT + 8, "pad%d" % i) for i in range(2)]
                for i in range(2):
                    s.memset("pool", PAD[i], 0.0)
                T1 = ar.f32(NT, "T1")
                T2 = ar.f32(NT, "T2")
                mL = ar.bf16(NT, "mL")
                mR = ar.bf16(NT, "mR")
                s.ld(mL, V(tokmaskb_d[0].partition_broadcast(128), Buf()))
                s.ld(mR, V(tokmaskb_d[1].partition_broadcast(128), Buf()))
                cnt = [0]

                def shiftmix(tile, mucol, out):
                    P = PAD[cnt[0] % 2]
                    cnt[0] += 1
                    s.ld(P[:, 1:1 + NT], V(uT_d[tile * 128:(tile + 1) * 128, :], UB[tile]))
                    s.tt("dve", T1, P[:, 0:NT], mL, ALU.mult)
                    s.tt("pool", T2, P[:, 2:2 + NT], mR, ALU.mult)
                    s.tt("pool", T1, T1, T2, ALU.add)
                    s.stt(T1, T1, 0.5, P[:, 1:1 + NT], ALU.mult, ALU.subtract)
                    s.stt(out, T1, mucol, P[:, 1:1 + NT], ALU.mult, ALU.add)
                return shiftmix, T1, T2

            shiftmix, T1, T2 = shift_tools()
            s.ld(T2, V(cst_d[:, CC["keepc"]:CC["keepc"] + NT], Buf()))
            s.cp("dve", keepc, T2)
            loraw = ar.f32(2 * D_A, "loraw")
            s.ld(loraw.r("p (d n) -> p d n", d=2), V(lora_d[l], Buf()))
            s.cp("act", lorab, loraw)
            WLAL = ar.f32(NT, "WLAL")
            shiftmix(24, PT(l, "mu", 24), WLAL)
            s.act(LIN[0:64, :], WLAL[0:64, :], AF.Tanh)
            s.act(LIN[64:128, :], WLAL[64:128, :], AF.Copy)

            for j in range(6):
                s.barrier()
                ar.reset(M1)
                R = ar.bf16(NT, "R")
                K = ar.bf16(NT, "K")
                Vv = ar.bf16(NT, "V")
                G = ar.bf16(NT, "G")
                KH = ar.f32(NT, "KH")
                YP = ar.f32(NT, "YP")
                VTP = ar.bf16(16 * 2 * 128, "VTP")
                VTP4 = VTP.r("p (c h n) -> p c h n", c=16, h=2)
                M2 = ar.mark()
                shiftmix, T1, T2 = shift_tools()
                shiftmix(j, PT(l, "mu", j), R)
                shiftmix(6 + j, PT(l, "mu", 6 + j), K)
                shiftmix(12 + j, PT(l, "mu", 12 + j), Vv)
                shiftmix(18 + j, PT(l, "mu", 18 + j), G)
                SQb = ar.bf16(NT, "SQb")
                s.ts("dve", KH, K, PT(l, "kk", j), None, ALU.mult)
                s.act(SQb, KH, AF.Square)
                for tb in range(4):
                    pq = ps(tb % 2, 0, 512)
                    ts_ = slice(tb * 512, (tb + 1) * 512)
                    s.mm(pq, bonesb, SQb[:, ts_])
                    s.act(T1[:, ts_], pq, AF.Ln, bias=1e-12)
                s.act(T1, T1, AF.Exp, scale=-0.5)
                s.tt("dve", KH, KH, T1, ALU.mult)
                s.memset("pool", VTP, 0.0)
                for half in range(2):
                    pb = psb(6 + half)
                    for k in range(8):
                        c = half * 8 + k
                        s.tr(pb[:, k * 128:(k + 1) * 128], Vv[:, c * 128:(c + 1) * 128], identb)
                    pb3 = pb.r("p (k n) -> p k n", k=8)
                    for hh in range(2):
                        dst = V(VTP4.ap[:, half * 8:half * 8 + 8, hh, hh * 64:(hh + 1) * 64], VTP.buf)
                        s.cp("act" if hh else "dve", dst, pb3[:, :, hh * 64:(hh + 1) * 64])
                for d in range(2):
                    s.barrier()
                    ar.reset(M2)
                    mk = masks[d]
                    T1 = ar.f32(NT, "T1")
                    SW = ar.f32(NT, "SW")
                    Aa = ar.bf16(NT, "Aa")
                    CS = ar.f32(NT, "CS")
                    KT_ = ar.bf16(NT, "KT")
                    Bf = ar.bf16(NT, "Bf")
                    E = ar.f32(NT, "E")
                    TOTS = ar.f32(16, "TOTS")
                    KR = ar.bf16(2 * NT, "KR")
                    KR4 = KR.r("p (c a t) -> p c a t", c=16, a=2)
                    BTl = ar.bf16(NT, "BTl")
                    KTl = ar.bf16(NT, "KTl")
                    BH = ar.bf16(NT, "BH")
                    KHt = ar.bf16(NT, "KHt")
                    BHT = ar.bf16(16 * 128, "BHT")
                    BHT3 = BHT.r("p (c n) -> p c n", c=16)
                    KHT = ar.bf16(16 * 128, "KHT")
                    KHT3 = KHT.r("p (c n) -> p c n", c=16)
                    KAT = ar.bf16(16 * 128, "KAT")
                    KAT3 = KAT.r("p (c n) -> p c n", c=16)
                    WLc = ar.f32(16, "WLc")
                    c16 = lambda t: t.r("p (c t) -> p c t", c=16)
                    for tb in range(4):
                        ts_ = slice(tb * 512, (tb + 1) * 512)
                        pw = ps(tb % 2, 0, 512)
                        s.mm(pw, lorab3[0:64, d, j * 128:(j + 1) * 128], LIN[0:64, ts_])
                        s.act(SW[:, ts_], pw, AF.Sigmoid, bias=PT(l, "w0", d * 6 + j))
                        pa = ps(2 + tb % 2, 0, 512)
                        s.mm(pa, lorab3[64:128, d, j * 128:(j + 1) * 128], LIN[64:128, ts_])
                        s.act(Aa[:, ts_], pa, AF.Sigmoid, bias=PT(l, "a0", d * 6 + j))
                    s.scan(CS, keepc, SW, 0.0, ALU.mult, ALU.add)
                    CS3 = c16(CS)
                    s.cp("dve", TOTS.r("p (c o) -> p c o", o=1), CS3[:, :, 127:128])
                    TOTb = TOTS.r("p (c o) -> p c o", o=1).bc([128, 16, 128])
                    s.act(WLc, TOTS, AF.Exp, scale=-DECAY_SCALE)
                    E3 = c16(E)
                    if d == 1:
                        s.tt("dve", E3, TOTb, CS3, ALU.subtract)
                        s.tt("dve", CS, E, SW, ALU.add)
                    s.ts("dve", T1, Aa, -1.0, PT(l, "ka", j), ALU.add, ALU.mult)
                    s.stt(KT_, T1, 1.0, K, ALU.add, ALU.mult)
                    s.tt("pool", Bf, KH, Aa, ALU.mult)
                    s.tt("dve", E3, TOTb, CS3, ALU.subtract)
                    s.act(E, E, AF.Exp, scale=-DECAY_SCALE)
                    s.tt("dve", BH, Bf, E, ALU.mult)
                    s.tt("pool", KHt, KT_, E, ALU.mult)
                    s.act(E, CS, AF.Exp, scale=DECAY_SCALE)
                    s.tt("dve", BTl, Bf, E, ALU.mult)
                    s.tt("pool", KTl, KT_, E, ALU.mult)
                    s.act(E, CS, AF.Exp, scale=-DECAY_SCALE)
                    s.tt("dve", V(KR4.ap[:, :, 1, :], KR.buf), c16(R), E3, ALU.mult)
                    s.tt("dve", T1, CS, SW, ALU.subtract)
                    s.act(T1, T1, AF.Exp, scale=-DECAY_SCALE)
                    s.tt("dve", V(KR4.ap[:, :, 0, :], KR.buf), c16(KH), c16(T1), ALU.mult)
                    for (src3, dst3, dstb) in ((c16(BH), BHT3, BHT), (c16(KHt), KHT3, KHT),
                                               (V(KR4.ap[:, :, 0, :], KR.buf), KAT3, KAT)):
                        for half in range(2):
                            pb = psb(6 + half)
                            for k in range(8):
                                c = half * 8 + k
                                s.tr(pb[:, k * 128:(k + 1) * 128], src3[:, c, :], identb)
                            s.cp("act" if half else "dve", V(dst3.ap[:, half * 8:half * 8 + 8, :], dstb.buf),
                                 pb.r("p (k n) -> p k n", k=8))
                    s.stt(T1, KT_, PT(l, "rk", j), R, ALU.mult, ALU.mult)
                    s.cp("act", KHt, T1)
                    for tb in range(4):
                        ts_ = slice(tb * 512, (tb + 1) * 512)
                        pq = ps(tb % 2, 0, 512)
                        s.mm(pq, bonesb, KHt[:, ts_])
                        if d == 0:
                            s.tt("dve", YP[:, ts_], pq, Vv[:, ts_], ALU.mult)
                        else:
                            s.tt("dve", T1[:, ts_], pq, Vv[:, ts_], ALU.mult)
                    if d == 1:
                        s.tt("pool", YP, YP, T1, ALU.add)
                    ST = ar.f32(64, "ST")
                    SBP = [ar.bf16(128, "SBP%d" % hh) for hh in range(2)]
                    ZIN = ar.f32(2 * 128, "ZIN")
                    ZIN3 = ZIN.r("p (h n) -> p h n", h=2)
                    MASKA = ar.bf16(256, "MASKA")
                    MASKB = ar.bf16(256, "MASKB")
                    MASKT = ar.bf16(128, "MASKT")
                    s.ts("dve", MASKA[:, 0:128], mk["strict"], -1.0, None, ALU.mult)
                    s.cp("dve", MASKA[:, 128:256], mk["incl"])
                    s.cp("dve", MASKB[:, 0:128], mk["strict"])
                    s.cp("dve", MASKB[:, 128:256], mk["incl"])
                    s.ts("dve", MASKT, mk["strictT"], -1.0, None, ALU.mult)
                    NB = 2
                    SA = [[ar.bf16(256, "SA%d%d" % (hh, r)) for r in range(NB)] for hh in range(2)]
                    SBq = [[ar.bf16(256, "SB%d%d" % (hh, r)) for r in range(NB)] for hh in range(2)]
                    MM_ = [[[ar.bf16(256, "MM%d%d%d" % (hh, r, q)) for q in range(2)] for r in range(NB)] for hh in range(2)]
                    XX = [[[ar.bf16(128, "XX%d%d%d" % (hh, r, q)) for q in range(2)] for r in range(NB)] for hh in range(2)]
                    XT7 = [[ar.bf16(128, "XT7%d%d" % (hh, r)) for r in range(NB)] for hh in range(2)]
                    NAP = [[ar.bf16(128, "NAP%d%d" % (hh, r)) for r in range(NB)] for hh in range(2)]
                    SO = [ar.f32(128, "SO%d" % i) for i in range(2)]
                    for hh in range(2):
                        s.memset("pool", SBP[hh], 0.0)
                        for r in range(NB):
                            s.memset("pool", NAP[hh][r], 0.0)
                    s.memset("pool", ZIN, 0.0)
                    for hh in range(2):
                        s.ld(ZIN3[0:64, hh, hh * 64:(hh + 1) * 64], V(srw_d[l, d, 2 * j + hh], Buf()))
                    s.barrier()
                    for hh in range(2):
                        hs = slice(hh * 64, (hh + 1) * 64)
                        pi = ps(2 + hh, 0, 64)
                        s.tr(pi, ZIN3[0:64, hh, :], ident[0:64, 0:64])
                        s.cp("act", ST[hs, :], pi[hs, :])
                        s.cp("dve", SBP[hh][hs, hs], ST[hs, :])
                    order = chunk_order(d)
                    sidx = 0
                    for ci, c in enumerate(order):
                        r = ci % NB
                        cs = slice(c * 128, (c + 1) * 128)
                        krc = V(KR4.ap[:, c, :, :], KR.buf).r("p a t -> p (a t)")
                        kac = V(KR4.ap[:, c, 0, :], KR.buf)
                        rho = V(KR4.ap[:, c, 1, :], KR.buf)
                        for hh in range(2):
                            hs = slice(hh * 64, (hh + 1) * 64)
                            p1 = ps(0 + hh, 0, 256, sub=0)
                            p2 = ps(0 + hh, 256, 512, sub=1)
                            p3 = ps(2 + hh, 0, 128)
                            s.mm(p1, BTl[hs, cs], krc[hs, :])
                            s.mm(p2, KTl[hs, cs], krc[hs, :])
                            s.mm(p3, kac[hs, :], BTl[hs, cs])
                            s.tt("dve", SA[hh][r], p1, MASKA, ALU.mult)
                            s.tt("dve", SBq[hh][r], p2, MASKB, ALU.mult)
                            s.tt("dve", MM_[hh][r][0][:, 0:128], p3, MASKT, ALU.mult)
                            s.cp("pool", MM_[hh][r][0][:, 128:256], SA[hh][r][:, 0:128])
                            pq = ps(2 + hh, 128, 192)
                            s.mm(pq, SBq[hh][r][:, 0:128], V(VTP4.ap[:, c, hh, hh * 64:(hh + 1) * 64], VTP.buf))
                            ksl = slice(hh * 64, (hh + 1) * 64)
                            qsl = slice((1 - hh) * 64, (2 - hh) * 64)
                            s.cp("pool", XX[hh][r][0][:, ksl], V(KAT3.ap[:, c, ksl], KAT.buf))
                            s.cp("act", XX[hh][r][0][:, qsl], pq)
                        for lev in range(7):
                            q = lev % 2
                            for hh in range(2):
                                Mc = MM_[hh][r][q]
                                Xc = XX[hh][r][q]
                                px = ps(4 + hh, 0, 128)
                                s.mm(px, Mc[:, 128:256], Xc, start=True, stop=False)
                                s.mm(px, identb, Xc, start=False, stop=True)
                                s.cp("act", XX[hh][r][1 - q], px)
                                if lev < 6:
                                    pm = ps(6 + hh, 0, 256)
                                    s.mm(pm[:, 0:128], Mc[:, 128:256], Mc[:, 0:128])
                                    s.mm(pm[:, 128:256], Mc[:, 0:128], Mc[:, 128:256])
                                    s.cp("dve", MM_[hh][r][1 - q], pm)
                        for hh in range(2):
                            pb = psb(2 + hh, 640, 768)
                            s.tr(pb, XX[hh][r][1], identb)
                            s.cp("dve", XT7[hh][r], pb)
                        if ci > 0 and is_boundary(c, d):
                            s.ts("dve", ST, ST, keepcol[:, 0:1], None, ALU.mult)
                            for hh in range(2):
                                hs = slice(hh * 64, (hh + 1) * 64)
                                s.cp("act", SBP[hh][hs, hs], ST[hs, :])
                        for hh in range(2):
                            hs = slice(hh * 64, (hh + 1) * 64)
                            usl = slice((1 - hh) * 64, (2 - hh) * 64)
                            pA = ps(2 + hh, 192, 256)
                            s.mm(pA, XT7[hh][r][hs, :], SBP[hh][hs, hs])
                            s.stt(NAP[hh][r][:, hs], pA, -1.0, XX[hh][r][1][:, usl], ALU.mult, ALU.subtract)
                        py = ps(3, 384, 512)
                        for hh in range(2):
                            hs = slice(hh * 64, (hh + 1) * 64)
                            s.mm(py, SBP[hh][hs, :], rho[hs, :], start=(hh == 0), stop=False)
                            s.mm(py, NAP[hh][r], SA[hh][r][:, 128:256], start=False, stop=False)
                            s.mm(py, V(VTP4.ap[:, c, hh, :], VTP.buf), SBq[hh][r][:, 128:256], start=False, stop=(hh == 1))
                        s.tt("dve", YP[:, cs], YP[:, cs], py, ALU.add)
                        for hh in range(2):
                            hs = slice(hh * 64, (hh + 1) * 64)
                            pS = ps(2 + hh, 256, 320)
                            s.mm(pS, V(BHT3.ap[:, c, :], BHT.buf), NAP[hh][r][:, hs], start=True, stop=False)
                            s.mm(pS, V(KHT3.ap[:, c, :], KHT.buf), V(VTP4.ap[:, c, hh, hs], VTP.buf), start=False, stop=True)
                            s.stt(ST[hs, :], ST[hs, :], WLc[hs, c:c + 1], pS[hs, :], ALU.mult, ALU.add)
                            s.cp("act", SBP[hh][hs, hs], ST[hs, :])
                        if is_seq_end(c, d):
                            seq = c // 2
                            po = ps(2, 384, 512, p0=0, p1=64)
                            s.tr(po, ST, ident)
                            so = SO[sidx % 2]
                            sidx += 1
                            s.cp("act", so[0:64, :], po)
                            s.st(V(nrw_d[seq, l, d, 2 * j:2 * j + 2].rearrange("h v k -> v h k"), OUTB),
                                 so[0:64, :].r("p (h k) -> p h k", h=2), q="sp", is_output=True)
                s.barrier()
                ar.reset(M2)
                T1 = ar.f32(NT, "T1")
                RB = ar.bf16(NT, "RB")
                D_ = ar.f32(NT, "D_")
                s.cp("act", RB, YP)
                for tb in range(4):
                    ts_ = slice(tb * 512, (tb + 1) * 512)
                    pq = ps(2 + tb % 2, 0, 512)
                    s.mm(pq, bonesb, RB[:, ts_])
                    s.stt(D_[:, ts_], pq, -1.0 / 64, YP[:, ts_], ALU.mult, ALU.add)
                s.act(RB, D_, AF.Square)
                for tb in range(4):
                    ts_ = slice(tb * 512, (tb + 1) * 512)
                    pq = ps(tb % 2, 0, 512)
                    s.mm(pq, bonesb, RB[:, ts_])
                    s.act(T1[:, ts_], pq, AF.Ln, scale=1.0 / 64, bias=GN_EPS)
                s.act(T1, T1, AF.Exp, scale=-0.5)
                s.tt("dve", D_, D_, T1, ALU.mult)
                s.ts("dve", D_, D_, PT(l, "gnw", j), PT(l, "gnb", j), ALU.mult, ALU.add)
                s.act(T1, G, AF.Silu)
                s.tt("dve", RB, D_, T1, ALU.mult)
                s.st(V(yT_d[j * 128:(j + 1) * 128, :], YB[j]), RB, q="sp", is_output=debug)

        if "mod" in phases:
            phase_mod()
        for l in range(L):
            if "norm" in phases:
                phase_normproj(l)
            if "rwkv" in phases:
                phase_rwkv(l)
            if "ssd" in phases:
                phase_ssd(l)
            if "mlstm" in phases:
                phase_mlstm(l)
            if "out" in phases:
                phase_out(l)
        if "final" in phases:
            phase_final()
        s.emit()
        build.stats = dict(cnt=dict(s.cnt), dcnt=dict(s.dcnt), peak=ar.peak)
    return nc


def _ptab(inp, l, prompt):
    t = np.zeros((128, NPT), np.float32)
    mu = inp["rwkv_mu"][l]
    for i in range(25):
        c0 = TILES[i][0]
        t[:, PCOL["mu"] + i] = mu[c0:c0 + 128]
    for j in range(6):
        sl = slice(j * 128, (j + 1) * 128)
        t[:, PCOL["kk"] + j] = inp["rwkv_kk"][l][sl]
        t[:, PCOL["ka"] + j] = inp["rwkv_ka"][l][sl]
        t[:, PCOL["rk"] + j] = inp["rwkv_rk"][l].reshape(-1)[sl]
        t[:, PCOL["gnw"] + j] = inp["rwkv_gnw"][l][sl]
        t[:, PCOL["gnb"] + j] = inp["rwkv_gnb"][l][sl]
        for d in range(2):
            t[:, PCOL["w0"] + d * 6 + j] = inp["rwkv_w0"][l, d][sl]
            t[:, PCOL["a0"] + d * 6 + j] = inp["rwkv_a0"][l, d][sl]
        t[:, PCOL["ssmD"] + j] = np.repeat(inp["ssm_d"][l][2 * j:2 * j + 2], 64)
        t[:, PCOL["ssmnorm"] + j] = inp["ssm_norm"][l][sl]
    for h in range(12):
        t[:, PCOL["ssmDrep"] + h] = inp["ssm_d"][l][h]
    sc = inp["ssm_conv"][l]
    mc = inp["ml_conv"][l]
    for i in range(10):
        sl = slice(i * 128, (i + 1) * 128)
        for kh in range(3):
            for kw in range(3):
                if prompt and kh != 1:
                    continue
                t[:, PCOL["sconv"] + i * 9 + kh * 3 + kw] = sc[kh, kw, sl]
        t[:, PCOL["sconvb"] + i] = inp["ssm_conv_b"][l][sl]
    for i in range(8):
        sl = slice(i * 128, (i + 1) * 128)
        for kh in range(3):
            for kw in range(3):
                if prompt and kh != 1:
                    continue
                t[:, PCOL["mconv"] + i * 9 + kh * 3 + kw] = mc[kh, kw, sl]
        t[:, PCOL["mconvb"] + i] = inp["ml_conv_b"][l][sl]
    t[0:24, PCOL["dtb"]] = inp["ssm_dt_bias"][l].reshape(-1)
    t[0:24, PCOL["alog"]] = inp["ssm_a_log"][l].reshape(-1)
    t[0:8, PCOL["ib"]] = inp["ml_ib"][l].reshape(-1)
    t[0:8, PCOL["fb"]] = inp["ml_fb"][l].reshape(-1)
    return t


def make_in_maps(inp, depth=DEPTH):
    f = lambda a: np.ascontiguousarray(np.asarray(a, dtype=np.float32))
    inp = {k: f(v) for k, v in inp.items()}
    L = depth
    cstv = make_consts()
    shared = {
        "cst": cstv,
        "w_mod": inp["w_mod"][:L],
        "b_modT": f(inp["b_mod"][:L].reshape(L, 48, 128).transpose(0, 2, 1)),
        "norm_gT": f(inp["norm_g"][:L].reshape(L, 16, 128).transpose(0, 2, 1)),
        "w_in": inp["w_in"][:L],
        "w_out": inp["w_out"][:L],
        "lora": f(np.concatenate([inp["rwkv_wup"][:L].transpose(0, 2, 1, 3), inp["rwkv_aup"][:L].transpose(0, 2, 1, 3)], axis=1)),
        "ml_norm": inp["ml_norm"][:L],
        "final_g": inp["final_g"],
    }
    pt_s = f(np.stack([_ptab(inp, l, False) for l in range(L)]))
    pt_p = f(np.stack([_ptab(inp, l, True) for l in range(L)]))
    t = np.arange(NT)
    tm_s = np.stack([(t != 0), (t != NT - 1), (t % 64 != 0), (t % 64 != 63)]).astype(np.float32)
    tm_p = np.stack([(t % 256 != 0), (t % 256 != 255), (t % 256 != 0), (t % 256 != 255)]).astype(np.float32)
    maps = []
    for core in range(8):
        m = dict(shared)
        if core < 4:
            b = core
            m["x0"] = inp["x_sample"][b]
            m["cond"] = f(inp["c"][b].reshape(16, 128).T)
            m["ptab"] = pt_s
            m["tokmask"] = tm_s
            m["tokmaskb"] = tm_s.astype(ml_dtypes.bfloat16)
            m["keep"] = np.ones((128, 1), np.float32)
            m["srw"] = inp["state_rwkv"][b][:L]
            m["sss"] = inp["state_ssm"][b][:L]
            m["smc"] = inp["state_mlstm_c"][b][:L]
            m["smn"] = inp["state_mlstm_n"][b][:L]
            m["smm"] = f(inp["state_mlstm_m"][b][:L].reshape(L, 8))
        else:
            g = core - 4
            m["x0"] = f(inp["x_prompt"][g * 8:(g + 1) * 8].reshape(NT, D_MODEL))
            m["cond"] = f(inp["c_ctx"].reshape(16, 128).T)
            m["ptab"] = pt_p
            m["tokmask"] = tm_p
            m["tokmaskb"] = tm_p.astype(ml_dtypes.bfloat16)
            m["keep"] = np.zeros((128, 1), np.float32)
            m["srw"] = np.zeros((L, 2, 12, 64, 64), np.float32)
            m["sss"] = np.zeros((L, 2, 12, 64, 128), np.float32)
            m["smc"] = np.zeros((L, 2, 4, 128, 128), np.float32)
            m["smn"] = np.zeros((L, 2, 4, 128), np.float32)
            m["smm"] = np.zeros((L, 8), np.float32)
        maps.append(m)
    return maps


_NC_CACHE = {}


def kernel(**inputs):
    if "nc" not in _NC_CACHE:
        _NC_CACHE["nc"] = build(DEPTH)
    nc = _NC_CACHE["nc"]
    maps = make_in_maps(inputs, DEPTH)
    res = run_bass_kernel_spmd(nc, maps, core_ids=list(range(8)))
    r = res.results
    y_sample = np.stack([r[b]["y"] for b in range(4)]).astype(np.float32)
    y_prompt = np.concatenate([r[4 + g]["y"].reshape(8, 256, D_MODEL) for g in range(4)]).astype(np.float32)
    cat = lambda k: np.concatenate([r[4 + g][k] for g in range(4)]).astype(np.float32)
    new_rwkv = cat("nrw")
    new_ssm = cat("nss")
    new_mc = cat("nmc")
    new_mn = cat("nmn")
    new_mm = cat("nmm").reshape(32, DEPTH, 2, 4)
    return (y_prompt, y_sample, new_rwkv, new_ssm, new_mc, new_mn, new_mm)
```
